# Optimizing a Trainium2 kernel written in Bass

```python
import math
import jax
import jax.numpy as jnp
from jax import lax
import numpy as np

D_MODEL = 1024
BATCH = 16
SEQ = 4096
DEPTH = 4

GRID_W = 64
CTX_LEN = 256
EPS = 1e-6
D_FF = 2816
FFN_RES_W = 0.5
N_MOD = 9
HY_W = 256
SSD_HEADS = 8
SSD_HEAD_DIM = 64
SSD_W = SSD_HEADS * SSD_HEAD_DIM
SSD_GROUPS = 2
SSD_STATE = 128
SSD_CHUNK = 64
SC_W = 256
D_MIX = HY_W + SSD_W + SC_W
SHORT_K = 3
HY_ORDER = 2
HY_EMB = 33
HY_FH = 64
HY_FOUT = HY_ORDER * 2 * HY_W
HY_FAST = 0.3
HY_SLOW = 1.5
HY_TARGET = 1e-2
HY_IN = 3 * HY_W
SSD_XBC = SSD_W + 2 * SSD_GROUPS * SSD_STATE
SC_IN = 3 * SC_W
IN_SPLITS = (HY_IN, HY_IN + SSD_W, HY_IN + SSD_W + SSD_XBC, HY_IN + SSD_W + SSD_XBC + 2 * SSD_HEADS)
D_IN = IN_SPLITS[-1] + SC_IN

kernel_name = 'hybrid_hyena_ssd_shortconv_dit'


def rmsnorm(x, g):
    xf = x.astype(jnp.float32)
    y = xf * lax.rsqrt(jnp.mean(xf * xf, axis=-1, keepdims=True) + EPS)
    return (y * g.astype(jnp.float32)).astype(x.dtype)


def sub_in(x, g_pre, mod, i):
    return rmsnorm(x, g_pre) * (1.0 + mod[:, :, 3 * i + 1]) + mod[:, :, 3 * i]


def sub_out(x, y, g_post, mod, i, res_w):
    return x + res_w * mod[:, :, 3 * i + 2] * rmsnorm(y, g_post)


def swiglu(h, w_in, w_out):
    g, u = jnp.split(h @ w_in, 2, axis=-1)
    return (jax.nn.silu(g) * u) @ w_out


def dwconv(u, w, b, n_seg):
    bsz, L, C = u.shape
    K = w.shape[0]
    p = K // 2
    seg = L // n_seg
    up = jnp.pad(u.reshape(bsz, n_seg, seg, C), ((0, 0), (0, 0), (p, p), (0, 0)))
    y = up[:, :, 0:seg] * w[0]
    for j in range(1, K):
        y = y + up[:, :, j:j + seg] * w[j]
    if b is not None:
        y = y + b
    return y.reshape(bsz, L, C)


def hyena_filters(L, fw1, fb1, fw2, fb2, fw3, fb3, fw4, freq):
    f32 = jnp.float32
    t = jnp.linspace(0.0, 1.0, L, dtype=f32)[:, None]
    bands = (HY_EMB - 1) // 2
    w = 2.0 * math.pi * jnp.arange(L, dtype=f32)[:, None] / L
    f = jnp.linspace(1e-4, bands - 1, bands, dtype=f32)[None, :]
    z = jnp.concatenate([t, jnp.cos(f * w), -jnp.sin(f * w)], axis=-1)
    fr = freq.astype(f32)
    h = jnp.sin(fr * (z @ fw1.astype(f32) + fb1.astype(f32)))
    h = jnp.sin(fr * (h @ fw2.astype(f32) + fb2.astype(f32)))
    h = jnp.sin(fr * (h @ fw3.astype(f32) + fb3.astype(f32)))
    h = h @ fw4.astype(f32)
    deltas = jnp.linspace(math.log(HY_TARGET) / HY_SLOW, math.log(HY_TARGET) / HY_FAST, HY_W, dtype=f32)
    decay = jnp.exp(-t * jnp.abs(deltas))
    h = h.reshape(L, HY_ORDER, 2, HY_W) * decay[:, None, None, :]
    k = jnp.concatenate([h[:, :, 0], jnp.zeros((1, HY_ORDER, HY_W), f32), h[:0:-1, :, 1]], axis=0)
    return jnp.fft.rfft(k, axis=0)


def hyena_long_conv(u, kf, bias):
    L = u.shape[1]
    v, x1, x2 = jnp.split(u.astype(jnp.float32), 3, axis=-1)
    bias = bias.astype(jnp.float32)
    z = v
    for o, gate in enumerate((x1, x2)):
        zf = jnp.fft.rfft(z, n=2 * L, axis=1)
        z = gate * (jnp.fft.irfft(zf * kf[:, o], n=2 * L, axis=1)[:, :L] + bias[o] * z)
    return z


def ssd_scan(xdt, dA, Bm, Cm, h0, want_y):
    b, l, nh, p = xdt.shape
    g = Bm.shape[2]
    r = nh // g
    n = Bm.shape[-1]
    q = SSD_CHUNK
    c = l // q
    X = xdt.reshape(b, c, q, g, r, p)
    A = dA.reshape(b, c, q, g, r)
    Bc = Bm.reshape(b, c, q, g, n)
    Cc = Cm.reshape(b, c, q, g, n)
    Acs = jnp.cumsum(A, axis=2)
    A_tot = Acs[:, :, -1]
    decay_to_end = jnp.exp(A_tot[:, :, None] - Acs)
    chunk_states = jnp.einsum('bcqgn,bcqgr,bcqgrp->bcgrpn', Bc, decay_to_end, X)

    def step(h_prev, inp):
        s, a = inp
        return jnp.exp(a)[..., None, None] * h_prev + s, h_prev

    h_final, h_starts = lax.scan(step, h0, (jnp.moveaxis(chunk_states, 1, 0), jnp.moveaxis(A_tot, 1, 0)))
    if not want_y:
        return h_final
    h_starts = jnp.moveaxis(h_starts, 0, 1)
    seg = Acs[:, :, :, None] - Acs[:, :, None, :]
    mask = jnp.tril(jnp.ones((q, q), dtype=bool))[:, :, None, None]
    Lmat = jnp.exp(jnp.where(mask, seg, -jnp.inf))
    scores = jnp.einsum('bclgn,bcsgn->bclsg', Cc, Bc)
    y_diag = jnp.einsum('bclsg,bclsgr,bcsgrp->bclgrp', scores, Lmat, X)
    y_off = jnp.einsum('bclgn,bcgrpn,bclgr->bclgrp', Cc, h_starts, jnp.exp(Acs))
    return (y_diag + y_off).reshape(b, l, nh, p), h_final


def token_mixers(p, n_seg, kf, h0_f, h0_b, want_out, hy_conv_w, hy_conv_b, hy_bias, ssd_conv_w, ssd_conv_b,
                 ssd_a_log, ssd_dt_bias, ssd_d, sc_conv_w, mix_gain):
    f32 = jnp.float32
    bsz, L, _ = p.shape
    hy, z, xbc, dt_raw, scp = jnp.split(p, IN_SPLITS, axis=-1)
    xbc = jax.nn.silu(dwconv(xbc, ssd_conv_w, ssd_conv_b, n_seg)).astype(f32)
    xs, Bm, Cm = jnp.split(xbc, [SSD_W, SSD_W + SSD_GROUPS * SSD_STATE], axis=-1)
    xs = xs.reshape(bsz, L, SSD_HEADS, SSD_HEAD_DIM)
    Bm = Bm.reshape(bsz, L, SSD_GROUPS, SSD_STATE)
    Cm = Cm.reshape(bsz, L, SSD_GROUPS, SSD_STATE)
    dt = jax.nn.softplus(dt_raw.astype(f32).reshape(bsz, L, 2, SSD_HEADS) + ssd_dt_bias.astype(f32))
    a = -jnp.exp(ssd_a_log.astype(f32))
    dt_f = dt[:, :, 0]
    res_f = ssd_scan(xs * dt_f[..., None], dt_f * a[0], Bm, Cm, h0_f, want_out)
    rev = lambda t: jnp.flip(t, axis=1)
    dt_b = rev(dt[:, :, 1])
    res_b = ssd_scan(rev(xs) * dt_b[..., None], dt_b * a[1], rev(Bm), rev(Cm), h0_b, want_out)
    if not want_out:
        return None, res_f, res_b
    (y_f, h_f), (y_b, h_b) = res_f, res_b
    y = y_f + rev(y_b) + ssd_d.astype(f32)[:, None] * xs
    y_ssd = rmsnorm(y.reshape(bsz, L, SSD_W) * jax.nn.silu(z.astype(f32)), mix_gain[HY_W:HY_W + SSD_W])
    y_hy = rmsnorm(hyena_long_conv(dwconv(hy, hy_conv_w, hy_conv_b, n_seg), kf, hy_bias), mix_gain[:HY_W])
    gb, gc, hx = jnp.split(scp, 3, axis=-1)
    y_sc = rmsnorm(gb * dwconv(gc * hx, sc_conv_w, None, n_seg), mix_gain[HY_W + SSD_W:])
    out = jnp.concatenate([y_hy.astype(p.dtype), y_ssd.astype(p.dtype), y_sc], axis=-1)
    return out, h_f, h_b


def setup_inputs(seed: int = 0) -> dict:
    key = jax.random.key(seed)
    ks = iter(jax.random.split(key, 40))
    f32 = jnp.float32
    nrm = lambda shape, scale: scale * jax.random.normal(next(ks), shape, f32)
    gain = lambda shape: 1.0 + 0.02 * jax.random.normal(next(ks), shape, f32)
    x = nrm((BATCH, SEQ, D_MODEL), 1.0)
    c = nrm((BATCH, D_MODEL), 1.0)
    ctx = nrm((BATCH, CTX_LEN, D_MODEL), 1.0)
    c_ctx = nrm((D_MODEL,), 1.0)
    w_mod = nrm((DEPTH, D_MODEL, N_MOD * D_MODEL), 0.5 * D_MODEL ** -0.5)
    b_mod = nrm((DEPTH, N_MOD * D_MODEL), 0.02)
    norm_g = gain((DEPTH, 6, D_MODEL))
    ffn_w_in = nrm((DEPTH, 2, D_MODEL, 2 * D_FF), D_MODEL ** -0.5)
    ffn_w_out = nrm((DEPTH, 2, D_FF, D_MODEL), D_FF ** -0.5)
    w_in = nrm((DEPTH, D_MODEL, D_IN), D_MODEL ** -0.5)
    w_out = nrm((DEPTH, D_MIX, D_MODEL), D_MIX ** -0.5)
    hy_conv_w = nrm((DEPTH, SHORT_K, HY_IN), SHORT_K ** -0.5)
    hy_conv_b = nrm((DEPTH, HY_IN), 0.02)
    hy_fw1 = nrm((DEPTH, HY_EMB, HY_FH), HY_EMB ** -0.5)
    hy_fb1 = nrm((DEPTH, HY_FH), 0.02)
    hy_fw2 = nrm((DEPTH, HY_FH, HY_FH), HY_FH ** -0.5)
    hy_fb2 = nrm((DEPTH, HY_FH), 0.02)
    hy_fw3 = nrm((DEPTH, HY_FH, HY_FH), HY_FH ** -0.5)
    hy_fb3 = nrm((DEPTH, HY_FH), 0.02)
    hy_fw4 = nrm((DEPTH, HY_FH, HY_FOUT), HY_FH ** -0.5)
    hy_freq = gain((DEPTH, HY_FH))
    hy_bias = nrm((DEPTH, HY_ORDER, HY_W), 1.0)
    ssd_conv_w = nrm((DEPTH, SHORT_K, SSD_XBC), SHORT_K ** -0.5)
    ssd_conv_b = nrm((DEPTH, SSD_XBC), 0.02)
    ssd_a_log = jnp.log(jax.random.uniform(next(ks), (DEPTH, 2, SSD_HEADS), f32, minval=1.0, maxval=16.0))
    u = jax.random.uniform(next(ks), (DEPTH, 2, SSD_HEADS), f32)
    dt0 = jnp.exp(u * (math.log(0.1) - math.log(1e-3)) + math.log(1e-3))
    ssd_dt_bias = dt0 + jnp.log(-jnp.expm1(-dt0))
    ssd_d = gain((DEPTH, SSD_HEADS))
    sc_conv_w = nrm((DEPTH, SHORT_K, SC_W), SHORT_K ** -0.5)
    mix_gain = gain((DEPTH, D_MIX))
    return {'x': x, 'c': c, 'ctx': ctx, 'c_ctx': c_ctx, 'w_mod': w_mod, 'b_mod': b_mod, 'norm_g': norm_g,
            'ffn_w_in': ffn_w_in, 'ffn_w_out': ffn_w_out, 'w_in': w_in, 'w_out': w_out,
            'hy_conv_w': hy_conv_w, 'hy_conv_b': hy_conv_b, 'hy_fw1': hy_fw1, 'hy_fb1': hy_fb1,
            'hy_fw2': hy_fw2, 'hy_fb2': hy_fb2, 'hy_fw3': hy_fw3, 'hy_fb3': hy_fb3, 'hy_fw4': hy_fw4,
            'hy_freq': hy_freq, 'hy_bias': hy_bias, 'ssd_conv_w': ssd_conv_w, 'ssd_conv_b': ssd_conv_b,
            'ssd_a_log': ssd_a_log, 'ssd_dt_bias': ssd_dt_bias, 'ssd_d': ssd_d, 'sc_conv_w': sc_conv_w,
            'mix_gain': mix_gain}


def reference(x, c, ctx, c_ctx, w_mod, b_mod, norm_g, ffn_w_in, ffn_w_out, w_in, w_out, hy_conv_w, hy_conv_b,
              hy_fw1, hy_fb1, hy_fw2, hy_fb2, hy_fw3, hy_fb3, hy_fw4, hy_freq, hy_bias, ssd_conv_w, ssd_conv_b,
              ssd_a_log, ssd_dt_bias, ssd_d, sc_conv_w, mix_gain):
    bsz, n_lat, _ = x.shape
    rows = n_lat // GRID_W
    cbsz, ctx_len, _ = ctx.shape
    silu_c = jax.nn.silu(c)
    silu_cc = jax.nn.silu(c_ctx)
    xc = ctx
    h_zero = jnp.zeros((cbsz, SSD_GROUPS, SSD_HEADS // SSD_GROUPS, SSD_HEAD_DIM, SSD_STATE), jnp.float32)
    for l in range(DEPTH):
        last = l == DEPTH - 1
        mod_x = (silu_c @ w_mod[l] + b_mod[l]).reshape(bsz, 1, N_MOD, D_MODEL)
        mod_c = (silu_cc @ w_mod[l] + b_mod[l]).reshape(1, 1, N_MOD, D_MODEL)
        g = norm_g[l]
        mp = (hy_conv_w[l], hy_conv_b[l], hy_bias[l], ssd_conv_w[l], ssd_conv_b[l], ssd_a_log[l],
              ssd_dt_bias[l], ssd_d[l], sc_conv_w[l], mix_gain[l])
        fp = (hy_fw1[l], hy_fb1[l], hy_fw2[l], hy_fb2[l], hy_fw3[l], hy_fb3[l], hy_fw4[l], hy_freq[l])
        x = sub_out(x, swiglu(sub_in(x, g[0], mod_x, 0), ffn_w_in[l, 0], ffn_w_out[l, 0]), g[1], mod_x, 0, FFN_RES_W)
        xc = sub_out(xc, swiglu(sub_in(xc, g[0], mod_c, 0), ffn_w_in[l, 0], ffn_w_out[l, 0]), g[1], mod_c, 0, FFN_RES_W)
        pc = sub_in(xc, g[2], mod_c, 1) @ w_in[l]
        kf_ctx = None if last else hyena_filters(ctx_len, *fp)
        y_c, h_f, h_b = token_mixers(pc, 1, kf_ctx, h_zero, h_zero, not last, *mp)
        pl = sub_in(x, g[2], mod_x, 1) @ w_in[l]
        y_l, _, _ = token_mixers(pl, rows, hyena_filters(n_lat, *fp), h_f, h_b, True, *mp)
        x = sub_out(x, y_l @ w_out[l], g[3], mod_x, 1, 1.0)
        x = sub_out(x, swiglu(sub_in(x, g[4], mod_x, 2), ffn_w_in[l, 1], ffn_w_out[l, 1]), g[5], mod_x, 2, FFN_RES_W)
        if not last:
            xc = sub_out(xc, y_c @ w_out[l], g[3], mod_c, 1, 1.0)
            xc = sub_out(xc, swiglu(sub_in(xc, g[4], mod_c, 2), ffn_w_in[l, 1], ffn_w_out[l, 1]), g[5], mod_c, 2, FFN_RES_W)
    return x
```

```python
import math
import contextlib
import numpy as np
import ml_dtypes
import concourse.bass as bass
import concourse.mybir as mybir
from concourse.bass_utils import run_bass_kernel_spmd

F32 = mybir.dt.float32
BF16 = mybir.dt.bfloat16
AF = mybir.ActivationFunctionType
ALU = mybir.AluOpType

D = 1024; DEPTH = 4; BATCH = 16; SEQ = 4096; CTX = 256; DFF = 2816; NMOD = 9
NCORE = 8; BPC = BATCH // NCORE
EPS = 1e-6
HYW = 256; SSDW = 512; NH = 8; HD = 64; NG = 2; NS = 128
NPXC = 25
NDS = 8
TWO_PI = 2.0 * math.pi
MAGIC = 12582912.0


class Tok:
    __slots__ = ("w", "r")

    def __init__(self):
        self.w = None
        self.r = {}


class V:
    def __init__(self, t, tok=None):
        self.t = t
        self.tok = tok or Tok()

    def __getitem__(self, k):
        return self.t[k]


class Sched:
    def __init__(self, nc, st):
        self.nc = nc
        self.st = st
        self.E = {"pe": nc.tensor, "act": nc.scalar, "dve": nc.vector, "pool": nc.gpsimd, "sp": nc.sync}
        self.sem = {}
        self.cnt = {}
        for k in ["pe", "act", "dve", "pool"]:
            self.sem["c_" + k] = st.enter_context(nc.semaphore("c_" + k))
            self.cnt["c_" + k] = 0
        self.drr = {"sp": 0, "pool": 0}
        for q in ["sp", "pool"]:
            for i in range(NDS):
                key = "d_%s%d" % (q, i)
                self.sem[key] = st.enter_context(nc.semaphore(key))
                self.cnt[key] = 0
        self.seen = {e: {} for e in self.E}
        self.nins = 0

    def _wait(self, eng, key, val):
        if eng == "pe" and key == "c_pe":
            return
        if self.seen[eng].get(key, 0) >= val:
            return
        self.E[eng].wait_ge(self.sem[key], val)
        self.seen[eng][key] = val

    def _deps(self, eng, reads, writes):
        for t in reads:
            if t.w:
                self._wait(eng, *t.w)
        for t in writes:
            if t.w:
                self._wait(eng, *t.w)
            for k, v in t.r.items():
                self._wait(eng, k, v)

    def _mark(self, me, reads, writes):
        for t in reads:
            t.r[me[0]] = me[1]
        for t in writes:
            t.w = me
            t.r = {}

    def op(self, eng, fn, reads=(), writes=()):
        reads = [x.tok if isinstance(x, V) else x for x in reads]
        writes = [x.tok if isinstance(x, V) else x for x in writes]
        self._deps(eng, reads, writes)
        ins = fn(self.E[eng])
        key = "c_" + eng
        self.cnt[key] += 1
        ins.then_inc(self.sem[key], 1)
        self._mark((key, self.cnt[key]), reads, writes)
        self.nins += 1

    def dma(self, q, out, in_, reads=(), writes=()):
        reads = [x.tok if isinstance(x, V) else x for x in reads]
        writes = [x.tok if isinstance(x, V) else x for x in writes]
        i = self.drr[q]
        self.drr[q] = (i + 1) % NDS
        key = "d_%s%d" % (q, i)
        if self.cnt[key] > 0:
            self._wait(q, key, self.cnt[key])
        self._deps(q, reads, writes)
        ins = self.E[q].dma_start(out=out, in_=in_)
        self.cnt[key] += 16
        ins.then_inc(self.sem[key], 16)
        self._mark((key, self.cnt[key]), reads, writes)
        self.nins += 1

    def barrier(self):
        for e in self.E:
            for k, v in self.cnt.items():
                if v > 0:
                    self._wait(e, k, v)


class DT:
    def __init__(self, ap, n):
        self.ap = ap
        self.toks = [Tok() for _ in range(max(1, n))]

    def tk(self, t0=None, t1=None):
        if t0 is None:
            return self.toks
        return self.toks[t0 // 128:(t1 + 127) // 128]


def _consts():
    f32 = np.float32
    c = {}
    r = np.arange(128)
    m = np.zeros((128, 5, 128), f32)
    m[:, 0] = (r[:, None] > r[None, :])
    m[:, 1] = (r[:, None] <= r[None, :])
    m[:, 2] = (r[:, None] < r[None, :])
    m[:, 3] = (r[:, None] >= r[None, :])
    m[:, 4] = np.eye(128)
    c["masks"] = m
    for nm, L in (("b", SEQ), ("s", CTX)):
        N = 2 * L
        nfc = (L + 1 + 127) // 128
        P = nfc * 128
        u = np.arange(P, dtype=np.int64)
        ph = (u[:, None] * u[None, :]) % N
        ang = ph.astype(np.float64) * (2.0 * math.pi / N)
        valid = ((u[:, None] <= L) & (u[None, :] <= L))
        def lay(g):
            g = g.astype(f32).astype(ml_dtypes.bfloat16).reshape(nfc, 128, nfc, 128)
            return np.ascontiguousarray(g.transpose(2, 1, 0, 3))
        c["gc_" + nm] = lay(np.where(valid, np.cos(ang), 0.0))
        c["gs_" + nm] = lay(np.where(valid, np.sin(ang), 0.0))
        wf = np.zeros(P, np.float64)
        wf[:L + 1] = 2.0 / N
        wf[0] = 1.0 / N
        wf[L] = 1.0 / N
        c["wf_" + nm] = np.ascontiguousarray(wf.reshape(nfc, 128).T).astype(f32)
        t = np.linspace(0.0, 1.0, L, dtype=f32)[:, None]
        bands = 16
        w = (2.0 * math.pi * np.arange(L, dtype=f32)[:, None] / L).astype(f32)
        f = np.linspace(1e-4, bands - 1, bands, dtype=f32)[None, :]
        z = np.concatenate([t, np.cos(f * w), -np.sin(f * w)], axis=-1).astype(f32)
        c["zT_" + nm] = np.ascontiguousarray(z.T)
        deltas = np.linspace(math.log(1e-2) / 1.5, math.log(1e-2) / 0.3, HYW, dtype=f32)
        c["dec_" + nm] = np.exp(-t * np.abs(deltas)).astype(f32)
    return c


def _fm(v):
    v = np.asarray(v, np.float32)
    n = v.shape[-1] // 128
    v = v.reshape(v.shape[:-1] + (n, 128))
    return np.ascontiguousarray(np.moveaxis(v, -1, 0))


def _wl(w, kc):
    K, M = w.shape
    return np.ascontiguousarray(w.reshape(kc, 128, M // 128, 128).transpose(2, 1, 0, 3))


PP = {}


def _pp_layout():
    off = 0
    for nm, n in (("ng", DEPTH * 6 * 8), ("bm", DEPTH * 72), ("hcw", DEPTH * 3 * 6), ("hcb", DEPTH * 6),
                  ("scw", DEPTH * 3 * 8), ("scb", DEPTH * 8), ("ccw", DEPTH * 3 * 2), ("mg", DEPTH * 8),
                  ("alog", DEPTH * 16), ("sd", DEPTH * 8),
                  ("wfb", 33), ("wfs", 3)):
        PP[nm] = (off, n)
        off += n
    return off


NPP = _pp_layout()


def _host_shared(inp):
    f32 = np.float32
    sh = {}
    sh.update(_consts())
    pp = np.zeros((128, NPP), f32)

    def put(nm, a):
        o, n = PP[nm]
        pp[:, o:o + n] = np.asarray(a, f32).reshape(128, n)
    put("ng", _fm(inp["norm_g"]))
    put("bm", _fm(inp["b_mod"]))
    put("hcw", _fm(inp["hy_conv_w"]))
    put("hcb", _fm(inp["hy_conv_b"]))
    put("scw", _fm(inp["ssd_conv_w"]))
    put("scb", _fm(inp["ssd_conv_b"]))
    put("ccw", _fm(inp["sc_conv_w"]))
    put("mg", _fm(inp["mix_gain"]))
    sh["ppb"] = np.ascontiguousarray(np.concatenate([
        np.broadcast_to(np.asarray(inp["mix_gain"], f32)[:, None, :HYW], (DEPTH, 128, HYW)),
        np.broadcast_to(np.asarray(inp["hy_bias"], f32).reshape(DEPTH, 1, 512), (DEPTH, 128, 512))], axis=2))
    put("alog", np.broadcast_to(np.asarray(inp["ssd_a_log"]).reshape(1, DEPTH, 16), (128, DEPTH, 16)))
    put("sd", np.broadcast_to(np.asarray(inp["ssd_d"]).reshape(1, DEPTH, 8), (128, DEPTH, 8)))
    put("wfb", sh.pop("wf_b"))
    put("wfs", sh.pop("wf_s"))
    sh["pp"] = pp
    sh["wm"] = np.stack([_wl(np.asarray(inp["w_mod"][l], f32), 8) for l in range(DEPTH)])
    sh["fwi"] = np.stack([np.stack([_wl(np.asarray(inp["ffn_w_in"][l, j], f32), 8) for j in range(2)])
                          for l in range(DEPTH)])
    sh["fwo"] = np.stack([np.stack([_wl(np.asarray(inp["ffn_w_out"][l, j], f32), 22) for j in range(2)])
                          for l in range(DEPTH)])
    wi = np.asarray(inp["w_in"], f32)
    wip = np.zeros((DEPTH, D, NPXC * 128), f32)
    wip[:, :, 0:2304] = wi[:, :, 0:2304]
    wip[:, :, 2304:2320] = wi[:, :, 2304:2320]
    wip[:, :, 2432:3200] = wi[:, :, 2320:3088]
    sh["wi"] = np.stack([_wl(wip[l], 8) for l in range(DEPTH)])
    sh["wo"] = np.stack([_wl(np.asarray(inp["w_out"][l], f32), 8) for l in range(DEPTH)])
    sh["fw1"] = np.ascontiguousarray(np.asarray(inp["hy_fw1"], f32).transpose(1, 0, 2))
    sh["fw23"] = np.ascontiguousarray(np.stack([np.asarray(inp["hy_fw2"], f32), np.asarray(inp["hy_fw3"], f32)],
                                               axis=1).transpose(2, 0, 1, 3))
    sh["fw4"] = np.ascontiguousarray(np.asarray(inp["hy_fw4"], f32))
    sh["fpp"] = np.ascontiguousarray(np.stack([np.asarray(inp["hy_fb1"], f32), np.asarray(inp["hy_fb2"], f32),
                                               np.asarray(inp["hy_fb3"], f32), np.asarray(inp["hy_freq"], f32)],
                                              axis=1).transpose(2, 0, 1))
    sh["dtb"] = np.ascontiguousarray(np.asarray(inp["ssd_dt_bias"], f32).reshape(DEPTH, 16).T)
    return sh


SHARED_SHAPES = None


def build(depth, shapes, dbg=None):
    nc = bass.Bass("TRN2", target_bir_lowering=False)
    din = {k: nc.dram_tensor(k, list(s), BF16 if k[:3] in ("gc_", "gs_") else F32, kind="ExternalInput").ap() for k, s in shapes.items()}
    out = nc.dram_tensor("out", [BPC, 8, 128, SEQ], F32, kind="ExternalOutput").ap()

    def scratch(nm, shape, nt, dt=F32):
        return DT(nc.dram_tensor(nm, list(shape), dt, kind="Internal").ap(), nt)

    with contextlib.ExitStack() as st:
        S = Sched(nc, st)
        st.enter_context(nc.allow_non_contiguous_dma(reason="small strided halo / layout DMAs"))

        def sb(nm, shape, dt=F32):
            return st.enter_context(nc.sbuf_tensor("sb_" + nm, list(shape), dt))

        class Ar:
            def __init__(self, t, size):
                self.t = t; self.size = size; self.o = 0

            def take(self, n, pat=None, parts=None, **kw):
                assert self.o + n <= self.size, (self.o, n, self.size)
                v = self.t[:, self.o:self.o + n] if parts is None else self.t[0:parts, self.o:self.o + n]
                self.o += n
                return V(v.rearrange(pat, **kw) if pat else v)

        def phase_start():
            S.barrier()
            A.o = 0; B.o = 0; C.o = 0

        def ps(nm):
            return V(st.enter_context(nc.psum_tensor(nm, [128, 512], F32)))
        P = [ps("ps%d" % i) for i in range(8)]
        AR_A = sb("arA", [128, 10752])
        AR_B = sb("arB", [128, 16384])
        AR_C = sb("arC", [128, 34048], BF16)
        A = Ar(AR_A, 10752); B = Ar(AR_B, 16384); C = Ar(AR_C, 34048)
        NW = 8
        WPOOL = [V(sb("w%d" % i, [128, 8, 128], BF16)) for i in range(NW)]
        wrr = [0]

        def wnext():
            wrr[0] = (wrr[0] + 1) % NW
            return WPOOL[wrr[0]]
        masks = V(sb("masks", [128, 5, 128]))
        ppt = V(sb("pp", [128, NPP]))
        ones = V(sb("ones", [128, 128]))
        cst = V(sb("cst", [128, 4]))
        sct = V(sb("sct", [128, 8, 3]))
        MOD = V(sb("mod", [128, 9, 8, 3]))
        GIN = V(sb("gin", [128, 3, 8, 3]))
        COUT = V(sb("cout", [128, 3, 8, 3]))
        Aneg = V(sb("aneg", [128, 16]))
        fw1 = V(sb("fw1", [33, 4, 64])); fw23 = V(sb("fw23", [64, 4, 2, 64])); ppb = V(sb("ppb", [128, 768]))
        fpp = V(sb("fpp", [64, 4, 4])); dtb = V(sb("dtb", [16, 4]))
        HSTD = scratch("hstd", [2, 2, 128, 512], 1)
        small = V(sb("small", [128, 64]))

        def pp(nm, *idx_shape):
            o, n = PP[nm]
            return ppt.t[:, o:o + n]

        def viewA(o, n):
            return V(AR_A[:, o:o + n])

        def viewB(o, n):
            return V(AR_B[:, o:o + n])

        S.dma("sp", masks.t[:], din["masks"], writes=[masks])
        S.dma("sp", ppt.t[:], din["pp"], writes=[ppt])
        S.dma("sp", sct.t[:], din["cT"], writes=[sct])
        S.dma("sp", fw1.t[:], din["fw1"], writes=[fw1])
        S.dma("sp", fw23.t[:], din["fw23"], writes=[fw23])
        S.dma("sp", fpp.t[:], din["fpp"], writes=[fpp])
        S.dma("sp", dtb.t[:], din["dtb"], writes=[dtb])
        S.op("dve", lambda e: e.memset(ones.t[:], 1.0), writes=[ones])
        S.op("dve", lambda e: e.memset(cst.t[:, 0:1], EPS), writes=[cst])
        S.op("dve", lambda e: e.memset(cst.t[:, 1:2], 1.0), writes=[cst])
        S.op("dve", lambda e: e.memset(cst.t[:, 2:3], 0.0), writes=[cst])
        S.op("act", lambda e: e.activation(out=sct.t[:], in_=sct.t[:], func=AF.Silu), reads=[sct], writes=[sct])
        ident = masks.t[:, 4, :]

        class Seq:
            pass
        seqs = []
        for s in range(2 * BPC):
            q = Seq()
            q.ctx = s >= BPC
            q.bl = s % BPC
            q.L = CTX if q.ctx else SEQ
            q.T = min(512, q.L)
            q.who = 2 if q.ctx else q.bl
            q.nt = q.L // 128
            q.nm = "s" if q.ctx else "b"
            src = din["xc"] if q.ctx else din["xl"]
            q.xin = DT(src[q.bl], q.nt)
            q.xs = scratch("xs%d" % s, [8, 128, q.L], q.nt)
            q.px = scratch("px%d" % s, [NPXC, 128, q.L], q.nt)
            q.ym = scratch("ym%d" % s, [8, 128, q.L], q.nt, BF16)
            q.yf = scratch("yf%d" % s, [q.L, 512], q.nt)
            q.hv = scratch("hv%d" % s, [q.L, 768], q.nt)
            q.z1 = scratch("z1%d" % s, [q.L, 256], q.nt)
            q.nfc = (q.L + 1 + 127) // 128
            q.first = True
            seqs.append(q)
        KS = {}
        EO = {}
        for nm, L in (("b", SEQ), ("s", CTX)):
            nfc = (L + 1 + 127) // 128
            KS[nm] = scratch("ks_" + nm, [nfc, 128, 2, 2, 256], 1)
            EO[nm] = scratch("eo_" + nm, [2, L, 512], 1, BF16)
        GT_ = {nm: (DT(din["gc_" + nm], 1), DT(din["gs_" + nm], 1)) for nm in "bs"}

        def xsrc(q):
            return q.xin if q.first else q.xs

        def rms(src_aps, T, nD, rstd, sq2, pn, reads, lnexp=False):
            n = len(src_aps)
            for i, a in enumerate(src_aps):
                sq = sq2[i % 2]
                S.op("act", lambda e, a=a, sq=sq: e.activation(out=sq.t[:, :T], in_=a, func=AF.Square),
                     reads=reads, writes=[sq])
                S.op("pe", lambda e, sq=sq, i=i: e.matmul(pn.t[:, :T], ones.t[:], sq.t[:, :T], start=(i == 0),
                                                           stop=(i == n - 1)), reads=[sq, ones], writes=[pn])
            if lnexp:
                S.op("act", lambda e: e.activation(out=rstd.t[:, :T], in_=pn.t[:, :T], func=AF.Ln,
                                                   bias=cst.t[:, 0:1], scale=1.0 / nD), reads=[pn, cst], writes=[rstd])
                S.op("act", lambda e: e.activation(out=rstd.t[:, :T], in_=rstd.t[:, :T], func=AF.Exp, scale=-0.5),
                     reads=[rstd], writes=[rstd])
                return
            S.op("act", lambda e: e.activation(out=rstd.t[:, :T], in_=pn.t[:, :T], func=AF.Sqrt,
                                               bias=cst.t[:, 0:1], scale=1.0 / nD), reads=[pn, cst], writes=[rstd])
            S.op("dve", lambda e: e.reciprocal(out=rstd.t[:, :T], in_=rstd.t[:, :T]), reads=[rstd], writes=[rstd])

        def load_norm(q, sub, t0, T, xt, ht, rstd, sq2, pn, tmp2):
            xsr = xsrc(q)
            S.dma("sp", xt.t[:, :, :T], xsr.ap[:, :, t0:t0 + T].rearrange("c p t -> p c t"),
                  reads=xsr.tk(t0, t0 + T), writes=[xt])
            rms([xt.t[:, kc, :T] for kc in range(8)], T, D, rstd, sq2, pn, [xt])
            for kc in range(8):
                tp = tmp2[kc % 2]
                S.op("dve", lambda e, kc=kc, tp=tp: e.scalar_tensor_tensor(
                    out=tp.t[:, :T], in0=xt.t[:, kc, :T], scalar=GIN.t[:, sub, kc, q.who:q.who + 1],
                    in1=rstd.t[:, :T], op0=ALU.mult, op1=ALU.mult), reads=[xt, GIN, rstd], writes=[tp])
                S.op("act", lambda e, kc=kc, tp=tp: e.activation(
                    out=ht.t[:, kc, :T], in_=tp.t[:, :T], func=AF.Identity,
                    bias=MOD.t[:, 3 * sub, kc, q.who:q.who + 1], scale=1.0), reads=[tp, MOD], writes=[ht])

        def resid_store(q, sub, t0, T, xt, yt, rstd, sq2, pn, tmp, dst=None):
            rms([yt.t[:, kc, :T] for kc in range(8)], T, D, rstd, sq2, pn, [yt])
            for kc in range(8):
                S.op("dve", lambda e, kc=kc: e.scalar_tensor_tensor(
                    out=yt.t[:, kc, :T], in0=yt.t[:, kc, :T], scalar=COUT.t[:, sub, kc, q.who:q.who + 1],
                    in1=rstd.t[:, :T], op0=ALU.mult, op1=ALU.mult), reads=[yt, COUT, rstd], writes=[yt])
                S.op("dve", lambda e, kc=kc: e.tensor_tensor(out=xt.t[:, kc, :T], in0=xt.t[:, kc, :T],
                                                             in1=yt.t[:, kc, :T], op=ALU.add),
                     reads=[xt, yt], writes=[xt])
            d = dst or q.xs
            S.dma("pool", d.ap[:, :, t0:t0 + T].rearrange("c p t -> p c t"), xt.t[:, :, :T],
                  reads=[xt], writes=d.tk(t0, t0 + T))

        W16 = {}

        def convert_weights():
            phase_start()
            stg = [B.take(4096) for _ in range(3)]
            out16 = [C.take(4096) for _ in range(3)]
            engs = ["act", "dve", "pool"]
            n = [0]
            for nm, grp in (("fwi", 4), ("fwo", 1), ("wi", 4), ("wo", 4)):
                src = din[nm]
                shp = list(src.shape)
                W16[nm] = DT(nc.dram_tensor(nm + "16", shp, BF16, kind="Internal").ap(), 1)
                dst = W16[nm].ap
                lead = shp[:-4]
                noc = shp[-4]
                F = shp[-2] * shp[-1]
                idxs = [()]
                for d_ in lead:
                    idxs = [i + (k,) for i in idxs for k in range(d_)]
                for ix in idxs:
                    sa = src; da = dst
                    for k in ix:
                        sa = sa[k]; da = da[k]
                    for o0 in range(0, noc, grp):
                        g = min(grp, noc - o0)
                        i = n[0] % 3; n[0] += 1
                        sv = stg[i].t[:, 0:g * F].rearrange("p (o f) -> p o f", o=g)
                        ov = out16[i].t[:, 0:g * F].rearrange("p (o f) -> p o f", o=g)
                        S.dma("sp", sv, sa[o0:o0 + g].rearrange("o p k m -> p o (k m)"), writes=[stg[i]])
                        if engs[i] == "act":
                            S.op("act", lambda e, sv=sv, ov=ov: e.activation(out=ov, in_=sv, func=AF.Copy),
                                 reads=[stg[i]], writes=[out16[i]])
                        else:
                            S.op(engs[i], lambda e, sv=sv, ov=ov: e.tensor_copy(out=ov, in_=sv),
                                 reads=[stg[i]], writes=[out16[i]])
                        S.dma("pool", da[o0:o0 + g].rearrange("o p k m -> p o (k m)"), ov, reads=[out16[i]])

        def mod_phase(l):
            phase_start()
            wmd = DT(din["wm"], 1)
            WF32 = [B.take(1024, "p (k m) -> p k m", k=8) for i in range(4)]
            for oc in range(72):
                w = WF32[oc % 4]
                S.dma("sp", w.t[:], wmd.ap[l, oc], writes=[w])
                pm = P[oc % 2]
                for kc in range(8):
                    S.op("pe", lambda e, kc=kc, w=w, pm=pm: e.matmul(pm.t[:, 0:3], w.t[:, kc, :], sct.t[:, kc, :],
                                                                     start=(kc == 0), stop=(kc == 7)),
                         reads=[w, sct], writes=[pm])
                o = PP["bm"][0] + l * 72 + oc
                S.op("act", lambda e, oc=oc, pm=pm, o=o: e.activation(
                    out=MOD.t[:, oc // 8, oc % 8, :], in_=pm.t[:, 0:3], func=AF.Identity,
                    bias=ppt.t[:, o:o + 1], scale=1.0), reads=[pm, ppt], writes=[MOD])
            ngo = PP["ng"][0] + l * 48
            for i in range(3):
                gpre = ppt.t[:, ngo + (2 * i) * 8: ngo + (2 * i) * 8 + 8].unsqueeze(2).broadcast_to([128, 8, 3])
                gpost = ppt.t[:, ngo + (2 * i + 1) * 8: ngo + (2 * i + 1) * 8 + 8].unsqueeze(2).broadcast_to([128, 8, 3])
                S.op("dve", lambda e, i=i: e.tensor_scalar(out=GIN.t[:, i], in0=MOD.t[:, 3 * i + 1], scalar1=1.0,
                                                           scalar2=None, op0=ALU.add), reads=[MOD], writes=[GIN])
                S.op("dve", lambda e, i=i, g=gpre: e.tensor_tensor(out=GIN.t[:, i], in0=GIN.t[:, i], in1=g,
                                                                   op=ALU.mult), reads=[GIN, ppt], writes=[GIN])
                rw = 1.0 if i == 1 else 0.5
                S.op("dve", lambda e, i=i, g=gpost, rw=rw: e.scalar_tensor_tensor(
                    out=COUT.t[:, i], in0=MOD.t[:, 3 * i + 2], scalar=rw, in1=g, op0=ALU.mult, op1=ALU.mult),
                    reads=[MOD, ppt], writes=[COUT])
            o = PP["alog"][0] + l * 16
            S.op("act", lambda e: e.activation(out=Aneg.t[:], in_=ppt.t[:, o:o + 16], func=AF.Exp),
                 reads=[ppt], writes=[Aneg])
            S.op("dve", lambda e: e.tensor_scalar(out=Aneg.t[:], in0=Aneg.t[:], scalar1=-1.0, scalar2=None,
                                                  op0=ALU.mult), reads=[Aneg], writes=[Aneg])

        def ffn_phase(q, l, j):
            phase_start()
            sub = 2 * j
            T = q.T
            xts = [B.take(4096, "p (c t) -> p c t", c=8) for i in range(2)]
            yt = B.take(4096, "p (c t) -> p c t", c=8)
            ht = C.take(4096, "p (c t) -> p c t", c=8)
            at = C.take(11264, "p (c t) -> p c t", c=22)
            wos = [C.take(2816, "p (c m) -> p c m", c=22) for i in range(3)]
            sq2 = [A.take(512) for i in range(2)]
            rstd = A.take(512)
            sg2 = [A.take(512) for i in range(2)]
            tmp2 = [A.take(512) for i in range(2)]
            fwi = W16["fwi"]
            fwo = W16["fwo"]
            for tt in range(q.L // T):
                t0 = tt * T
                xt = xts[tt % 2]
                load_norm(q, sub, t0, T, xt, ht, rstd, sq2, P[7], tmp2)
                for jf in range(22):
                    wg = wnext(); S.dma("sp", wg.t[:], fwi.ap[l, j, jf], writes=[wg])
                    wu = wnext(); S.dma("sp", wu.t[:], fwi.ap[l, j, 22 + jf], writes=[wu])
                    pg = P[(2 * jf) % 4]; pu = P[(2 * jf) % 4 + 1]
                    for kc in range(8):
                        S.op("pe", lambda e, kc=kc, wg=wg, pg=pg: e.matmul(pg.t[:, :T], wg.t[:, kc, :], ht.t[:, kc, :T],
                                                                           start=(kc == 0), stop=(kc == 7)),
                             reads=[wg, ht], writes=[pg])
                    for kc in range(8):
                        S.op("pe", lambda e, kc=kc, wu=wu, pu=pu: e.matmul(pu.t[:, :T], wu.t[:, kc, :], ht.t[:, kc, :T],
                                                                           start=(kc == 0), stop=(kc == 7)),
                             reads=[wu, ht], writes=[pu])
                    sg = sg2[jf % 2]
                    S.op("act", lambda e, sg=sg, pg=pg: e.activation(out=sg.t[:, :T], in_=pg.t[:, :T], func=AF.Silu),
                         reads=[pg], writes=[sg])
                    S.op("dve", lambda e, sg=sg, pu=pu, jf=jf: e.tensor_tensor(out=at.t[:, jf, :T], in0=sg.t[:, :T],
                                                                               in1=pu.t[:, :T], op=ALU.mult),
                         reads=[sg, pu], writes=[at])
                for oc in range(8):
                    wo = wos[oc % 3]
                    S.dma("sp", wo.t[:], fwo.ap[l, j, oc], writes=[wo])
                    py = P[4 + oc % 2]
                    for fc in range(22):
                        S.op("pe", lambda e, fc=fc, wo=wo, py=py: e.matmul(py.t[:, :T], wo.t[:, fc, :], at.t[:, fc, :T],
                                                                           start=(fc == 0), stop=(fc == 21)),
                             reads=[wo, at], writes=[py])
                    S.op("act", lambda e, oc=oc, py=py: e.activation(out=yt.t[:, oc, :T], in_=py.t[:, :T],
                                                                     func=AF.Copy), reads=[py], writes=[yt])
                resid_store(q, sub, t0, T, xt, yt, rstd, sq2, P[7], None)
            q.first = False


        def proj_phase(q, l):
            phase_start()
            T = q.T
            xts = [B.take(4096, "p (c t) -> p c t", c=8) for i in range(2)]
            ht = C.take(4096, "p (c t) -> p c t", c=8)
            sq2 = [A.take(512) for i in range(2)]
            rstd = A.take(512)
            tmp2 = [A.take(512) for i in range(2)]
            ot4 = [A.take(512) for i in range(4)]
            wi = W16["wi"]
            for tt in range(q.L // T):
                t0 = tt * T
                xt = xts[tt % 2]
                load_norm(q, 1, t0, T, xt, ht, rstd, sq2, P[7], tmp2)
                for oc in range(NPXC):
                    w = wnext(); S.dma("sp", w.t[:], wi.ap[l, oc], writes=[w])
                    pm = P[oc % 4]; o = ot4[oc % 4]
                    for kc in range(8):
                        S.op("pe", lambda e, kc=kc, w=w, pm=pm: e.matmul(pm.t[:, :T], w.t[:, kc, :], ht.t[:, kc, :T],
                                                                         start=(kc == 0), stop=(kc == 7)),
                             reads=[w, ht], writes=[pm])
                    S.op("act", lambda e, o=o, pm=pm: e.activation(out=o.t[:, :T], in_=pm.t[:, :T], func=AF.Copy),
                         reads=[pm], writes=[o])
                    S.dma("pool", q.px.ap[oc, :, t0:t0 + T], o.t[:, :T], reads=[o], writes=q.px.tk(t0, t0 + T))

        def conv3(acc_ap, raw_ap3, SL, wo, nw, kc, accv, rawv):
            def wj(j):
                o = wo + j * nw + kc
                return ppt.t[:, o:o + 1]
            S.op("dve", lambda e: e.tensor_scalar(out=acc_ap, in0=raw_ap3[:, :, 1:SL + 1], scalar1=wj(1), scalar2=None,
                                                  op0=ALU.mult), reads=[rawv, ppt], writes=[accv])
            S.op("dve", lambda e: e.scalar_tensor_tensor(out=acc_ap, in0=raw_ap3[:, :, 0:SL], scalar=wj(0), in1=acc_ap,
                                                         op0=ALU.mult, op1=ALU.add), reads=[rawv, ppt, accv], writes=[accv])
            S.op("dve", lambda e: e.scalar_tensor_tensor(out=acc_ap, in0=raw_ap3[:, :, 2:SL + 2], scalar=wj(2), in1=acc_ap,
                                                         op0=ALU.mult, op1=ALU.add), reads=[rawv, ppt, accv], writes=[accv])

        def load_halo(q, raw, c0, nch, t0, n, SL):
            nsub = n // SL
            S.op("dve", lambda e: e.memset(raw.t[:], 0.0), writes=[raw])
            for s_ in range(nsub):
                a = t0 + s_ * SL
                S.dma("sp", raw.t[:, :, s_, 1:SL + 1], q.px.ap[c0:c0 + nch, :, a:a + SL].rearrange("c p t -> p c t"),
                      reads=q.px.tk(a, a + SL), writes=[raw])
            if q.ctx:
                if t0 > 0:
                    S.dma("sp", raw.t[:, :, 0, 0:1], q.px.ap[c0:c0 + nch, :, t0 - 1:t0].rearrange("c p t -> p c t"),
                          reads=q.px.tk(t0 - 1, t0), writes=[raw])
                if t0 + n < q.L:
                    S.dma("sp", raw.t[:, :, nsub - 1, SL + 1:SL + 2],
                          q.px.ap[c0:c0 + nch, :, t0 + n:t0 + n + 1].rearrange("c p t -> p c t"),
                          reads=q.px.tk(t0 + n, t0 + n + 1), writes=[raw])

        def sc_phase(q, l):
            phase_start()
            T = q.T
            SL = 64 if not q.ctx else T
            nsub = T // SL
            gt = B.take(6 * T, "p (c t) -> p c t", c=6)
            raw = B.take(2 * nsub * (SL + 2), "p (c s t) -> p c s t", c=2, s=nsub)
            acc = B.take(2 * T, "p (c t) -> p c t", c=2)
            o16 = C.take(2 * T, "p (c t) -> p c t", c=2)
            sq2 = [A.take(512) for i in range(2)]
            rstd = A.take(512)
            for tt in range(q.L // T):
                t0 = tt * T
                S.dma("sp", gt.t[:, :, :T], q.px.ap[19:25, :, t0:t0 + T].rearrange("c p t -> p c t"),
                      reads=q.px.tk(t0, t0 + T), writes=[gt])
                S.op("dve", lambda e: e.memset(raw.t[:], 0.0), writes=[raw])
                for c2 in range(2):
                    S.op("dve", lambda e, c2=c2: e.tensor_tensor(
                        out=raw.t[:, c2, :, 1:SL + 1], in0=gt.t[:, 2 + c2, :T].rearrange("p (s t) -> p s t", s=nsub),
                        in1=gt.t[:, 4 + c2, :T].rearrange("p (s t) -> p s t", s=nsub), op=ALU.mult),
                        reads=[gt], writes=[raw])
                    accap = acc.t[:, c2, :T].rearrange("p (s t) -> p s t", s=nsub)
                    conv3(accap, raw.t[:, c2], SL, PP["ccw"][0] + l * 6, 2, c2, acc, raw)
                    S.op("dve", lambda e, c2=c2: e.tensor_tensor(out=acc.t[:, c2, :T], in0=acc.t[:, c2, :T],
                                                                 in1=gt.t[:, c2, :T], op=ALU.mult),
                         reads=[acc, gt], writes=[acc])
                rms([acc.t[:, c2, :T] for c2 in range(2)], T, 256, rstd, sq2, P[7], [acc])
                for c2 in range(2):
                    o = PP["mg"][0] + l * 8 + 6 + c2
                    S.op("dve", lambda e, c2=c2, o=o: e.scalar_tensor_tensor(
                        out=o16.t[:, c2, :T], in0=acc.t[:, c2, :T], scalar=ppt.t[:, o:o + 1], in1=rstd.t[:, :T],
                        op0=ALU.mult, op1=ALU.mult), reads=[acc, ppt, rstd], writes=[o16])
                S.dma("pool", q.ym.ap[6:8, :, t0:t0 + T].rearrange("c p t -> p c t"), o16.t[:, :, :T],
                      reads=[o16], writes=q.ym.tk(t0, t0 + T))

        def hyprep_phase(q, l):
            phase_start()
            n = 128
            SL = 64 if not q.ctx else 128
            nsub = n // SL
            raws = [B.take(6 * nsub * (SL + 2), "p (c s t) -> p c s t", c=6, s=nsub) for i in range(2)]
            acc = B.take(768, "p (c t) -> p c t", c=6)
            ots = [A.take(768) for i in range(2)]
            for ci in range(q.nt):
                t0 = ci * 128
                raw = raws[ci % 2]
                load_halo(q, raw, 0, 6, t0, n, SL)
                for kc in range(6):
                    accap = acc.t[:, kc, :].rearrange("p (s t) -> p s t", s=nsub)
                    conv3(accap, raw.t[:, kc], SL, PP["hcw"][0] + l * 18, 6, kc, acc, raw)
                    o = PP["hcb"][0] + l * 6 + kc
                    S.op("act", lambda e, kc=kc, o=o: e.activation(out=acc.t[:, kc, :], in_=acc.t[:, kc, :],
                                                                   func=AF.Identity, bias=ppt.t[:, o:o + 1], scale=1.0),
                         reads=[acc, ppt], writes=[acc])
                    pt = P[kc // 4]
                    S.op("pe", lambda e, kc=kc, pt=pt: e.transpose(pt.t[:, (kc % 4) * 128:(kc % 4 + 1) * 128],
                                                                    acc.t[:, kc, :], ident),
                         reads=[acc, masks], writes=[pt])
                ot = ots[ci % 2]
                S.op("act", lambda e, ot=ot: e.activation(out=ot.t[:, 0:512], in_=P[0].t[:, 0:512], func=AF.Copy),
                     reads=[P[0]], writes=[ot])
                S.op("act", lambda e, ot=ot: e.activation(out=ot.t[:, 512:768], in_=P[1].t[:, 0:256], func=AF.Copy),
                     reads=[P[1]], writes=[ot])
                S.dma("pool", q.hv.ap[t0:t0 + 128, :], ot.t[:], reads=[ot], writes=q.hv.tk(t0, t0 + 128))

        def filter_phase(l, nm):
            phase_start()
            L = SEQ if nm == "b" else CTX
            T = min(512, L)
            zt2 = [A.take(512, parts=33) for i in range(2)]
            hs = [A.take(512, parts=64) for i in range(3)]
            arg = A.take(512, parts=64); kk = A.take(512, parts=64)
            dec2 = [A.take(256) for i in range(2)]
            hf2 = [A.take(1024) for i in range(2)]
            eo2 = [C.take(1024) for i in range(2)]
            fw4 = B.take(1024, parts=64)
            S.dma("sp", fw4.t[:], din["fw4"][l], writes=[fw4])
            PI_LO = 3.1415925
            for tt in range(L // T):
                t0 = tt * T
                zt = zt2[tt % 2]
                S.dma("sp", zt.t[:, :T], din["zT_" + nm][:, t0:t0 + T], writes=[zt])
                src, K = zt, 33
                for i in range(3):
                    lhsT = fw1.t[:, l, :] if i == 0 else fw23.t[:, l, i - 1, :]
                    wv = fw1 if i == 0 else fw23
                    S.op("pe", lambda e, lhsT=lhsT, src=src, K=K: e.matmul(P[0].t[0:64, :T], lhsT, src.t[0:K, :T],
                                                                          start=True, stop=True),
                         reads=[wv, src], writes=[P[0]])
                    S.op("dve", lambda e, i=i: e.tensor_scalar(out=arg.t[:, :T], in0=P[0].t[0:64, :T],
                                                               scalar1=fpp.t[:, l, i:i + 1], scalar2=fpp.t[:, l, 3:4],
                                                               op0=ALU.add, op1=ALU.mult), reads=[P[0], fpp], writes=[arg])
                    S.op("dve", lambda e: e.tensor_scalar(out=kk.t[:, :T], in0=arg.t[:, :T], scalar1=1.0 / TWO_PI,
                                                          scalar2=MAGIC, op0=ALU.mult, op1=ALU.add),
                         reads=[arg], writes=[kk])
                    S.op("dve", lambda e: e.tensor_scalar(out=kk.t[:, :T], in0=kk.t[:, :T], scalar1=-MAGIC,
                                                          scalar2=-TWO_PI, op0=ALU.add, op1=ALU.mult),
                         reads=[kk], writes=[kk])
                    S.op("dve", lambda e: e.tensor_tensor(out=arg.t[:, :T], in0=arg.t[:, :T], in1=kk.t[:, :T],
                                                          op=ALU.add), reads=[arg, kk], writes=[arg])
                    S.op("dve", lambda e: e.tensor_scalar(out=arg.t[:, :T], in0=arg.t[:, :T], scalar1=-PI_LO,
                                                          scalar2=PI_LO, op0=ALU.max, op1=ALU.min),
                         reads=[arg], writes=[arg])
                    h = hs[i]
                    S.op("act", lambda e, h=h: e.activation(out=h.t[:, :T], in_=arg.t[:, :T], func=AF.Sin),
                         reads=[arg], writes=[h])
                    src, K = h, 64
                for sub in range(T // 128):
                    ti = tt * (T // 128) + sub
                    dec = dec2[ti % 2]; hf = hf2[ti % 2]; eo = eo2[ti % 2]
                    S.dma("sp", dec.t[:], din["dec_" + nm][ti * 128:(ti + 1) * 128, :], writes=[dec])
                    for o in range(2):
                        S.op("pe", lambda e, o=o, sub=sub: e.matmul(P[1 + o].t[:, 0:512], hs[2].t[:, sub * 128:(sub + 1) * 128],
                                                                    fw4.t[:, o * 512:(o + 1) * 512], start=True, stop=True),
                             reads=[hs[2], fw4], writes=[P[1 + o]])
                        S.op("dve", lambda e, o=o, hf=hf, dec=dec: e.tensor_tensor(
                            out=hf.t[:, o * 512:(o + 1) * 512].rearrange("p (d c) -> p d c", d=2),
                            in0=P[1 + o].t[:, 0:512].rearrange("p (d c) -> p d c", d=2),
                            in1=dec.t[:].unsqueeze(1).broadcast_to([128, 2, 256]), op=ALU.mult),
                            reads=[P[1 + o], dec], writes=[hf])
                    if ti == 0:
                        for o in range(2):
                            S.op("dve", lambda e, o=o, hf=hf: e.memset(hf.t[0:1, o * 512 + 256:o * 512 + 512], 0.0),
                                 writes=[hf])
                    hv4 = hf.t[:].rearrange("p (o d c) -> p o d c", o=2, d=2)
                    S.op("dve", lambda e, eo=eo, hv4=hv4: e.tensor_tensor(
                        out=eo.t[:, 0:512].rearrange("p (o c) -> p o c", o=2), in0=hv4[:, :, 0, :], in1=hv4[:, :, 1, :],
                        op=ALU.add), reads=[hf], writes=[eo])
                    S.op("dve", lambda e, eo=eo, hv4=hv4: e.tensor_tensor(
                        out=eo.t[:, 512:1024].rearrange("p (o c) -> p o c", o=2), in0=hv4[:, :, 1, :], in1=hv4[:, :, 0, :],
                        op=ALU.subtract), reads=[hf], writes=[eo])
                    for ri in range(2):
                        S.dma("pool", EO[nm].ap[ri, ti * 128:(ti + 1) * 128, :], eo.t[:, ri * 512:(ri + 1) * 512],
                              reads=[eo], writes=EO[nm].tk())

        def load_g(G, gt, col, nrow):
            S.dma("sp", gt.t[:, 0:nrow, :], G.ap[col, :, 0:nrow, :], writes=[gt])

        def kspec_phase(l, nm):
            phase_start()
            L = SEQ if nm == "b" else CTX
            NT = L // 128
            nfc = (L + 1 + 127) // 128
            rhs = C.take(NT * 512, "p (i c) -> p i c", i=NT)
            gts = [C.take(4224, "p (i v) -> p i v", i=33) for i in range(3)]
            kts = [A.take(512) for i in range(2)]
            wfo = PP["wfb" if nm == "b" else "wfs"][0]
            for ri in range(2):
                S.dma("sp", rhs.t[:], EO[nm].ap[ri].rearrange("(i p) c -> p i c", p=128), reads=EO[nm].tk(), writes=[rhs])
                for fc in range(nfc):
                    gt = gts[fc % 3]
                    load_g(GT_[nm][ri], gt, fc, NT)
                    pk = P[fc % 2]
                    for i in range(NT):
                        S.op("pe", lambda e, i=i, gt=gt, pk=pk: e.matmul(pk.t[:, 0:512], gt.t[:, i, :], rhs.t[:, i, :],
                                                                         start=(i == 0), stop=(i == NT - 1)),
                             reads=[gt, rhs], writes=[pk])
                    kt = kts[fc % 2]
                    S.op("act", lambda e, kt=kt, pk=pk, fc=fc: e.activation(out=kt.t[:], in_=pk.t[:, 0:512], func=AF.Copy,
                                                                            scale=ppt.t[:, wfo + fc:wfo + fc + 1]),
                         reads=[pk, ppt], writes=[kt])
                    S.dma("pool", KS[nm].ap[fc, :, :, ri, :], kt.t[:].rearrange("p (o c) -> p o c", o=2),
                          reads=[kt], writes=KS[nm].tk())

        def conv_phase(q, l, order):
            phase_start()
            NT = q.nt; nfc = q.nfc; nm = q.nm
            zt = B.take(NT * 256, "p (i c) -> p i c", i=NT)
            z16 = C.take(NT * 256, "p (i c) -> p i c", i=NT)
            Y = C.take(nfc * 512, "p (f r c) -> p f r c", f=nfc, r=2)
            gts4 = [C.take(4224, "p (i v) -> p i v", i=33) for i in range(2)]
            gts = gts4
            kt2 = [A.take(512, "p (r c) -> p r c", r=2) for i in range(2)]
            tm = [A.take(256) for i in range(4)]
            gate2 = [A.take(256) for i in range(2)]
            ot2 = [C.take(256) for i in range(2)]
            srcd = q.hv if order == 0 else q.z1
            srcap = q.hv.ap[:, 0:256] if order == 0 else q.z1.ap
            S.dma("sp", zt.t[:], srcap.rearrange("(i p) c -> p i c", p=128), reads=srcd.tk(), writes=[zt])
            S.op("pool", lambda e: e.tensor_copy(out=z16.t[:], in_=zt.t[:]), reads=[zt], writes=[z16])
            Gc, Gs = GT_[nm]
            for fc in range(nfc):
                load_g(Gc, gts[0], fc, NT)
                load_g(Gs, gts[1], fc, NT)
                for i in range(NT):
                    S.op("pe", lambda e, i=i: e.matmul(P[0].t[:, 0:256], gts[0].t[:, i, :], z16.t[:, i, :], start=(i == 0),
                                                       stop=(i == NT - 1)), reads=[gts[0], z16], writes=[P[0]])
                for i in range(NT):
                    S.op("pe", lambda e, i=i: e.matmul(P[1].t[:, 0:256], gts[1].t[:, i, :], z16.t[:, i, :], start=(i == 0),
                                                       stop=(i == NT - 1)), reads=[gts[1], z16], writes=[P[1]])
                kt = kt2[fc % 2]
                S.dma("sp", kt.t[:], KS[nm].ap[fc, :, order, :, :], reads=KS[nm].tk(), writes=[kt])
                pc, psn = P[0].t[:, 0:256], P[1].t[:, 0:256]
                S.op("dve", lambda e, kt=kt: e.tensor_tensor(out=tm[0].t[:], in0=pc, in1=kt.t[:, 0, :], op=ALU.mult),
                     reads=[P[0], kt], writes=[tm[0]])
                S.op("dve", lambda e, kt=kt: e.tensor_tensor(out=tm[1].t[:], in0=psn, in1=kt.t[:, 1, :], op=ALU.mult),
                     reads=[P[1], kt], writes=[tm[1]])
                S.op("dve", lambda e, fc=fc: e.tensor_tensor(out=Y.t[:, fc, 0, :], in0=tm[0].t[:], in1=tm[1].t[:], op=ALU.add),
                     reads=[tm[0], tm[1]], writes=[Y])
                S.op("dve", lambda e, kt=kt: e.tensor_tensor(out=tm[2].t[:], in0=psn, in1=kt.t[:, 0, :], op=ALU.mult),
                     reads=[P[1], kt], writes=[tm[2]])
                S.op("dve", lambda e, kt=kt: e.tensor_tensor(out=tm[3].t[:], in0=pc, in1=kt.t[:, 1, :], op=ALU.mult),
                     reads=[P[0], kt], writes=[tm[3]])
                S.op("dve", lambda e, fc=fc: e.tensor_tensor(out=Y.t[:, fc, 1, :], in0=tm[2].t[:], in1=tm[3].t[:],
                                                             op=ALU.subtract), reads=[tm[2], tm[3]], writes=[Y])
            for tt in range(NT):
                load_g(Gc, gts[0], tt, nfc)
                load_g(Gs, gts[1], tt, nfc)
                py = P[2 + tt % 2]
                for i in range(nfc):
                    S.op("pe", lambda e, i=i, py=py: e.matmul(py.t[:, 0:256], gts[0].t[:, i, :], Y.t[:, i, 0, :],
                                                              start=(i == 0), stop=False), reads=[gts[0], Y], writes=[py])
                for i in range(nfc):
                    S.op("pe", lambda e, i=i, py=py: e.matmul(py.t[:, 0:256], gts[1].t[:, i, :], Y.t[:, i, 1, :],
                                                              start=False, stop=(i == nfc - 1)), reads=[gts[1], Y], writes=[py])
                gate = gate2[tt % 2]
                S.dma("sp", gate.t[:], q.hv.ap[tt * 128:(tt + 1) * 128, 256 * (order + 1):256 * (order + 2)],
                      reads=q.hv.tk(tt * 128, tt * 128 + 128), writes=[gate])
                t_ = tm[tt % 2]
                S.op("dve", lambda e, t_=t_, tt=tt: e.tensor_tensor(out=t_.t[:], in0=zt.t[:, tt, :],
                                                                    in1=ppb.t[:, 256 + order * 256:512 + order * 256],
                                                                    op=ALU.mult), reads=[zt, ppb], writes=[t_])
                S.op("dve", lambda e, t_=t_, py=py: e.tensor_tensor(out=t_.t[:], in0=t_.t[:], in1=py.t[:, 0:256], op=ALU.add),
                     reads=[t_, py], writes=[t_])
                S.op("dve", lambda e, t_=t_, gate=gate: e.tensor_tensor(out=t_.t[:], in0=t_.t[:], in1=gate.t[:], op=ALU.mult),
                     reads=[t_, gate], writes=[t_])
                if order == 0:
                    S.dma("pool", q.z1.ap[tt * 128:(tt + 1) * 128, :], t_.t[:], reads=[t_],
                          writes=q.z1.tk(tt * 128, tt * 128 + 128))
                else:
                    sq = tm[2 + tt % 2]
                    S.op("dve", lambda e, t_=t_, sq=sq: e.tensor_tensor(out=sq.t[:], in0=t_.t[:], in1=t_.t[:], op=ALU.mult),
                         reads=[t_], writes=[sq])
                    S.op("dve", lambda e, sq=sq: e.tensor_reduce(out=small.t[:, 0:1], in_=sq.t[:], op=ALU.add,
                                                                 axis=mybir.AxisListType.X), reads=[sq], writes=[small])
                    S.op("act", lambda e: e.activation(out=small.t[:, 1:2], in_=small.t[:, 0:1], func=AF.Sqrt,
                                                       bias=cst.t[:, 0:1], scale=1.0 / 256), reads=[small, cst], writes=[small])
                    S.op("dve", lambda e: e.reciprocal(out=small.t[:, 2:3], in_=small.t[:, 1:2]), reads=[small], writes=[small])
                    S.op("dve", lambda e, t_=t_: e.scalar_tensor_tensor(out=t_.t[:], in0=t_.t[:], scalar=small.t[:, 2:3],
                                                                        in1=ppb.t[:, 0:256], op0=ALU.mult, op1=ALU.mult),
                         reads=[t_, small, ppb], writes=[t_])
                    for c2 in range(2):
                        S.op("pe", lambda e, c2=c2, t_=t_: e.transpose(P[4].t[:, c2 * 128:(c2 + 1) * 128],
                                                                        t_.t[:, c2 * 128:(c2 + 1) * 128], ident),
                             reads=[t_, masks], writes=[P[4]])
                    ot = ot2[tt % 2]
                    S.op("act", lambda e, ot=ot: e.activation(out=ot.t[:], in_=P[4].t[:, 0:256], func=AF.Copy),
                         reads=[P[4]], writes=[ot])
                    S.dma("pool", q.ym.ap[0:2, :, tt * 128:(tt + 1) * 128].rearrange("c p t -> p c t"),
                          ot.t[:].rearrange("p (c t) -> p c t", c=2), reads=[ot], writes=q.ym.tk(tt * 128, tt * 128 + 128))

        def outproj_phase(q, l):
            phase_start()
            T = q.T
            xts = [B.take(4096, "p (c t) -> p c t", c=8) for i in range(2)]
            yt = B.take(4096, "p (c t) -> p c t", c=8)
            ymt = C.take(4096, "p (c t) -> p c t", c=8)
            sq2 = [A.take(512) for i in range(2)]
            rstd = A.take(512)
            wod = W16["wo"]
            for tt in range(q.L // T):
                t0 = tt * T
                xt = xts[tt % 2]
                S.dma("sp", xt.t[:, :, :T], q.xs.ap[:, :, t0:t0 + T].rearrange("c p t -> p c t"),
                      reads=q.xs.tk(t0, t0 + T), writes=[xt])
                S.dma("sp", ymt.t[:, :, :T], q.ym.ap[:, :, t0:t0 + T].rearrange("c p t -> p c t"),
                      reads=q.ym.tk(t0, t0 + T), writes=[ymt])
                for oc in range(8):
                    w = wnext(); S.dma("sp", w.t[:], wod.ap[l, oc], writes=[w])
                    py = P[oc % 2]
                    for kc in range(8):
                        S.op("pe", lambda e, kc=kc, w=w, py=py: e.matmul(py.t[:, :T], w.t[:, kc, :], ymt.t[:, kc, :T],
                                                                         start=(kc == 0), stop=(kc == 7)),
                             reads=[w, ymt], writes=[py])
                    S.op("act", lambda e, oc=oc, py=py: e.activation(out=yt.t[:, oc, :T], in_=py.t[:, :T], func=AF.Copy),
                         reads=[py], writes=[yt])
                resid_store(q, 1, t0, T, xt, yt, rstd, sq2, P[7], None)

        def ssd_phase(q, l, want_out):
            phase_start()
            NT = q.nt
            SL = 64 if not q.ctx else 128
            nsub = 128 // SL
            b = q.bl
            raws = [B.take(8 * nsub * (SL + 2), "p (c s t) -> p c s t", c=8, s=nsub) for _ in range(2)]
            acc = B.take(1024, "p (c s t) -> p c s t", c=8, s=nsub)
            ct0 = B.take(1024, "p (c s t) -> p c s t", c=8, s=nsub)
            ct2 = B.take(1024, "p (c s t) -> p c s t", c=8, s=nsub)
            xbcs = [B.take(1024, "p (c t) -> p c t", c=8) for _ in range(2)]
            ltall = B.take(1024, "p (h t) -> p h t", h=8)
            ehall = B.take(1024, "p (h t) -> p h t", h=8)
            mhall = B.take(1024, "p (h t) -> p h t", h=8)
            dtrs = [A.take(128, parts=16) for _ in range(2)]
            dtfs = [A.take(128, parts=16) for _ in range(2)]
            xs_tms = [A.take(512) for _ in range(2)]
            bdts = [A.take(272) for _ in range(2)]
            a_tms = [A.take(16) for _ in range(2)]
            xdt = A.take(512); xd = A.take(512)
            gm = A.take(256, "p (g t) -> p g t", g=2)
            ecs = A.take(16); H = A.take(512, "p (g t) -> p g t", g=2); ysb = A.take(512); tmp = A.take(512)
            yfls = [A.take(512) for _ in range(3)]
            zts = [A.take(512, "p (c t) -> p c t", c=4) for _ in range(3)]
            zsg = A.take(512, "p (c t) -> p c t", c=4)
            yfm = A.take(512, "p (c t) -> p c t", c=4)
            rstd = A.take(128); sq2 = [A.take(128) for _ in range(2)]
            y16 = C.take(512, "p (c t) -> p c t", c=4)
            scwo = PP["scw"][0] + l * 24
            scbo = PP["scb"][0] + l * 8
            sdo = PP["sd"][0] + l * 8

            def wb(j):
                return ppt.t[:, scwo + j * 8:scwo + j * 8 + 8].unsqueeze(2).unsqueeze(3).broadcast_to([128, 8, nsub, SL])

            def silu_to(out_ap, x_ap, sg, xv, sgv, outv):
                S.op("act", lambda e: e.activation(out=sg, in_=x_ap, func=AF.Exp, scale=-1.0), reads=[xv], writes=[sgv])
                S.op("act", lambda e: e.activation(out=sg, in_=sg, func=AF.Ln, bias=cst.t[:, 1:2], scale=1.0),
                     reads=[sgv, cst], writes=[sgv])
                S.op("act", lambda e: e.activation(out=sg, in_=sg, func=AF.Exp, scale=-1.0), reads=[sgv], writes=[sgv])
                S.op("dve", lambda e: e.tensor_tensor(out=out_ap, in0=x_ap, in1=sg, op=ALU.mult),
                     reads=[xv, sgv], writes=[outv])

            def prep_load(ci, pos, d):
                t0 = ci * 128
                k = pos % 2; k3 = pos % 3
                raw = raws[k]; dtr = dtrs[k]
                if q.ctx:
                    S.op("dve", lambda e: e.memset(raw.t[:], 0.0), writes=[raw])
                for s_ in range(nsub):
                    a0 = t0 + s_ * SL
                    S.dma("sp", raw.t[:, :, s_, 1:SL + 1], q.px.ap[10:18, :, a0:a0 + SL].rearrange("c p t -> p c t"),
                          reads=q.px.tk(a0, a0 + SL), writes=[raw])
                if q.ctx:
                    if t0 > 0:
                        S.dma("sp", raw.t[:, :, 0, 0:1], q.px.ap[10:18, :, t0 - 1:t0].rearrange("c p t -> p c t"),
                              reads=q.px.tk(t0 - 1, t0), writes=[raw])
                    if t0 + 128 < q.L:
                        S.dma("sp", raw.t[:, :, nsub - 1, SL + 1:SL + 2],
                              q.px.ap[10:18, :, t0 + 128:t0 + 129].rearrange("c p t -> p c t"),
                              reads=q.px.tk(t0 + 128, t0 + 129), writes=[raw])
                S.dma("sp", dtr.t[:], q.px.ap[18, 0:16, t0:t0 + 128], reads=q.px.tk(t0, t0 + 128), writes=[dtr])
                if want_out and d == 1:
                    S.dma("sp", zts[k3].t[:], q.px.ap[6:10, :, t0:t0 + 128].rearrange("c p t -> p c t"),
                          reads=q.px.tk(t0, t0 + 128), writes=[zts[k3]])
                    S.dma("sp", yfls[k3].t[:], q.yf.ap[t0:t0 + 128, :], reads=q.yf.tk(t0, t0 + 128), writes=[yfls[k3]])

            def prep(ci, pos, d):
                k = pos % 2; k3 = pos % 3
                raw = raws[k]; xbc = xbcs[k]; dtr = dtrs[k]; dtf = dtfs[k]; xs_tm = xs_tms[k]; bdt = bdts[k]
                a_tm = a_tms[k]
                S.op("pool", lambda e: e.tensor_tensor(out=ct0.t[:], in0=raw.t[:, :, :, 0:SL], in1=wb(0), op=ALU.mult),
                     reads=[raw, ppt], writes=[ct0])
                S.op("pool", lambda e: e.tensor_tensor(out=ct2.t[:], in0=raw.t[:, :, :, 2:SL + 2], in1=wb(2), op=ALU.mult),
                     reads=[raw, ppt], writes=[ct2])
                S.op("dve", lambda e: e.tensor_tensor(out=acc.t[:], in0=raw.t[:, :, :, 1:SL + 1], in1=wb(1), op=ALU.mult),
                     reads=[raw, ppt], writes=[acc])
                S.op("dve", lambda e: e.tensor_tensor(out=acc.t[:], in0=acc.t[:], in1=ct0.t[:], op=ALU.add),
                     reads=[acc, ct0], writes=[acc])
                S.op("dve", lambda e: e.tensor_tensor(out=acc.t[:], in0=acc.t[:], in1=ct2.t[:], op=ALU.add),
                     reads=[acc, ct2], writes=[acc])
                S.op("dve", lambda e: e.tensor_tensor(
                    out=acc.t[:], in0=acc.t[:],
                    in1=ppt.t[:, scbo:scbo + 8].unsqueeze(2).unsqueeze(3).broadcast_to([128, 8, nsub, SL]), op=ALU.add),
                    reads=[acc, ppt], writes=[acc])
                yield
                silu_to(xbc.t[:].rearrange("p c (s t) -> p c s t", s=nsub), acc.t[:], ct0.t[:], acc, ct0, xbc)
                S.op("act", lambda e: e.activation(out=dtr.t[:], in_=dtr.t[:], func=AF.Exp, bias=dtb.t[:, l:l + 1],
                                                   scale=1.0), reads=[dtr, dtb], writes=[dtr])
                S.op("act", lambda e: e.activation(out=dtf.t[:], in_=dtr.t[:], func=AF.Ln, bias=cst.t[0:16, 1:2],
                                                   scale=1.0), reads=[dtr, cst], writes=[dtf])
                yield
                for kc in range(4):
                    S.op("pe", lambda e, kc=kc: e.transpose(P[0].t[:, kc * 128:(kc + 1) * 128], xbc.t[:, kc, :], ident),
                         reads=[xbc, masks], writes=[P[0]])
                S.op("act", lambda e: e.activation(out=xs_tm.t[:], in_=P[0].t[:, 0:512], func=AF.Copy),
                     reads=[P[0]], writes=[xs_tm])
                for g in range(2):
                    S.op("pe", lambda e, g=g: e.transpose(P[1].t[:, g * 128:(g + 1) * 128], xbc.t[:, 4 + g, :], ident),
                         reads=[xbc, masks], writes=[P[1]])
                S.op("pe", lambda e: e.transpose(P[1].t[:, 256:272], dtf.t[:], masks.t[0:16, 4, 0:16]),
                     reads=[dtf, masks], writes=[P[1]])
                S.op("act", lambda e: e.activation(out=bdt.t[:], in_=P[1].t[:, 0:272], func=AF.Copy),
                     reads=[P[1]], writes=[bdt])
                S.op("dve", lambda e: e.tensor_tensor(out=a_tm.t[:], in0=bdt.t[:, 256:272], in1=Aneg.t[:], op=ALU.mult),
                     reads=[bdt, Aneg], writes=[a_tm])
                if want_out and d == 1:
                    silu_to(zts[k3].t[:], zts[k3].t[:], zsg.t[:], zts[k3], zsg, zts[k3])

            def main(ci, pos, d):
                t0 = ci * 128
                k = pos % 2; k3 = pos % 3
                xbc = xbcs[k]; xs_tm = xs_tms[k]; bdt = bdts[k]; a_tm = a_tms[k]
                mA = masks.t[:, 0 if d == 0 else 2, :]
                mB = masks.t[:, 1 if d == 0 else 3, :]
                col = 127 if d == 0 else 0
                a = a_tm.t[:, 8 * d:8 * d + 8]
                S.op("dve", lambda e: e.tensor_tensor(
                    out=xdt.t[:].rearrange("p (h c) -> p h c", h=8), in0=xs_tm.t[:].rearrange("p (h c) -> p h c", h=8),
                    in1=bdt.t[:, 256 + 8 * d:264 + 8 * d].unsqueeze(2).broadcast_to([128, 8, 64]), op=ALU.mult),
                    reads=[xs_tm, bdt], writes=[xdt])
                if want_out:
                    for g in range(2):
                        S.op("pe", lambda e, g=g: e.matmul(P[2].t[:, g * 128:(g + 1) * 128], xbc.t[:, 4 + g, :],
                                                           xbc.t[:, 6 + g, :], start=True, stop=True),
                             reads=[xbc], writes=[P[2]])
                    S.op("dve", lambda e: e.tensor_tensor(
                        out=gm.t[:], in0=P[2].t[:, 0:256].rearrange("p (g t) -> p g t", g=2),
                        in1=mB.unsqueeze(1).broadcast_to([128, 2, 128]), op=ALU.mult), reads=[P[2], masks], writes=[gm])
                S.op("pe", lambda e: e.matmul(P[3].t[:, 0:8], mB, a, start=True, stop=True),
                     reads=[masks, a_tm], writes=[P[3]])
                S.op("pe", lambda e: e.matmul(P[3].t[:, 8:16], ones.t[:], a, start=True, stop=True),
                     reads=[ones, a_tm], writes=[P[3]])
                S.op("act", lambda e: e.activation(out=ecs.t[:], in_=P[3].t[:, 0:16], func=AF.Exp),
                     reads=[P[3]], writes=[ecs])
                S.op("dve", lambda e: e.tensor_tensor(
                    out=ltall.t[:], in0=mA.unsqueeze(1).broadcast_to([128, 8, 128]),
                    in1=a.unsqueeze(2).broadcast_to([128, 8, 128]), op=ALU.mult), reads=[masks, a_tm], writes=[ltall])
                for h in range(8):
                    pseg = P[4 + h // 4]
                    S.op("pe", lambda e, h=h, pseg=pseg: e.matmul(pseg.t[:, (h % 4) * 128:(h % 4 + 1) * 128], ltall.t[:, h, :],
                                                                  mB, start=True, stop=True),
                         reads=[ltall, masks], writes=[pseg])
                yield
                for hh in range(2):
                    S.op("act", lambda e, hh=hh: e.activation(
                        out=ehall.t[:, 4 * hh:4 * hh + 4, :], in_=P[4 + hh].t[:, 0:512].rearrange("p (h t) -> p h t", h=4),
                        func=AF.Exp), reads=[P[4 + hh]], writes=[ehall])
                if want_out:
                    for g in range(2):
                        S.op("dve", lambda e, g=g: e.tensor_tensor(
                            out=mhall.t[:, 4 * g:4 * g + 4, :], in0=ehall.t[:, 4 * g:4 * g + 4, :],
                            in1=gm.t[:, g, :].unsqueeze(1).broadcast_to([128, 4, 128]), op=ALU.mult),
                            reads=[ehall, gm], writes=[mhall])
                    for h in range(8):
                        S.op("pe", lambda e, h=h: e.matmul(P[6].t[:, h * 64:(h + 1) * 64], mhall.t[:, h, :],
                                                           xdt.t[:, h * 64:(h + 1) * 64], start=True, stop=True),
                             reads=[mhall, xdt], writes=[P[6]])
                S.op("dve", lambda e: e.tensor_tensor(
                    out=xd.t[:].rearrange("p (h c) -> p h c", h=8), in0=xdt.t[:].rearrange("p (h c) -> p h c", h=8),
                    in1=ehall.t[:, :, col:col + 1].broadcast_to([128, 8, 64]), op=ALU.mult),
                    reads=[ehall, xdt], writes=[xd])
                yield
                if want_out:
                    for g in range(2):
                        S.op("pe", lambda e, g=g: e.matmul(P[7].t[:, g * 256:(g + 1) * 256], xbc.t[:, 6 + g, :], H.t[:, g, :],
                                                           start=True, stop=True), reads=[xbc, H], writes=[P[7]])
                    S.op("dve", lambda e: e.tensor_tensor(
                        out=tmp.t[:].rearrange("p (h c) -> p h c", h=8), in0=P[7].t[:, 0:512].rearrange("p (h c) -> p h c", h=8),
                        in1=ecs.t[:, 0:8].unsqueeze(2).broadcast_to([128, 8, 64]), op=ALU.mult),
                        reads=[P[7], ecs], writes=[tmp])
                    S.op("dve", lambda e: e.tensor_tensor(out=ysb.t[:], in0=tmp.t[:], in1=P[6].t[:, 0:512], op=ALU.add),
                         reads=[tmp, P[6]], writes=[ysb])
                    if d == 0:
                        S.op("dve", lambda e: e.tensor_tensor(
                            out=tmp.t[:].rearrange("p (h c) -> p h c", h=8), in0=xs_tm.t[:].rearrange("p (h c) -> p h c", h=8),
                            in1=ppt.t[:, sdo:sdo + 8].unsqueeze(2).broadcast_to([128, 8, 64]), op=ALU.mult),
                            reads=[xs_tm, ppt], writes=[tmp])
                        S.op("dve", lambda e: e.tensor_tensor(out=ysb.t[:], in0=ysb.t[:], in1=tmp.t[:], op=ALU.add),
                             reads=[ysb, tmp], writes=[ysb])
                        S.dma("pool", q.yf.ap[t0:t0 + 128, :], ysb.t[:], reads=[ysb], writes=q.yf.tk(t0, t0 + 128))
                    else:
                        S.op("dve", lambda e: e.tensor_tensor(out=ysb.t[:], in0=ysb.t[:], in1=yfls[k3].t[:], op=ALU.add),
                             reads=[ysb, yfls[k3]], writes=[ysb])
                for g in range(2):
                    S.op("pe", lambda e, g=g: e.matmul(P[7].t[:, g * 256:(g + 1) * 256], bdt.t[:, g * 128:(g + 1) * 128],
                                                       xd.t[:, g * 256:(g + 1) * 256], start=True, stop=True),
                         reads=[bdt, xd], writes=[P[7]])
                S.op("dve", lambda e: e.tensor_tensor(
                    out=H.t[:].rearrange("p g (h c) -> p (g h) c", h=4), in0=H.t[:].rearrange("p g (h c) -> p (g h) c", h=4),
                    in1=ecs.t[:, 8:16].unsqueeze(2).broadcast_to([128, 8, 64]), op=ALU.mult),
                    reads=[H, ecs], writes=[H])
                S.op("dve", lambda e: e.tensor_tensor(out=H.t[:].rearrange("p g t -> p (g t)"),
                                                      in0=H.t[:].rearrange("p g t -> p (g t)"),
                                                      in1=P[7].t[:, 0:512], op=ALU.add), reads=[H, P[7]], writes=[H])
                yield
                if want_out and d == 1:
                    for kc in range(4):
                        S.op("pe", lambda e, kc=kc: e.transpose(P[6].t[:, kc * 128:(kc + 1) * 128],
                                                                ysb.t[:, kc * 128:(kc + 1) * 128], ident),
                             reads=[ysb, masks], writes=[P[6]])
                    S.op("dve", lambda e: e.tensor_tensor(out=yfm.t[:], in0=P[6].t[:, 0:512].rearrange("p (c t) -> p c t", c=4),
                                                          in1=zts[k3].t[:], op=ALU.mult), reads=[P[6], zts[k3]], writes=[yfm])
                    rms([yfm.t[:, kc, :] for kc in range(4)], 128, 512, rstd, sq2, P[2], [yfm], lnexp=True)
                    S.op("dve", lambda e: e.tensor_tensor(
                        out=yfm.t[:], in0=yfm.t[:], in1=rstd.t[:, 0:128].unsqueeze(1).broadcast_to([128, 4, 128]),
                        op=ALU.mult), reads=[yfm, rstd], writes=[yfm])
                    mgo = PP["mg"][0] + l * 8 + 2
                    S.op("dve", lambda e: e.tensor_tensor(
                        out=y16.t[:], in0=yfm.t[:], in1=ppt.t[:, mgo:mgo + 4].unsqueeze(2).broadcast_to([128, 4, 128]),
                        op=ALU.mult), reads=[yfm, ppt], writes=[y16])
                    S.dma("pool", q.ym.ap[2:6, :, t0:t0 + 128].rearrange("c p t -> p c t"), y16.t[:],
                          reads=[y16], writes=q.ym.tk(t0, t0 + 128))

            for d in range(2):
                if q.ctx:
                    S.op("dve", lambda e: e.memset(H.t[:], 0.0), writes=[H])
                else:
                    S.dma("sp", H.t[:].rearrange("p g t -> p (g t)"), HSTD.ap[b, d], reads=HSTD.tk(), writes=[H])
                order = list(range(NT)) if d == 0 else list(range(NT - 1, -1, -1))
                if d == 0 and not q.ctx:
                    for k in range(2):
                        S.op("dve", lambda e, k=k: e.memset(raws[k].t[:], 0.0), writes=[raws[k]])
                prep_load(order[0], 0, d)
                if len(order) > 1:
                    prep_load(order[1], 1, d)
                for _ in prep(order[0], 0, d):
                    pass
                for idx, ci in enumerate(order):
                    if idx + 2 < len(order):
                        prep_load(order[idx + 2], idx + 2, d)
                    gm_ = main(ci, idx, d)
                    gp_ = prep(order[idx + 1], idx + 1, d) if idx + 1 < len(order) else iter(())
                    dm = dp = False
                    while not (dm and dp):
                        if not dm:
                            try:
                                next(gm_)
                            except StopIteration:
                                dm = True
                        if not dp:
                            try:
                                next(gp_)
                            except StopIteration:
                                dp = True
                if q.ctx:
                    S.dma("pool", HSTD.ap[b, d], H.t[:].rearrange("p g t -> p (g t)"), reads=[H], writes=HSTD.tk())

        def mixer(q, l, last):
            proj_phase(q, l)
            if dbg == "proj":
                return
            want = not (q.ctx and last)
            ssd_phase(q, l, want)
            if not want:
                return
            sc_phase(q, l)
            hyprep_phase(q, l)
            conv_phase(q, l, 0)
            conv_phase(q, l, 1)

        convert_weights()
        for l in range(depth):
            last = (l == depth - 1)
            mod_phase(l)
            S.dma("sp", ppb.t[:], din["ppb"][l], writes=[ppb])
            for q in seqs:
                ffn_phase(q, l, 0)
            if dbg == "ffn1":
                break
            filter_phase(l, "b"); kspec_phase(l, "b")
            if not last:
                filter_phase(l, "s"); kspec_phase(l, "s")
            for q in seqs:
                if q.ctx:
                    mixer(q, l, last)
            for q in seqs:
                if not q.ctx:
                    mixer(q, l, last)
            if dbg in ("mix", "proj"):
                break
            for q in seqs:
                if q.ctx and last:
                    continue
                outproj_phase(q, l)
                if dbg == "outp":
                    continue
                ffn_phase(q, l, 1)
        S.barrier()
        B.o = 0
        ot = [B.take(4096, "p (c t) -> p c t", c=8) for i in range(2)]
        i = 0
        for q in seqs:
            if q.ctx:
                continue
            for tt in range(q.L // 512):
                o = ot[i % 2]; i += 1
                srcd = q.xs
                S.dma("sp", o.t[:], srcd.ap[:, :, tt * 512:(tt + 1) * 512].rearrange("c p t -> p c t"),
                      reads=srcd.tk(tt * 512, tt * 512 + 512), writes=[o])
                S.dma("pool", out[q.bl, :, :, tt * 512:(tt + 1) * 512].rearrange("c p t -> p c t"), o.t[:], reads=[o])
        S.barrier()
        print("instructions:", S.nins)
    return nc


def _prep(inputs):
    sh = _host_shared(inputs)
    x = np.asarray(inputs["x"], np.float32)
    ctx = np.asarray(inputs["ctx"], np.float32)
    c = np.asarray(inputs["c"], np.float32)
    cc = np.asarray(inputs["c_ctx"], np.float32)
    maps = []
    for core in range(NCORE):
        m = dict(sh)
        b0 = core * BPC
        m["xl"] = np.ascontiguousarray(x[b0:b0 + BPC].transpose(0, 2, 1)).reshape(BPC, 8, 128, SEQ)
        m["xc"] = np.ascontiguousarray(ctx[b0:b0 + BPC].transpose(0, 2, 1)).reshape(BPC, 8, 128, CTX)
        cT = np.stack([c[b0], c[b0 + 1], cc], axis=-1)
        m["cT"] = np.ascontiguousarray(cT.reshape(8, 128, 3).transpose(1, 0, 2))
        maps.append(m)
    return maps


def kernel(**inputs):
    maps = _prep(inputs)
    shapes = {k: v.shape for k, v in maps[0].items()}
    nc = build(DEPTH, shapes)
    res = run_bass_kernel_spmd(nc, maps, core_ids=list(range(NCORE)))
    outs = []
    for core in range(NCORE):
        o = res.results[core]["out"].reshape(BPC, D, SEQ)
        outs.append(o.transpose(0, 2, 1))
    return np.ascontiguousarray(np.concatenate(outs, axis=0)).astype(np.float32)
```

```python
import math
import contextlib
import numpy as np
import ml_dtypes
import concourse.bass as bass
import concourse.mybir as mybir
from concourse.bass_utils import run_bass_kernel_spmd

F32 = mybir.dt.float32
BF16 = mybir.dt.bfloat16
AF = mybir.ActivationFunctionType
ALU = mybir.AluOpType

D = 1024; DEPTH = 4; BATCH = 16; SEQ = 4096; CTX = 256; DFF = 2816; NMOD = 9
NCORE = 8; BPC = BATCH // NCORE
EPS = 1e-6
HYW = 256; SSDW = 512; NH = 8; HD = 64; NG = 2; NS = 128
NPXC = 25
NDS = 8
TWO_PI = 2.0 * math.pi
MAGIC = 12582912.0


class Tok:
    __slots__ = ("w", "r")

    def __init__(self):
        self.w = None
        self.r = {}


class V:
    def __init__(self, t, tok=None):
        self.t = t
        self.tok = tok or Tok()

    def __getitem__(self, k):
        return self.t[k]


class Sched:
    def __init__(self, nc, st):
        self.nc = nc
        self.st = st
        self.E = {"pe": nc.tensor, "act": nc.scalar, "dve": nc.vector, "pool": nc.gpsimd, "sp": nc.sync}
        self.sem = {}
        self.cnt = {}
        for k in ["pe", "act", "dve", "pool"]:
            self.sem["c_" + k] = st.enter_context(nc.semaphore("c_" + k))
            self.cnt["c_" + k] = 0
        self.drr = {"sp": 0, "pool": 0}
        for q in ["sp", "pool"]:
            for i in range(NDS):
                key = "d_%s%d" % (q, i)
                self.sem[key] = st.enter_context(nc.semaphore(key))
                self.cnt[key] = 0
        self.seen = {e: {} for e in self.E}
        self.nins = 0

    def _wait(self, eng, key, val):
        if eng == "pe" and key == "c_pe":
            return
        if self.seen[eng].get(key, 0) >= val:
            return
        self.E[eng].wait_ge(self.sem[key], val)
        self.seen[eng][key] = val

    def _deps(self, eng, reads, writes):
        for t in reads:
            if t.w:
                self._wait(eng, *t.w)
        for t in writes:
            if t.w:
                self._wait(eng, *t.w)
            for k, v in t.r.items():
                self._wait(eng, k, v)

    def _mark(self, me, reads, writes):
        for t in reads:
            t.r[me[0]] = me[1]
        for t in writes:
            t.w = me
            t.r = {}

    def op(self, eng, fn, reads=(), writes=()):
        reads = [x.tok if isinstance(x, V) else x for x in reads]
        writes = [x.tok if isinstance(x, V) else x for x in writes]
        self._deps(eng, reads, writes)
        ins = fn(self.E[eng])
        key = "c_" + eng
        self.cnt[key] += 1
        ins.then_inc(self.sem[key], 1)
        self._mark((key, self.cnt[key]), reads, writes)
        self.nins += 1

    def dma(self, q, out, in_, reads=(), writes=()):
        reads = [x.tok if isinstance(x, V) else x for x in reads]
        writes = [x.tok if isinstance(x, V) else x for x in writes]
        i = self.drr[q]
        self.drr[q] = (i + 1) % NDS
        key = "d_%s%d" % (q, i)
        if self.cnt[key] > 0:
            self._wait(q, key, self.cnt[key])
        self._deps(q, reads, writes)
        ins = self.E[q].dma_start(out=out, in_=in_)
        self.cnt[key] += 16
        ins.then_inc(self.sem[key], 16)
        self._mark((key, self.cnt[key]), reads, writes)
        self.nins += 1

    def barrier(self):
        for e in self.E:
            for k, v in self.cnt.items():
                if v > 0:
                    self._wait(e, k, v)


class DT:
    def __init__(self, ap, n):
        self.ap = ap
        self.toks = [Tok() for _ in range(max(1, n))]

    def tk(self, t0=None, t1=None):
        if t0 is None:
            return self.toks
        return self.toks[t0 // 128:(t1 + 127) // 128]


def _consts():
    f32 = np.float32
    c = {}
    r = np.arange(128)
    m = np.zeros((128, 5, 128), f32)
    m[:, 0] = (r[:, None] > r[None, :])
    m[:, 1] = (r[:, None] <= r[None, :])
    m[:, 2] = (r[:, None] < r[None, :])
    m[:, 3] = (r[:, None] >= r[None, :])
    m[:, 4] = np.eye(128)
    c["masks"] = m
    for nm, L in (("b", SEQ), ("s", CTX)):
        N = 2 * L
        nfc = (L + 1 + 127) // 128
        P = nfc * 128
        u = np.arange(P, dtype=np.int64)
        ph = (u[:, None] * u[None, :]) % N
        ang = ph.astype(np.float64) * (2.0 * math.pi / N)
        valid = ((u[:, None] <= L) & (u[None, :] <= L))
        def lay(g):
            g = g.astype(f32).astype(ml_dtypes.bfloat16).reshape(nfc, 128, nfc, 128)
            return np.ascontiguousarray(g.transpose(2, 1, 0, 3))
        c["gc_" + nm] = lay(np.where(valid, np.cos(ang), 0.0))
        c["gs_" + nm] = lay(np.where(valid, np.sin(ang), 0.0))
        wf = np.zeros(P, np.float64)
        wf[:L + 1] = 2.0 / N
        wf[0] = 1.0 / N
        wf[L] = 1.0 / N
        c["wf_" + nm] = np.ascontiguousarray(wf.reshape(nfc, 128).T).astype(f32)
        t = np.linspace(0.0, 1.0, L, dtype=f32)[:, None]
        bands = 16
        w = (2.0 * math.pi * np.arange(L, dtype=f32)[:, None] / L).astype(f32)
        f = np.linspace(1e-4, bands - 1, bands, dtype=f32)[None, :]
        z = np.concatenate([t, np.cos(f * w), -np.sin(f * w)], axis=-1).astype(f32)
        c["zT_" + nm] = np.ascontiguousarray(z.T)
        deltas = np.linspace(math.log(1e-2) / 1.5, math.log(1e-2) / 0.3, HYW, dtype=f32)
        c["dec_" + nm] = np.exp(-t * np.abs(deltas)).astype(f32)
    return c


def _fm(v):
    v = np.asarray(v, np.float32)
    n = v.shape[-1] // 128
    v = v.reshape(v.shape[:-1] + (n, 128))
    return np.ascontiguousarray(np.moveaxis(v, -1, 0))


def _wl(w, kc):
    K, M = w.shape
    return np.ascontiguousarray(w.reshape(kc, 128, M // 128, 128).transpose(2, 1, 0, 3))


PP = {}


def _pp_layout():
    off = 0
    for nm, n in (("ng", DEPTH * 6 * 8), ("bm", DEPTH * 72), ("hcw", DEPTH * 3 * 6), ("hcb", DEPTH * 6),
                  ("scw", DEPTH * 3 * 8), ("scb", DEPTH * 8), ("ccw", DEPTH * 3 * 2), ("mg", DEPTH * 8),
                  ("alog", DEPTH * 16), ("sd", DEPTH * 8),
                  ("wfb", 33), ("wfs", 3)):
        PP[nm] = (off, n)
        off += n
    return off


NPP = _pp_layout()


def _host_shared(inp):
    f32 = np.float32
    sh = {}
    sh.update(_consts())
    pp = np.zeros((128, NPP), f32)

    def put(nm, a):
        o, n = PP[nm]
        pp[:, o:o + n] = np.asarray(a, f32).reshape(128, n)
    put("ng", _fm(inp["norm_g"]))
    put("bm", _fm(inp["b_mod"]))
    put("hcw", _fm(inp["hy_conv_w"]))
    put("hcb", _fm(inp["hy_conv_b"]))
    put("scw", _fm(inp["ssd_conv_w"]))
    put("scb", _fm(inp["ssd_conv_b"]))
    put("ccw", _fm(inp["sc_conv_w"]))
    put("mg", _fm(inp["mix_gain"]))
    sh["ppb"] = np.ascontiguousarray(np.concatenate([
        np.broadcast_to(np.asarray(inp["mix_gain"], f32)[:, None, :HYW], (DEPTH, 128, HYW)),
        np.broadcast_to(np.asarray(inp["hy_bias"], f32).reshape(DEPTH, 1, 512), (DEPTH, 128, 512))], axis=2))
    put("alog", np.broadcast_to(np.asarray(inp["ssd_a_log"]).reshape(1, DEPTH, 16), (128, DEPTH, 16)))
    put("sd", np.broadcast_to(np.asarray(inp["ssd_d"]).reshape(1, DEPTH, 8), (128, DEPTH, 8)))
    put("wfb", sh.pop("wf_b"))
    put("wfs", sh.pop("wf_s"))
    sh["pp"] = pp
    sh["wm"] = np.stack([_wl(np.asarray(inp["w_mod"][l], f32), 8) for l in range(DEPTH)])
    sh["fwi"] = np.stack([np.stack([_wl(np.asarray(inp["ffn_w_in"][l, j], f32), 8) for j in range(2)])
                          for l in range(DEPTH)])
    sh["fwo"] = np.stack([np.stack([_wl(np.asarray(inp["ffn_w_out"][l, j], f32), 22) for j in range(2)])
                          for l in range(DEPTH)])
    wi = np.asarray(inp["w_in"], f32)
    wip = np.zeros((DEPTH, D, NPXC * 128), f32)
    wip[:, :, 0:2304] = wi[:, :, 0:2304]
    wip[:, :, 2304:2320] = wi[:, :, 2304:2320]
    wip[:, :, 2432:3200] = wi[:, :, 2320:3088]
    sh["wi"] = np.stack([_wl(wip[l], 8) for l in range(DEPTH)])
    sh["wo"] = np.stack([_wl(np.asarray(inp["w_out"][l], f32), 8) for l in range(DEPTH)])
    sh["fw1"] = np.ascontiguousarray(np.asarray(inp["hy_fw1"], f32).transpose(1, 0, 2))
    sh["fw23"] = np.ascontiguousarray(np.stack([np.asarray(inp["hy_fw2"], f32), np.asarray(inp["hy_fw3"], f32)],
                                               axis=1).transpose(2, 0, 1, 3))
    sh["fw4"] = np.ascontiguousarray(np.asarray(inp["hy_fw4"], f32))
    sh["fpp"] = np.ascontiguousarray(np.stack([np.asarray(inp["hy_fb1"], f32), np.asarray(inp["hy_fb2"], f32),
                                               np.asarray(inp["hy_fb3"], f32), np.asarray(inp["hy_freq"], f32)],
                                              axis=1).transpose(2, 0, 1))
    sh["dtb"] = np.ascontiguousarray(np.asarray(inp["ssd_dt_bias"], f32).reshape(DEPTH, 16).T)
    return sh


SHARED_SHAPES = None


def build(depth, shapes, dbg=None):
    nc = bass.Bass("TRN2", target_bir_lowering=False)
    din = {k: nc.dram_tensor(k, list(s), BF16 if k[:3] in ("gc_", "gs_") else F32, kind="ExternalInput").ap() for k, s in shapes.items()}
    out = nc.dram_tensor("out", [BPC, 8, 128, SEQ], F32, kind="ExternalOutput").ap()

    def scratch(nm, shape, nt, dt=F32):
        return DT(nc.dram_tensor(nm, list(shape), dt, kind="Internal").ap(), nt)

    with contextlib.ExitStack() as st:
        S = Sched(nc, st)
        st.enter_context(nc.allow_non_contiguous_dma(reason="small strided halo / layout DMAs"))

        def sb(nm, shape, dt=F32):
            return st.enter_context(nc.sbuf_tensor("sb_" + nm, list(shape), dt))

        class Ar:
            def __init__(self, t, size):
                self.t = t; self.size = size; self.o = 0

            def take(self, n, pat=None, parts=None, **kw):
                assert self.o + n <= self.size, (self.o, n, self.size)
                v = self.t[:, self.o:self.o + n] if parts is None else self.t[0:parts, self.o:self.o + n]
                self.o += n
                return V(v.rearrange(pat, **kw) if pat else v)

        def phase_start():
            S.barrier()
            A.o = 0; B.o = 0; C.o = 0

        def ps(nm):
            return V(st.enter_context(nc.psum_tensor(nm, [128, 512], F32)))
        P = [ps("ps%d" % i) for i in range(8)]
        AR_A = sb("arA", [128, 10752])
        AR_B = sb("arB", [128, 16384])
        AR_C = sb("arC", [128, 34048], BF16)
        A = Ar(AR_A, 10752); B = Ar(AR_B, 16384); C = Ar(AR_C, 34048)
        NW = 8
        WPOOL = [V(sb("w%d" % i, [128, 8, 128], BF16)) for i in range(NW)]
        wrr = [0]

        def wnext():
            wrr[0] = (wrr[0] + 1) % NW
            return WPOOL[wrr[0]]
        masks = V(sb("masks", [128, 5, 128]))
        ppt = V(sb("pp", [128, NPP]))
        ones = V(sb("ones", [128, 128]))
        cst = V(sb("cst", [128, 4]))
        sct = V(sb("sct", [128, 8, 3]))
        MOD = V(sb("mod", [128, 9, 8, 3]))
        GIN = V(sb("gin", [128, 3, 8, 3]))
        COUT = V(sb("cout", [128, 3, 8, 3]))
        Aneg = V(sb("aneg", [128, 16]))
        fw1 = V(sb("fw1", [33, 4, 64])); fw23 = V(sb("fw23", [64, 4, 2, 64])); ppb = V(sb("ppb", [128, 768]))
        fpp = V(sb("fpp", [64, 4, 4])); dtb = V(sb("dtb", [16, 4]))
        HSTD = scratch("hstd", [2, 2, 128, 512], 1)
        small = V(sb("small", [128, 64]))

        def pp(nm, *idx_shape):
            o, n = PP[nm]
            return ppt.t[:, o:o + n]

        def viewA(o, n):
            return V(AR_A[:, o:o + n])

        def viewB(o, n):
            return V(AR_B[:, o:o + n])

        S.dma("sp", masks.t[:], din["masks"], writes=[masks])
        S.dma("sp", ppt.t[:], din["pp"], writes=[ppt])
        S.dma("sp", sct.t[:], din["cT"], writes=[sct])
        S.dma("sp", fw1.t[:], din["fw1"], writes=[fw1])
        S.dma("sp", fw23.t[:], din["fw23"], writes=[fw23])
        S.dma("sp", fpp.t[:], din["fpp"], writes=[fpp])
        S.dma("sp", dtb.t[:], din["dtb"], writes=[dtb])
        S.op("dve", lambda e: e.memset(ones.t[:], 1.0), writes=[ones])
        S.op("dve", lambda e: e.memset(cst.t[:, 0:1], EPS), writes=[cst])
        S.op("dve", lambda e: e.memset(cst.t[:, 1:2], 1.0), writes=[cst])
        S.op("dve", lambda e: e.memset(cst.t[:, 2:3], 0.0), writes=[cst])
        S.op("act", lambda e: e.activation(out=sct.t[:], in_=sct.t[:], func=AF.Silu), reads=[sct], writes=[sct])
        ident = masks.t[:, 4, :]

        class Seq:
            pass
        seqs = []
        for s in range(2 * BPC):
            q = Seq()
            q.ctx = s >= BPC
            q.bl = s % BPC
            q.L = CTX if q.ctx else SEQ
            q.T = min(512, q.L)
            q.who = 2 if q.ctx else q.bl
            q.nt = q.L // 128
            q.nm = "s" if q.ctx else "b"
            src = din["xc"] if q.ctx else din["xl"]
            q.xin = DT(src[q.bl], q.nt)
            q.xs = scratch("xs%d" % s, [8, 128, q.L], q.nt)
            q.px = scratch("px%d" % s, [NPXC, 128, q.L], q.nt)
            q.ym = scratch("ym%d" % s, [8, 128, q.L], q.nt, BF16)
            q.yf = scratch("yf%d" % s, [q.L, 512], q.nt)
            q.hv = scratch("hv%d" % s, [q.L, 768], q.nt)
            q.z1 = scratch("z1%d" % s, [q.L, 256], q.nt)
            q.nfc = (q.L + 1 + 127) // 128
            q.first = True
            seqs.append(q)
        KS = {}
        EO = {}
        for nm, L in (("b", SEQ), ("s", CTX)):
            nfc = (L + 1 + 127) // 128
            KS[nm] = scratch("ks_" + nm, [nfc, 128, 2, 2, 256], 1)
            EO[nm] = scratch("eo_" + nm, [2, L, 512], 1, BF16)
        GT_ = {nm: (DT(din["gc_" + nm], 1), DT(din["gs_" + nm], 1)) for nm in "bs"}

        def xsrc(q):
            return q.xin if q.first else q.xs

        def rms(src_aps, T, nD, rstd, sq2, pn, reads, lnexp=False):
            n = len(src_aps)
            for i, a in enumerate(src_aps):
                sq = sq2[i % 2]
                S.op("act", lambda e, a=a, sq=sq: e.activation(out=sq.t[:, :T], in_=a, func=AF.Square),
                     reads=reads, writes=[sq])
                S.op("pe", lambda e, sq=sq, i=i: e.matmul(pn.t[:, :T], ones.t[:], sq.t[:, :T], start=(i == 0),
                                                           stop=(i == n - 1)), reads=[sq, ones], writes=[pn])
            if lnexp:
                S.op("act", lambda e: e.activation(out=rstd.t[:, :T], in_=pn.t[:, :T], func=AF.Ln,
                                                   bias=cst.t[:, 0:1], scale=1.0 / nD), reads=[pn, cst], writes=[rstd])
                S.op("act", lambda e: e.activation(out=rstd.t[:, :T], in_=rstd.t[:, :T], func=AF.Exp, scale=-0.5),
                     reads=[rstd], writes=[rstd])
                return
            S.op("act", lambda e: e.activation(out=rstd.t[:, :T], in_=pn.t[:, :T], func=AF.Sqrt,
                                               bias=cst.t[:, 0:1], scale=1.0 / nD), reads=[pn, cst], writes=[rstd])
            S.op("dve", lambda e: e.reciprocal(out=rstd.t[:, :T], in_=rstd.t[:, :T]), reads=[rstd], writes=[rstd])

        def load_norm(q, sub, t0, T, xt, ht, rstd, sq2, pn, tmp2):
            xsr = xsrc(q)
            S.dma("sp", xt.t[:, :, :T], xsr.ap[:, :, t0:t0 + T].rearrange("c p t -> p c t"),
                  reads=xsr.tk(t0, t0 + T), writes=[xt])
            rms([xt.t[:, kc, :T] for kc in range(8)], T, D, rstd, sq2, pn, [xt])
            for kc in range(8):
                tp = tmp2[kc % 2]
                S.op("dve", lambda e, kc=kc, tp=tp: e.scalar_tensor_tensor(
                    out=tp.t[:, :T], in0=xt.t[:, kc, :T], scalar=GIN.t[:, sub, kc, q.who:q.who + 1],
                    in1=rstd.t[:, :T], op0=ALU.mult, op1=ALU.mult), reads=[xt, GIN, rstd], writes=[tp])
                S.op("act", lambda e, kc=kc, tp=tp: e.activation(
                    out=ht.t[:, kc, :T], in_=tp.t[:, :T], func=AF.Identity,
                    bias=MOD.t[:, 3 * sub, kc, q.who:q.who + 1], scale=1.0), reads=[tp, MOD], writes=[ht])

        def resid_store(q, sub, t0, T, xt, yt, rstd, sq2, pn, tmp, dst=None):
            rms([yt.t[:, kc, :T] for kc in range(8)], T, D, rstd, sq2, pn, [yt])
            for kc in range(8):
                S.op("dve", lambda e, kc=kc: e.scalar_tensor_tensor(
                    out=yt.t[:, kc, :T], in0=yt.t[:, kc, :T], scalar=COUT.t[:, sub, kc, q.who:q.who + 1],
                    in1=rstd.t[:, :T], op0=ALU.mult, op1=ALU.mult), reads=[yt, COUT, rstd], writes=[yt])
                S.op("dve", lambda e, kc=kc: e.tensor_tensor(out=xt.t[:, kc, :T], in0=xt.t[:, kc, :T],
                                                             in1=yt.t[:, kc, :T], op=ALU.add),
                     reads=[xt, yt], writes=[xt])
            d = dst or q.xs
            S.dma("pool", d.ap[:, :, t0:t0 + T].rearrange("c p t -> p c t"), xt.t[:, :, :T],
                  reads=[xt], writes=d.tk(t0, t0 + T))

        W16 = {}

        def convert_weights():
            phase_start()
            stg = [B.take(4096) for _ in range(3)]
            out16 = [C.take(4096) for _ in range(3)]
            engs = ["act", "dve", "pool"]
            n = [0]
            for nm, grp in (("fwi", 4), ("fwo", 1), ("wi", 4), ("wo", 4)):
                src = din[nm]
                shp = list(src.shape)
                W16[nm] = DT(nc.dram_tensor(nm + "16", shp, BF16, kind="Internal").ap(), 1)
                dst = W16[nm].ap
                lead = shp[:-4]
                noc = shp[-4]
                F = shp[-2] * shp[-1]
                idxs = [()]
                for d_ in lead:
                    idxs = [i + (k,) for i in idxs for k in range(d_)]
                for ix in idxs:
                    sa = src; da = dst
                    for k in ix:
                        sa = sa[k]; da = da[k]
                    for o0 in range(0, noc, grp):
                        g = min(grp, noc - o0)
                        i = n[0] % 3; n[0] += 1
                        sv = stg[i].t[:, 0:g * F].rearrange("p (o f) -> p o f", o=g)
                        ov = out16[i].t[:, 0:g * F].rearrange("p (o f) -> p o f", o=g)
                        S.dma("sp", sv, sa[o0:o0 + g].rearrange("o p k m -> p o (k m)"), writes=[stg[i]])
                        if engs[i] == "act":
                            S.op("act", lambda e, sv=sv, ov=ov: e.activation(out=ov, in_=sv, func=AF.Copy),
                                 reads=[stg[i]], writes=[out16[i]])
                        else:
                            S.op(engs[i], lambda e, sv=sv, ov=ov: e.tensor_copy(out=ov, in_=sv),
                                 reads=[stg[i]], writes=[out16[i]])
                        S.dma("pool", da[o0:o0 + g].rearrange("o p k m -> p o (k m)"), ov, reads=[out16[i]])

        def mod_phase(l):
            phase_start()
            wmd = DT(din["wm"], 1)
            WF32 = [B.take(1024, "p (k m) -> p k m", k=8) for i in range(4)]
            for oc in range(72):
                w = WF32[oc % 4]
                S.dma("sp", w.t[:], wmd.ap[l, oc], writes=[w])
                pm = P[oc % 2]
                for kc in range(8):
                    S.op("pe", lambda e, kc=kc, w=w, pm=pm: e.matmul(pm.t[:, 0:3], w.t[:, kc, :], sct.t[:, kc, :],
                                                                     start=(kc == 0), stop=(kc == 7)),
                         reads=[w, sct], writes=[pm])
                o = PP["bm"][0] + l * 72 + oc
                S.op("act", lambda e, oc=oc, pm=pm, o=o: e.activation(
                    out=MOD.t[:, oc // 8, oc % 8, :], in_=pm.t[:, 0:3], func=AF.Identity,
                    bias=ppt.t[:, o:o + 1], scale=1.0), reads=[pm, ppt], writes=[MOD])
            ngo = PP["ng"][0] + l * 48
            for i in range(3):
                gpre = ppt.t[:, ngo + (2 * i) * 8: ngo + (2 * i) * 8 + 8].unsqueeze(2).broadcast_to([128, 8, 3])
                gpost = ppt.t[:, ngo + (2 * i + 1) * 8: ngo + (2 * i + 1) * 8 + 8].unsqueeze(2).broadcast_to([128, 8, 3])
                S.op("dve", lambda e, i=i: e.tensor_scalar(out=GIN.t[:, i], in0=MOD.t[:, 3 * i + 1], scalar1=1.0,
                                                           scalar2=None, op0=ALU.add), reads=[MOD], writes=[GIN])
                S.op("dve", lambda e, i=i, g=gpre: e.tensor_tensor(out=GIN.t[:, i], in0=GIN.t[:, i], in1=g,
                                                                   op=ALU.mult), reads=[GIN, ppt], writes=[GIN])
                rw = 1.0 if i == 1 else 0.5
                S.op("dve", lambda e, i=i, g=gpost, rw=rw: e.scalar_tensor_tensor(
                    out=COUT.t[:, i], in0=MOD.t[:, 3 * i + 2], scalar=rw, in1=g, op0=ALU.mult, op1=ALU.mult),
                    reads=[MOD, ppt], writes=[COUT])
            o = PP["alog"][0] + l * 16
            S.op("act", lambda e: e.activation(out=Aneg.t[:], in_=ppt.t[:, o:o + 16], func=AF.Exp),
                 reads=[ppt], writes=[Aneg])
            S.op("dve", lambda e: e.tensor_scalar(out=Aneg.t[:], in0=Aneg.t[:], scalar1=-1.0, scalar2=None,
                                                  op0=ALU.mult), reads=[Aneg], writes=[Aneg])

        def ffn_phase(q, l, j):
            phase_start()
            sub = 2 * j
            T = q.T
            xts = [B.take(4096, "p (c t) -> p c t", c=8) for i in range(2)]
            yt = B.take(4096, "p (c t) -> p c t", c=8)
            ht = C.take(4096, "p (c t) -> p c t", c=8)
            at = C.take(11264, "p (c t) -> p c t", c=22)
            wos = [C.take(2816, "p (c m) -> p c m", c=22) for i in range(3)]
            sq2 = [A.take(512) for i in range(2)]
            rstd = A.take(512)
            sg2 = [A.take(512) for i in range(2)]
            tmp2 = [A.take(512) for i in range(2)]
            fwi = W16["fwi"]
            fwo = W16["fwo"]
            for tt in range(q.L // T):
                t0 = tt * T
                xt = xts[tt % 2]
                load_norm(q, sub, t0, T, xt, ht, rstd, sq2, P[7], tmp2)
                for jf in range(22):
                    wg = wnext(); S.dma("sp", wg.t[:], fwi.ap[l, j, jf], writes=[wg])
                    wu = wnext(); S.dma("sp", wu.t[:], fwi.ap[l, j, 22 + jf], writes=[wu])
                    pg = P[(2 * jf) % 4]; pu = P[(2 * jf) % 4 + 1]
                    for kc in range(8):
                        S.op("pe", lambda e, kc=kc, wg=wg, pg=pg: e.matmul(pg.t[:, :T], wg.t[:, kc, :], ht.t[:, kc, :T],
                                                                           start=(kc == 0), stop=(kc == 7)),
                             reads=[wg, ht], writes=[pg])
                    for kc in range(8):
                        S.op("pe", lambda e, kc=kc, wu=wu, pu=pu: e.matmul(pu.t[:, :T], wu.t[:, kc, :], ht.t[:, kc, :T],
                                                                           start=(kc == 0), stop=(kc == 7)),
                             reads=[wu, ht], writes=[pu])
                    sg = sg2[jf % 2]
                    S.op("act", lambda e, sg=sg, pg=pg: e.activation(out=sg.t[:, :T], in_=pg.t[:, :T], func=AF.Silu),
                         reads=[pg], writes=[sg])
                    S.op("dve", lambda e, sg=sg, pu=pu, jf=jf: e.tensor_tensor(out=at.t[:, jf, :T], in0=sg.t[:, :T],
                                                                               in1=pu.t[:, :T], op=ALU.mult),
                         reads=[sg, pu], writes=[at])
                for oc in range(8):
                    wo = wos[oc % 3]
                    S.dma("sp", wo.t[:], fwo.ap[l, j, oc], writes=[wo])
                    py = P[4 + oc % 2]
                    for fc in range(22):
                        S.op("pe", lambda e, fc=fc, wo=wo, py=py: e.matmul(py.t[:, :T], wo.t[:, fc, :], at.t[:, fc, :T],
                                                                           start=(fc == 0), stop=(fc == 21)),
                             reads=[wo, at], writes=[py])
                    S.op("act", lambda e, oc=oc, py=py: e.activation(out=yt.t[:, oc, :T], in_=py.t[:, :T],
                                                                     func=AF.Copy), reads=[py], writes=[yt])
                resid_store(q, sub, t0, T, xt, yt, rstd, sq2, P[7], None)
            q.first = False


        def proj_phase(q, l):
            phase_start()
            T = q.T
            xts = [B.take(4096, "p (c t) -> p c t", c=8) for i in range(2)]
            ht = C.take(4096, "p (c t) -> p c t", c=8)
            sq2 = [A.take(512) for i in range(2)]
            rstd = A.take(512)
            tmp2 = [A.take(512) for i in range(2)]
            ot4 = [A.take(512) for i in range(4)]
            wi = W16["wi"]
            for tt in range(q.L // T):
                t0 = tt * T
                xt = xts[tt % 2]
                load_norm(q, 1, t0, T, xt, ht, rstd, sq2, P[7], tmp2)
                for oc in range(NPXC):
                    w = wnext(); S.dma("sp", w.t[:], wi.ap[l, oc], writes=[w])
                    pm = P[oc % 4]; o = ot4[oc % 4]
                    for kc in range(8):
                        S.op("pe", lambda e, kc=kc, w=w, pm=pm: e.matmul(pm.t[:, :T], w.t[:, kc, :], ht.t[:, kc, :T],
                                                                         start=(kc == 0), stop=(kc == 7)),
                             reads=[w, ht], writes=[pm])
                    S.op("act", lambda e, o=o, pm=pm: e.activation(out=o.t[:, :T], in_=pm.t[:, :T], func=AF.Copy),
                         reads=[pm], writes=[o])
                    S.dma("pool", q.px.ap[oc, :, t0:t0 + T], o.t[:, :T], reads=[o], writes=q.px.tk(t0, t0 + T))

        def conv3(acc_ap, raw_ap3, SL, wo, nw, kc, accv, rawv):
            def wj(j):
                o = wo + j * nw + kc
                return ppt.t[:, o:o + 1]
            S.op("dve", lambda e: e.tensor_scalar(out=acc_ap, in0=raw_ap3[:, :, 1:SL + 1], scalar1=wj(1), scalar2=None,
                                                  op0=ALU.mult), reads=[rawv, ppt], writes=[accv])
            S.op("dve", lambda e: e.scalar_tensor_tensor(out=acc_ap, in0=raw_ap3[:, :, 0:SL], scalar=wj(0), in1=acc_ap,
                                                         op0=ALU.mult, op1=ALU.add), reads=[rawv, ppt, accv], writes=[accv])
            S.op("dve", lambda e: e.scalar_tensor_tensor(out=acc_ap, in0=raw_ap3[:, :, 2:SL + 2], scalar=wj(2), in1=acc_ap,
                                                         op0=ALU.mult, op1=ALU.add), reads=[rawv, ppt, accv], writes=[accv])

        def load_halo(q, raw, c0, nch, t0, n, SL):
            nsub = n // SL
            S.op("dve", lambda e: e.memset(raw.t[:], 0.0), writes=[raw])
            for s_ in range(nsub):
                a = t0 + s_ * SL
                S.dma("sp", raw.t[:, :, s_, 1:SL + 1], q.px.ap[c0:c0 + nch, :, a:a + SL].rearrange("c p t -> p c t"),
                      reads=q.px.tk(a, a + SL), writes=[raw])
            if q.ctx:
                if t0 > 0:
                    S.dma("sp", raw.t[:, :, 0, 0:1], q.px.ap[c0:c0 + nch, :, t0 - 1:t0].rearrange("c p t -> p c t"),
                          reads=q.px.tk(t0 - 1, t0), writes=[raw])
                if t0 + n < q.L:
                    S.dma("sp", raw.t[:, :, nsub - 1, SL + 1:SL + 2],
                          q.px.ap[c0:c0 + nch, :, t0 + n:t0 + n + 1].rearrange("c p t -> p c t"),
                          reads=q.px.tk(t0 + n, t0 + n + 1), writes=[raw])

        def sc_phase(q, l):
            phase_start()
            T = q.T
            SL = 64 if not q.ctx else T
            nsub = T // SL
            gt = B.take(6 * T, "p (c t) -> p c t", c=6)
            raw = B.take(2 * nsub * (SL + 2), "p (c s t) -> p c s t", c=2, s=nsub)
            acc = B.take(2 * T, "p (c t) -> p c t", c=2)
            o16 = C.take(2 * T, "p (c t) -> p c t", c=2)
            sq2 = [A.take(512) for i in range(2)]
            rstd = A.take(512)
            for tt in range(q.L // T):
                t0 = tt * T
                S.dma("sp", gt.t[:, :, :T], q.px.ap[19:25, :, t0:t0 + T].rearrange("c p t -> p c t"),
                      reads=q.px.tk(t0, t0 + T), writes=[gt])
                S.op("dve", lambda e: e.memset(raw.t[:], 0.0), writes=[raw])
                for c2 in range(2):
                    S.op("dve", lambda e, c2=c2: e.tensor_tensor(
                        out=raw.t[:, c2, :, 1:SL + 1], in0=gt.t[:, 2 + c2, :T].rearrange("p (s t) -> p s t", s=nsub),
                        in1=gt.t[:, 4 + c2, :T].rearrange("p (s t) -> p s t", s=nsub), op=ALU.mult),
                        reads=[gt], writes=[raw])
                    accap = acc.t[:, c2, :T].rearrange("p (s t) -> p s t", s=nsub)
                    conv3(accap, raw.t[:, c2], SL, PP["ccw"][0] + l * 6, 2, c2, acc, raw)
                    S.op("dve", lambda e, c2=c2: e.tensor_tensor(out=acc.t[:, c2, :T], in0=acc.t[:, c2, :T],
                                                                 in1=gt.t[:, c2, :T], op=ALU.mult),
                         reads=[acc, gt], writes=[acc])
                rms([acc.t[:, c2, :T] for c2 in range(2)], T, 256, rstd, sq2, P[7], [acc])
                for c2 in range(2):
                    o = PP["mg"][0] + l * 8 + 6 + c2
                    S.op("dve", lambda e, c2=c2, o=o: e.scalar_tensor_tensor(
                        out=o16.t[:, c2, :T], in0=acc.t[:, c2, :T], scalar=ppt.t[:, o:o + 1], in1=rstd.t[:, :T],
                        op0=ALU.mult, op1=ALU.mult), reads=[acc, ppt, rstd], writes=[o16])
                S.dma("pool", q.ym.ap[6:8, :, t0:t0 + T].rearrange("c p t -> p c t"), o16.t[:, :, :T],
                      reads=[o16], writes=q.ym.tk(t0, t0 + T))

        def hyprep_phase(q, l):
            phase_start()
            n = 128
            SL = 64 if not q.ctx else 128
            nsub = n // SL
            raws = [B.take(6 * nsub * (SL + 2), "p (c s t) -> p c s t", c=6, s=nsub) for i in range(2)]
            acc = B.take(768, "p (c t) -> p c t", c=6)
            ots = [A.take(768) for i in range(2)]
            for ci in range(q.nt):
                t0 = ci * 128
                raw = raws[ci % 2]
                load_halo(q, raw, 0, 6, t0, n, SL)
                for kc in range(6):
                    accap = acc.t[:, kc, :].rearrange("p (s t) -> p s t", s=nsub)
                    conv3(accap, raw.t[:, kc], SL, PP["hcw"][0] + l * 18, 6, kc, acc, raw)
                    o = PP["hcb"][0] + l * 6 + kc
                    S.op("act", lambda e, kc=kc, o=o: e.activation(out=acc.t[:, kc, :], in_=acc.t[:, kc, :],
                                                                   func=AF.Identity, bias=ppt.t[:, o:o + 1], scale=1.0),
                         reads=[acc, ppt], writes=[acc])
                    pt = P[kc // 4]
                    S.op("pe", lambda e, kc=kc, pt=pt: e.transpose(pt.t[:, (kc % 4) * 128:(kc % 4 + 1) * 128],
                                                                    acc.t[:, kc, :], ident),
                         reads=[acc, masks], writes=[pt])
                ot = ots[ci % 2]
                S.op("act", lambda e, ot=ot: e.activation(out=ot.t[:, 0:512], in_=P[0].t[:, 0:512], func=AF.Copy),
                     reads=[P[0]], writes=[ot])
                S.op("act", lambda e, ot=ot: e.activation(out=ot.t[:, 512:768], in_=P[1].t[:, 0:256], func=AF.Copy),
                     reads=[P[1]], writes=[ot])
                S.dma("pool", q.hv.ap[t0:t0 + 128, :], ot.t[:], reads=[ot], writes=q.hv.tk(t0, t0 + 128))

        def filter_phase(l, nm):
            phase_start()
            L = SEQ if nm == "b" else CTX
            T = min(512, L)
            zt2 = [A.take(512, parts=33) for i in range(2)]
            hs = [A.take(512, parts=64) for i in range(3)]
            arg = A.take(512, parts=64); kk = A.take(512, parts=64)
            dec2 = [A.take(256) for i in range(2)]
            hf2 = [A.take(1024) for i in range(2)]
            eo2 = [C.take(1024) for i in range(2)]
            fw4 = B.take(1024, parts=64)
            S.dma("sp", fw4.t[:], din["fw4"][l], writes=[fw4])
            PI_LO = 3.1415925
            for tt in range(L // T):
                t0 = tt * T
                zt = zt2[tt % 2]
                S.dma("sp", zt.t[:, :T], din["zT_" + nm][:, t0:t0 + T], writes=[zt])
                src, K = zt, 33
                for i in range(3):
                    lhsT = fw1.t[:, l, :] if i == 0 else fw23.t[:, l, i - 1, :]
                    wv = fw1 if i == 0 else fw23
                    S.op("pe", lambda e, lhsT=lhsT, src=src, K=K: e.matmul(P[0].t[0:64, :T], lhsT, src.t[0:K, :T],
                                                                          start=True, stop=True),
                         reads=[wv, src], writes=[P[0]])
                    S.op("dve", lambda e, i=i: e.tensor_scalar(out=arg.t[:, :T], in0=P[0].t[0:64, :T],
                                                               scalar1=fpp.t[:, l, i:i + 1], scalar2=fpp.t[:, l, 3:4],
                                                               op0=ALU.add, op1=ALU.mult), reads=[P[0], fpp], writes=[arg])
                    S.op("dve", lambda e: e.tensor_scalar(out=kk.t[:, :T], in0=arg.t[:, :T], scalar1=1.0 / TWO_PI,
                                                          scalar2=MAGIC, op0=ALU.mult, op1=ALU.add),
                         reads=[arg], writes=[kk])
                    S.op("dve", lambda e: e.tensor_scalar(out=kk.t[:, :T], in0=kk.t[:, :T], scalar1=-MAGIC,
                                                          scalar2=-TWO_PI, op0=ALU.add, op1=ALU.mult),
                         reads=[kk], writes=[kk])
                    S.op("dve", lambda e: e.tensor_tensor(out=arg.t[:, :T], in0=arg.t[:, :T], in1=kk.t[:, :T],
                                                          op=ALU.add), reads=[arg, kk], writes=[arg])
                    S.op("dve", lambda e: e.tensor_scalar(out=arg.t[:, :T], in0=arg.t[:, :T], scalar1=-PI_LO,
                                                          scalar2=PI_LO, op0=ALU.max, op1=ALU.min),
                         reads=[arg], writes=[arg])
                    h = hs[i]
                    S.op("act", lambda e, h=h: e.activation(out=h.t[:, :T], in_=arg.t[:, :T], func=AF.Sin),
                         reads=[arg], writes=[h])
                    src, K = h, 64
                for sub in range(T // 128):
                    ti = tt * (T // 128) + sub
                    dec = dec2[ti % 2]; hf = hf2[ti % 2]; eo = eo2[ti % 2]
                    S.dma("sp", dec.t[:], din["dec_" + nm][ti * 128:(ti + 1) * 128, :], writes=[dec])
                    for o in range(2):
                        S.op("pe", lambda e, o=o, sub=sub: e.matmul(P[1 + o].t[:, 0:512], hs[2].t[:, sub * 128:(sub + 1) * 128],
                                                                    fw4.t[:, o * 512:(o + 1) * 512], start=True, stop=True),
                             reads=[hs[2], fw4], writes=[P[1 + o]])
                        S.op("dve", lambda e, o=o, hf=hf, dec=dec: e.tensor_tensor(
                            out=hf.t[:, o * 512:(o + 1) * 512].rearrange("p (d c) -> p d c", d=2),
                            in0=P[1 + o].t[:, 0:512].rearrange("p (d c) -> p d c", d=2),
                            in1=dec.t[:].unsqueeze(1).broadcast_to([128, 2, 256]), op=ALU.mult),
                            reads=[P[1 + o], dec], writes=[hf])
                    if ti == 0:
                        for o in range(2):
                            S.op("dve", lambda e, o=o, hf=hf: e.memset(hf.t[0:1, o * 512 + 256:o * 512 + 512], 0.0),
                                 writes=[hf])
                    hv4 = hf.t[:].rearrange("p (o d c) -> p o d c", o=2, d=2)
                    S.op("dve", lambda e, eo=eo, hv4=hv4: e.tensor_tensor(
                        out=eo.t[:, 0:512].rearrange("p (o c) -> p o c", o=2), in0=hv4[:, :, 0, :], in1=hv4[:, :, 1, :],
                        op=ALU.add), reads=[hf], writes=[eo])
                    S.op("dve", lambda e, eo=eo, hv4=hv4: e.tensor_tensor(
                        out=eo.t[:, 512:1024].rearrange("p (o c) -> p o c", o=2), in0=hv4[:, :, 1, :], in1=hv4[:, :, 0, :],
                        op=ALU.subtract), reads=[hf], writes=[eo])
                    for ri in range(2):
                        S.dma("pool", EO[nm].ap[ri, ti * 128:(ti + 1) * 128, :], eo.t[:, ri * 512:(ri + 1) * 512],
                              reads=[eo], writes=EO[nm].tk())

        def load_g(G, gt, col, nrow):
            S.dma("sp", gt.t[:, 0:nrow, :], G.ap[col, :, 0:nrow, :], writes=[gt])

        def kspec_phase(l, nm):
            phase_start()
            L = SEQ if nm == "b" else CTX
            NT = L // 128
            nfc = (L + 1 + 127) // 128
            rhs = C.take(NT * 512, "p (i c) -> p i c", i=NT)
            gts = [C.take(4224, "p (i v) -> p i v", i=33) for i in range(3)]
            kts = [A.take(512) for i in range(2)]
            wfo = PP["wfb" if nm == "b" else "wfs"][0]
            for ri in range(2):
                S.dma("sp", rhs.t[:], EO[nm].ap[ri].rearrange("(i p) c -> p i c", p=128), reads=EO[nm].tk(), writes=[rhs])
                for fc in range(nfc):
                    gt = gts[fc % 3]
                    load_g(GT_[nm][ri], gt, fc, NT)
                    pk = P[fc % 2]
                    for i in range(NT):
                        S.op("pe", lambda e, i=i, gt=gt, pk=pk: e.matmul(pk.t[:, 0:512], gt.t[:, i, :], rhs.t[:, i, :],
                                                                         start=(i == 0), stop=(i == NT - 1)),
                             reads=[gt, rhs], writes=[pk])
                    kt = kts[fc % 2]
                    S.op("act", lambda e, kt=kt, pk=pk, fc=fc: e.activation(out=kt.t[:], in_=pk.t[:, 0:512], func=AF.Copy,
                                                                            scale=ppt.t[:, wfo + fc:wfo + fc + 1]),
                         reads=[pk, ppt], writes=[kt])
                    S.dma("pool", KS[nm].ap[fc, :, :, ri, :], kt.t[:].rearrange("p (o c) -> p o c", o=2),
                          reads=[kt], writes=KS[nm].tk())

        def conv_phase(q, l, order):
            phase_start()
            NT = q.nt; nfc = q.nfc; nm = q.nm
            zt = B.take(NT * 256, "p (i c) -> p i c", i=NT)
            z16 = C.take(NT * 256, "p (i c) -> p i c", i=NT)
            Y = C.take(nfc * 512, "p (f r c) -> p f r c", f=nfc, r=2)
            gts4 = [C.take(4224, "p (i v) -> p i v", i=33) for i in range(2)]
            gts = gts4
            kt2 = [A.take(512, "p (r c) -> p r c", r=2) for i in range(2)]
            tm = [A.take(256) for i in range(4)]
            gate2 = [A.take(256) for i in range(2)]
            ot2 = [C.take(256) for i in range(2)]
            srcd = q.hv if order == 0 else q.z1
            srcap = q.hv.ap[:, 0:256] if order == 0 else q.z1.ap
            S.dma("sp", zt.t[:], srcap.rearrange("(i p) c -> p i c", p=128), reads=srcd.tk(), writes=[zt])
            S.op("pool", lambda e: e.tensor_copy(out=z16.t[:], in_=zt.t[:]), reads=[zt], writes=[z16])
            Gc, Gs = GT_[nm]
            for fc in range(nfc):
                load_g(Gc, gts[0], fc, NT)
                load_g(Gs, gts[1], fc, NT)
                for i in range(NT):
                    S.op("pe", lambda e, i=i: e.matmul(P[0].t[:, 0:256], gts[0].t[:, i, :], z16.t[:, i, :], start=(i == 0),
                                                       stop=(i == NT - 1)), reads=[gts[0], z16], writes=[P[0]])
                for i in range(NT):
                    S.op("pe", lambda e, i=i: e.matmul(P[1].t[:, 0:256], gts[1].t[:, i, :], z16.t[:, i, :], start=(i == 0),
                                                       stop=(i == NT - 1)), reads=[gts[1], z16], writes=[P[1]])
                kt = kt2[fc % 2]
                S.dma("sp", kt.t[:], KS[nm].ap[fc, :, order, :, :], reads=KS[nm].tk(), writes=[kt])
                pc, psn = P[0].t[:, 0:256], P[1].t[:, 0:256]
                S.op("dve", lambda e, kt=kt: e.tensor_tensor(out=tm[0].t[:], in0=pc, in1=kt.t[:, 0, :], op=ALU.mult),
                     reads=[P[0], kt], writes=[tm[0]])
                S.op("dve", lambda e, kt=kt: e.tensor_tensor(out=tm[1].t[:], in0=psn, in1=kt.t[:, 1, :], op=ALU.mult),
                     reads=[P[1], kt], writes=[tm[1]])
                S.op("dve", lambda e, fc=fc: e.tensor_tensor(out=Y.t[:, fc, 0, :], in0=tm[0].t[:], in1=tm[1].t[:], op=ALU.add),
                     reads=[tm[0], tm[1]], writes=[Y])
                S.op("dve", lambda e, kt=kt: e.tensor_tensor(out=tm[2].t[:], in0=psn, in1=kt.t[:, 0, :], op=ALU.mult),
                     reads=[P[1], kt], writes=[tm[2]])
                S.op("dve", lambda e, kt=kt: e.tensor_tensor(out=tm[3].t[:], in0=pc, in1=kt.t[:, 1, :], op=ALU.mult),
                     reads=[P[0], kt], writes=[tm[3]])
                S.op("dve", lambda e, fc=fc: e.tensor_tensor(out=Y.t[:, fc, 1, :], in0=tm[2].t[:], in1=tm[3].t[:],
                                                             op=ALU.subtract), reads=[tm[2], tm[3]], writes=[Y])
            for tt in range(NT):
                load_g(Gc, gts[0], tt, nfc)
                load_g(Gs, gts[1], tt, nfc)
                py = P[2 + tt % 2]
                for i in range(nfc):
                    S.op("pe", lambda e, i=i, py=py: e.matmul(py.t[:, 0:256], gts[0].t[:, i, :], Y.t[:, i, 0, :],
                                                              start=(i == 0), stop=False), reads=[gts[0], Y], writes=[py])
                for i in range(nfc):
                    S.op("pe", lambda e, i=i, py=py: e.matmul(py.t[:, 0:256], gts[1].t[:, i, :], Y.t[:, i, 1, :],
                                                              start=False, stop=(i == nfc - 1)), reads=[gts[1], Y], writes=[py])
                gate = gate2[tt % 2]
                S.dma("sp", gate.t[:], q.hv.ap[tt * 128:(tt + 1) * 128, 256 * (order + 1):256 * (order + 2)],
                      reads=q.hv.tk(tt * 128, tt * 128 + 128), writes=[gate])
                t_ = tm[tt % 2]
                S.op("dve", lambda e, t_=t_, tt=tt: e.tensor_tensor(out=t_.t[:], in0=zt.t[:, tt, :],
                                                                    in1=ppb.t[:, 256 + order * 256:512 + order * 256],
                                                                    op=ALU.mult), reads=[zt, ppb], writes=[t_])
                S.op("dve", lambda e, t_=t_, py=py: e.tensor_tensor(out=t_.t[:], in0=t_.t[:], in1=py.t[:, 0:256], op=ALU.add),
                     reads=[t_, py], writes=[t_])
                S.op("dve", lambda e, t_=t_, gate=gate: e.tensor_tensor(out=t_.t[:], in0=t_.t[:], in1=gate.t[:], op=ALU.mult),
                     reads=[t_, gate], writes=[t_])
                if order == 0:
                    S.dma("pool", q.z1.ap[tt * 128:(tt + 1) * 128, :], t_.t[:], reads=[t_],
                          writes=q.z1.tk(tt * 128, tt * 128 + 128))
                else:
                    sq = tm[2 + tt % 2]
                    S.op("dve", lambda e, t_=t_, sq=sq: e.tensor_tensor(out=sq.t[:], in0=t_.t[:], in1=t_.t[:], op=ALU.mult),
                         reads=[t_], writes=[sq])
                    S.op("dve", lambda e, sq=sq: e.tensor_reduce(out=small.t[:, 0:1], in_=sq.t[:], op=ALU.add,
                                                                 axis=mybir.AxisListType.X), reads=[sq], writes=[small])
                    S.op("act", lambda e: e.activation(out=small.t[:, 1:2], in_=small.t[:, 0:1], func=AF.Sqrt,
                                                       bias=cst.t[:, 0:1], scale=1.0 / 256), reads=[small, cst], writes=[small])
                    S.op("dve", lambda e: e.reciprocal(out=small.t[:, 2:3], in_=small.t[:, 1:2]), reads=[small], writes=[small])
                    S.op("dve", lambda e, t_=t_: e.scalar_tensor_tensor(out=t_.t[:], in0=t_.t[:], scalar=small.t[:, 2:3],
                                                                        in1=ppb.t[:, 0:256], op0=ALU.mult, op1=ALU.mult),
                         reads=[t_, small, ppb], writes=[t_])
                    for c2 in range(2):
                        S.op("pe", lambda e, c2=c2, t_=t_: e.transpose(P[4].t[:, c2 * 128:(c2 + 1) * 128],
                                                                        t_.t[:, c2 * 128:(c2 + 1) * 128], ident),
                             reads=[t_, masks], writes=[P[4]])
                    ot = ot2[tt % 2]
                    S.op("act", lambda e, ot=ot: e.activation(out=ot.t[:], in_=P[4].t[:, 0:256], func=AF.Copy),
                         reads=[P[4]], writes=[ot])
                    S.dma("pool", q.ym.ap[0:2, :, tt * 128:(tt + 1) * 128].rearrange("c p t -> p c t"),
                          ot.t[:].rearrange("p (c t) -> p c t", c=2), reads=[ot], writes=q.ym.tk(tt * 128, tt * 128 + 128))

        def conv2_phase(qs, l, order):
            phase_start()
            q0 = qs[0]
            NT = q0.nt; nfc = q0.nfc; nm = q0.nm
            z16 = C.take(NT * 512, "p (i s c) -> p i s c", i=NT, s=2)
            Ylast = C.take(1024, "p (r s c) -> p r s c", r=2, s=2)
            gts = [C.take(4224, "p (i v) -> p i v", i=33) for _ in range(2)]
            ot2 = [C.take(512, "p (s c t) -> p s c t", s=2, c=2) for _ in range(2)]
            Ymain = V(AR_B[:, :].bitcast(BF16)[:, 0:(nfc - 1) * 1024].rearrange("p (f r s c) -> p f r s c", f=nfc - 1, r=2, s=2))
            B.o = 16384
            stg = [A.take(512, "p (s c) -> p s c", s=2) for _ in range(2)]
            kt2 = [A.take(512, "p (r c) -> p r c", r=2) for _ in range(2)]
            tm = [A.take(512, "p (s c) -> p s c", s=2) for _ in range(4)]
            gate2 = [A.take(512, "p (s c) -> p s c", s=2) for _ in range(2)]

            def Yv(fc, r):
                if fc < nfc - 1:
                    return Ymain.t[:, fc, r], Ymain
                return Ylast.t[:, r], Ylast

            def srcrows(q, r0):
                if order == 0:
                    return q.hv, q.hv.ap[r0:r0 + 128, 0:256]
                return q.z1, q.z1.ap[r0:r0 + 128, :]
            for i in range(NT):
                st_ = stg[i % 2]
                for s_, q in enumerate(qs):
                    sd, sa = srcrows(q, i * 128)
                    S.dma("sp", st_.t[:, s_, :], sa, reads=sd.tk(i * 128, i * 128 + 128), writes=[st_])
                if i % 2:
                    S.op("act", lambda e, i=i, st_=st_: e.activation(out=z16.t[:, i], in_=st_.t[:], func=AF.Copy),
                         reads=[st_], writes=[z16])
                else:
                    S.op("dve", lambda e, i=i, st_=st_: e.tensor_copy(out=z16.t[:, i], in_=st_.t[:]),
                         reads=[st_], writes=[z16])
            Gc, Gs = GT_[nm]
            for fc in range(nfc):
                load_g(Gc, gts[0], fc, NT)
                load_g(Gs, gts[1], fc, NT)
                for gi in range(2):
                    for i in range(NT):
                        S.op("pe", lambda e, i=i, gi=gi: e.matmul(P[gi].t[:, 0:512], gts[gi].t[:, i, :],
                                                                  z16.t[:, i].rearrange("p s c -> p (s c)"),
                                                                  start=(i == 0), stop=(i == NT - 1)),
                             reads=[gts[gi], z16], writes=[P[gi]])
                kt = kt2[fc % 2]
                S.dma("sp", kt.t[:], KS[nm].ap[fc, :, order, :, :], reads=KS[nm].tk(), writes=[kt])
                pc = P[0].t[:, 0:512].rearrange("p (s c) -> p s c", s=2)
                psn = P[1].t[:, 0:512].rearrange("p (s c) -> p s c", s=2)
                kre = kt.t[:, 0, :].unsqueeze(1).broadcast_to([128, 2, 256])
                kim = kt.t[:, 1, :].unsqueeze(1).broadcast_to([128, 2, 256])
                yre, yrev = Yv(fc, 0)
                yim, yimv = Yv(fc, 1)
                S.op("dve", lambda e, kre=kre: e.tensor_tensor(out=tm[0].t[:], in0=pc, in1=kre, op=ALU.mult),
                     reads=[P[0], kt], writes=[tm[0]])
                S.op("dve", lambda e, kim=kim: e.tensor_tensor(out=tm[1].t[:], in0=psn, in1=kim, op=ALU.mult),
                     reads=[P[1], kt], writes=[tm[1]])
                S.op("dve", lambda e, yre=yre: e.tensor_tensor(out=yre, in0=tm[0].t[:], in1=tm[1].t[:], op=ALU.add),
                     reads=[tm[0], tm[1]], writes=[yrev])
                S.op("dve", lambda e, kre=kre: e.tensor_tensor(out=tm[2].t[:], in0=psn, in1=kre, op=ALU.mult),
                     reads=[P[1], kt], writes=[tm[2]])
                S.op("dve", lambda e, kim=kim: e.tensor_tensor(out=tm[3].t[:], in0=pc, in1=kim, op=ALU.mult),
                     reads=[P[0], kt], writes=[tm[3]])
                S.op("dve", lambda e, yim=yim: e.tensor_tensor(out=yim, in0=tm[2].t[:], in1=tm[3].t[:], op=ALU.subtract),
                     reads=[tm[2], tm[3]], writes=[yimv])
            for tt in range(NT):
                load_g(Gc, gts[0], tt, nfc)
                load_g(Gs, gts[1], tt, nfc)
                py = P[2 + tt % 2]
                r0 = tt * 128
                n_mm = 2 * nfc
                k_ = 0
                for gi in range(2):
                    for i in range(nfc):
                        ya, yv_ = Yv(i, gi)
                        S.op("pe", lambda e, i=i, gi=gi, ya=ya, k_=k_, py=py: e.matmul(
                            py.t[:, 0:512], gts[gi].t[:, i, :], ya.rearrange("p s c -> p (s c)"),
                            start=(k_ == 0), stop=(k_ == n_mm - 1)), reads=[gts[gi], Ymain, Ylast], writes=[py])
                        k_ += 1
                gate = gate2[tt % 2]
                zf = stg[tt % 2]
                for s_, q in enumerate(qs):
                    S.dma("sp", gate.t[:, s_, :], q.hv.ap[r0:r0 + 128, 256 * (order + 1):256 * (order + 2)],
                          reads=q.hv.tk(r0, r0 + 128), writes=[gate])
                    sd, sa = srcrows(q, r0)
                    S.dma("sp", zf.t[:, s_, :], sa, reads=sd.tk(r0, r0 + 128), writes=[zf])
                t_ = tm[tt % 2]
                bb = ppb.t[:, 256 + order * 256:512 + order * 256].unsqueeze(1).broadcast_to([128, 2, 256])
                S.op("dve", lambda e, t_=t_, zf=zf: e.tensor_tensor(out=t_.t[:], in0=zf.t[:], in1=bb, op=ALU.mult),
                     reads=[zf, ppb], writes=[t_])
                S.op("dve", lambda e, t_=t_, py=py: e.tensor_tensor(
                    out=t_.t[:], in0=t_.t[:], in1=py.t[:, 0:512].rearrange("p (s c) -> p s c", s=2), op=ALU.add),
                    reads=[t_, py], writes=[t_])
                S.op("dve", lambda e, t_=t_, gate=gate: e.tensor_tensor(out=t_.t[:], in0=t_.t[:], in1=gate.t[:], op=ALU.mult),
                     reads=[t_, gate], writes=[t_])
                if order == 0:
                    for s_, q in enumerate(qs):
                        S.dma("pool", q.z1.ap[r0:r0 + 128, :], t_.t[:, s_, :], reads=[t_], writes=q.z1.tk(r0, r0 + 128))
                else:
                    sq = tm[2 + tt % 2]
                    S.op("dve", lambda e, t_=t_, sq=sq: e.tensor_tensor(out=sq.t[:], in0=t_.t[:], in1=t_.t[:], op=ALU.mult),
                         reads=[t_], writes=[sq])
                    S.op("dve", lambda e, sq=sq: e.tensor_reduce(out=small.t[:, 0:2], in_=sq.t[:], op=ALU.add,
                                                                 axis=mybir.AxisListType.X), reads=[sq], writes=[small])
                    S.op("act", lambda e: e.activation(out=small.t[:, 2:4], in_=small.t[:, 0:2], func=AF.Sqrt,
                                                       bias=cst.t[:, 0:1], scale=1.0 / 256), reads=[small, cst], writes=[small])
                    S.op("dve", lambda e: e.reciprocal(out=small.t[:, 4:6], in_=small.t[:, 2:4]), reads=[small], writes=[small])
                    S.op("dve", lambda e, t_=t_: e.tensor_tensor(
                        out=t_.t[:], in0=t_.t[:], in1=small.t[:, 4:6].unsqueeze(2).broadcast_to([128, 2, 256]), op=ALU.mult),
                        reads=[t_, small], writes=[t_])
                    S.op("dve", lambda e, t_=t_: e.tensor_tensor(
                        out=t_.t[:], in0=t_.t[:], in1=ppb.t[:, 0:256].unsqueeze(1).broadcast_to([128, 2, 256]), op=ALU.mult),
                        reads=[t_, ppb], writes=[t_])
                    for s_ in range(2):
                        for c2 in range(2):
                            j = s_ * 2 + c2
                            S.op("pe", lambda e, s_=s_, c2=c2, j=j, t_=t_: e.transpose(
                                P[4].t[:, j * 128:(j + 1) * 128], t_.t[:, s_, c2 * 128:(c2 + 1) * 128], ident),
                                reads=[t_, masks], writes=[P[4]])
                    ot = ot2[tt % 2]
                    S.op("act", lambda e, ot=ot: e.activation(out=ot.t[:].rearrange("p s c t -> p (s c t)"),
                                                              in_=P[4].t[:, 0:512], func=AF.Copy), reads=[P[4]], writes=[ot])
                    for s_, q in enumerate(qs):
                        S.dma("pool", q.ym.ap[0:2, :, r0:r0 + 128].rearrange("c p t -> p c t"), ot.t[:, s_],
                              reads=[ot], writes=q.ym.tk(r0, r0 + 128))

        def outproj_phase(q, l):
            phase_start()
            T = q.T
            xts = [B.take(4096, "p (c t) -> p c t", c=8) for i in range(2)]
            yt = B.take(4096, "p (c t) -> p c t", c=8)
            ymt = C.take(4096, "p (c t) -> p c t", c=8)
            sq2 = [A.take(512) for i in range(2)]
            rstd = A.take(512)
            wod = W16["wo"]
            for tt in range(q.L // T):
                t0 = tt * T
                xt = xts[tt % 2]
                S.dma("sp", xt.t[:, :, :T], q.xs.ap[:, :, t0:t0 + T].rearrange("c p t -> p c t"),
                      reads=q.xs.tk(t0, t0 + T), writes=[xt])
                S.dma("sp", ymt.t[:, :, :T], q.ym.ap[:, :, t0:t0 + T].rearrange("c p t -> p c t"),
                      reads=q.ym.tk(t0, t0 + T), writes=[ymt])
                for oc in range(8):
                    w = wnext(); S.dma("sp", w.t[:], wod.ap[l, oc], writes=[w])
                    py = P[oc % 2]
                    for kc in range(8):
                        S.op("pe", lambda e, kc=kc, w=w, py=py: e.matmul(py.t[:, :T], w.t[:, kc, :], ymt.t[:, kc, :T],
                                                                         start=(kc == 0), stop=(kc == 7)),
                             reads=[w, ymt], writes=[py])
                    S.op("act", lambda e, oc=oc, py=py: e.activation(out=yt.t[:, oc, :T], in_=py.t[:, :T], func=AF.Copy),
                         reads=[py], writes=[yt])
                resid_store(q, 1, t0, T, xt, yt, rstd, sq2, P[7], None)

        def ssd_phase(q, l, want_out):
            phase_start()
            NT = q.nt
            SL = 64 if not q.ctx else 128
            nsub = 128 // SL
            b = q.bl
            raws = [B.take(8 * nsub * (SL + 2), "p (c s t) -> p c s t", c=8, s=nsub) for _ in range(2)]
            acc = B.take(1024, "p (c s t) -> p c s t", c=8, s=nsub)
            ct0 = B.take(1024, "p (c s t) -> p c s t", c=8, s=nsub)
            ct2 = B.take(1024, "p (c s t) -> p c s t", c=8, s=nsub)
            xbcs = [B.take(1024, "p (c t) -> p c t", c=8) for _ in range(2)]
            ltall = B.take(1024, "p (h t) -> p h t", h=8)
            ehall = B.take(1024, "p (h t) -> p h t", h=8)
            mhall = B.take(1024, "p (h t) -> p h t", h=8)
            dtrs = [A.take(128, parts=16) for _ in range(2)]
            dtfs = [A.take(128, parts=16) for _ in range(2)]
            xs_tms = [A.take(512) for _ in range(2)]
            bdts = [A.take(272) for _ in range(2)]
            a_tms = [A.take(16) for _ in range(2)]
            xdt = A.take(512); xd = A.take(512)
            gm = A.take(256, "p (g t) -> p g t", g=2)
            ecs = A.take(16); H = A.take(512, "p (g t) -> p g t", g=2); ysb = A.take(512); tmp = A.take(512)
            yfls = [A.take(512) for _ in range(3)]
            zts = [A.take(512, "p (c t) -> p c t", c=4) for _ in range(3)]
            zsg = A.take(512, "p (c t) -> p c t", c=4)
            yfm = A.take(512, "p (c t) -> p c t", c=4)
            rstd = A.take(128); sq2 = [A.take(128) for _ in range(2)]
            y16 = C.take(512, "p (c t) -> p c t", c=4)
            scwo = PP["scw"][0] + l * 24
            scbo = PP["scb"][0] + l * 8
            sdo = PP["sd"][0] + l * 8

            def wb(j):
                return ppt.t[:, scwo + j * 8:scwo + j * 8 + 8].unsqueeze(2).unsqueeze(3).broadcast_to([128, 8, nsub, SL])

            def silu_to(out_ap, x_ap, sg, xv, sgv, outv):
                S.op("act", lambda e: e.activation(out=sg, in_=x_ap, func=AF.Exp, scale=-1.0), reads=[xv], writes=[sgv])
                S.op("act", lambda e: e.activation(out=sg, in_=sg, func=AF.Ln, bias=cst.t[:, 1:2], scale=1.0),
                     reads=[sgv, cst], writes=[sgv])
                S.op("act", lambda e: e.activation(out=sg, in_=sg, func=AF.Exp, scale=-1.0), reads=[sgv], writes=[sgv])
                S.op("dve", lambda e: e.tensor_tensor(out=out_ap, in0=x_ap, in1=sg, op=ALU.mult),
                     reads=[xv, sgv], writes=[outv])

            def prep_load(ci, pos, d):
                t0 = ci * 128
                k = pos % 2; k3 = pos % 3
                raw = raws[k]; dtr = dtrs[k]
                if q.ctx:
                    S.op("dve", lambda e: e.memset(raw.t[:], 0.0), writes=[raw])
                for s_ in range(nsub):
                    a0 = t0 + s_ * SL
                    S.dma("sp", raw.t[:, :, s_, 1:SL + 1], q.px.ap[10:18, :, a0:a0 + SL].rearrange("c p t -> p c t"),
                          reads=q.px.tk(a0, a0 + SL), writes=[raw])
                if q.ctx:
                    if t0 > 0:
                        S.dma("sp", raw.t[:, :, 0, 0:1], q.px.ap[10:18, :, t0 - 1:t0].rearrange("c p t -> p c t"),
                              reads=q.px.tk(t0 - 1, t0), writes=[raw])
                    if t0 + 128 < q.L:
                        S.dma("sp", raw.t[:, :, nsub - 1, SL + 1:SL + 2],
                              q.px.ap[10:18, :, t0 + 128:t0 + 129].rearrange("c p t -> p c t"),
                              reads=q.px.tk(t0 + 128, t0 + 129), writes=[raw])
                S.dma("sp", dtr.t[:], q.px.ap[18, 0:16, t0:t0 + 128], reads=q.px.tk(t0, t0 + 128), writes=[dtr])
                if want_out and d == 1:
                    S.dma("sp", zts[k3].t[:], q.px.ap[6:10, :, t0:t0 + 128].rearrange("c p t -> p c t"),
                          reads=q.px.tk(t0, t0 + 128), writes=[zts[k3]])
                    S.dma("sp", yfls[k3].t[:], q.yf.ap[t0:t0 + 128, :], reads=q.yf.tk(t0, t0 + 128), writes=[yfls[k3]])

            def prep(ci, pos, d):
                k = pos % 2; k3 = pos % 3
                raw = raws[k]; xbc = xbcs[k]; dtr = dtrs[k]; dtf = dtfs[k]; xs_tm = xs_tms[k]; bdt = bdts[k]
                a_tm = a_tms[k]
                S.op("dve", lambda e: e.tensor_tensor(out=ct0.t[:], in0=raw.t[:, :, :, 0:SL], in1=wb(0), op=ALU.mult),
                     reads=[raw, ppt], writes=[ct0])
                S.op("dve", lambda e: e.tensor_tensor(out=ct2.t[:], in0=raw.t[:, :, :, 2:SL + 2], in1=wb(2), op=ALU.mult),
                     reads=[raw, ppt], writes=[ct2])
                S.op("dve", lambda e: e.tensor_tensor(out=acc.t[:], in0=raw.t[:, :, :, 1:SL + 1], in1=wb(1), op=ALU.mult),
                     reads=[raw, ppt], writes=[acc])
                S.op("dve", lambda e: e.tensor_tensor(out=acc.t[:], in0=acc.t[:], in1=ct0.t[:], op=ALU.add),
                     reads=[acc, ct0], writes=[acc])
                S.op("dve", lambda e: e.tensor_tensor(out=acc.t[:], in0=acc.t[:], in1=ct2.t[:], op=ALU.add),
                     reads=[acc, ct2], writes=[acc])
                S.op("dve", lambda e: e.tensor_tensor(
                    out=acc.t[:], in0=acc.t[:],
                    in1=ppt.t[:, scbo:scbo + 8].unsqueeze(2).unsqueeze(3).broadcast_to([128, 8, nsub, SL]), op=ALU.add),
                    reads=[acc, ppt], writes=[acc])
                yield
                silu_to(xbc.t[:].rearrange("p c (s t) -> p c s t", s=nsub), acc.t[:], ct0.t[:], acc, ct0, xbc)
                S.op("act", lambda e: e.activation(out=dtr.t[:], in_=dtr.t[:], func=AF.Exp, bias=dtb.t[:, l:l + 1],
                                                   scale=1.0), reads=[dtr, dtb], writes=[dtr])
                S.op("act", lambda e: e.activation(out=dtf.t[:], in_=dtr.t[:], func=AF.Ln, bias=cst.t[0:16, 1:2],
                                                   scale=1.0), reads=[dtr, cst], writes=[dtf])
                yield
                for kc in range(4):
                    S.op("pe", lambda e, kc=kc: e.transpose(P[0].t[:, kc * 128:(kc + 1) * 128], xbc.t[:, kc, :], ident),
                         reads=[xbc, masks], writes=[P[0]])
                S.op("act", lambda e: e.activation(out=xs_tm.t[:], in_=P[0].t[:, 0:512], func=AF.Copy),
                     reads=[P[0]], writes=[xs_tm])
                for g in range(2):
                    S.op("pe", lambda e, g=g: e.transpose(P[1].t[:, g * 128:(g + 1) * 128], xbc.t[:, 4 + g, :], ident),
                         reads=[xbc, masks], writes=[P[1]])
                S.op("pe", lambda e: e.transpose(P[1].t[:, 256:272], dtf.t[:], masks.t[0:16, 4, 0:16]),
                     reads=[dtf, masks], writes=[P[1]])
                S.op("act", lambda e: e.activation(out=bdt.t[:], in_=P[1].t[:, 0:272], func=AF.Copy),
                     reads=[P[1]], writes=[bdt])
                S.op("dve", lambda e: e.tensor_tensor(out=a_tm.t[:], in0=bdt.t[:, 256:272], in1=Aneg.t[:], op=ALU.mult),
                     reads=[bdt, Aneg], writes=[a_tm])
                if want_out and d == 1:
                    silu_to(zts[k3].t[:], zts[k3].t[:], zsg.t[:], zts[k3], zsg, zts[k3])

            def main(ci, pos, d):
                t0 = ci * 128
                k = pos % 2; k3 = pos % 3
                xbc = xbcs[k]; xs_tm = xs_tms[k]; bdt = bdts[k]; a_tm = a_tms[k]
                mA = masks.t[:, 0 if d == 0 else 2, :]
                mB = masks.t[:, 1 if d == 0 else 3, :]
                col = 127 if d == 0 else 0
                a = a_tm.t[:, 8 * d:8 * d + 8]
                S.op("dve", lambda e: e.tensor_tensor(
                    out=xdt.t[:].rearrange("p (h c) -> p h c", h=8), in0=xs_tm.t[:].rearrange("p (h c) -> p h c", h=8),
                    in1=bdt.t[:, 256 + 8 * d:264 + 8 * d].unsqueeze(2).broadcast_to([128, 8, 64]), op=ALU.mult),
                    reads=[xs_tm, bdt], writes=[xdt])
                if want_out:
                    for g in range(2):
                        S.op("pe", lambda e, g=g: e.matmul(P[2].t[:, g * 128:(g + 1) * 128], xbc.t[:, 4 + g, :],
                                                           xbc.t[:, 6 + g, :], start=True, stop=True),
                             reads=[xbc], writes=[P[2]])
                    S.op("dve", lambda e: e.tensor_tensor(
                        out=gm.t[:], in0=P[2].t[:, 0:256].rearrange("p (g t) -> p g t", g=2),
                        in1=mB.unsqueeze(1).broadcast_to([128, 2, 128]), op=ALU.mult), reads=[P[2], masks], writes=[gm])
                S.op("pe", lambda e: e.matmul(P[3].t[:, 0:8], mB, a, start=True, stop=True),
                     reads=[masks, a_tm], writes=[P[3]])
                S.op("pe", lambda e: e.matmul(P[3].t[:, 8:16], ones.t[:], a, start=True, stop=True),
                     reads=[ones, a_tm], writes=[P[3]])
                S.op("act", lambda e: e.activation(out=ecs.t[:], in_=P[3].t[:, 0:16], func=AF.Exp),
                     reads=[P[3]], writes=[ecs])
                S.op("dve", lambda e: e.tensor_tensor(
                    out=ltall.t[:], in0=mA.unsqueeze(1).broadcast_to([128, 8, 128]),
                    in1=a.unsqueeze(2).broadcast_to([128, 8, 128]), op=ALU.mult), reads=[masks, a_tm], writes=[ltall])
                for h in range(8):
                    pseg = P[4 + h // 4]
                    S.op("pe", lambda e, h=h, pseg=pseg: e.matmul(pseg.t[:, (h % 4) * 128:(h % 4 + 1) * 128], ltall.t[:, h, :],
                                                                  mB, start=True, stop=True),
                         reads=[ltall, masks], writes=[pseg])
                yield
                for hh in range(2):
                    S.op("act", lambda e, hh=hh: e.activation(
                        out=ehall.t[:, 4 * hh:4 * hh + 4, :], in_=P[4 + hh].t[:, 0:512].rearrange("p (h t) -> p h t", h=4),
                        func=AF.Exp), reads=[P[4 + hh]], writes=[ehall])
                if want_out:
                    for g in range(2):
                        S.op("dve", lambda e, g=g: e.tensor_tensor(
                            out=mhall.t[:, 4 * g:4 * g + 4, :], in0=ehall.t[:, 4 * g:4 * g + 4, :],
                            in1=gm.t[:, g, :].unsqueeze(1).broadcast_to([128, 4, 128]), op=ALU.mult),
                            reads=[ehall, gm], writes=[mhall])
                    for h in range(8):
                        S.op("pe", lambda e, h=h: e.matmul(P[6].t[:, h * 64:(h + 1) * 64], mhall.t[:, h, :],
                                                           xdt.t[:, h * 64:(h + 1) * 64], start=True, stop=True),
                             reads=[mhall, xdt], writes=[P[6]])
                S.op("dve", lambda e: e.tensor_tensor(
                    out=xd.t[:].rearrange("p (h c) -> p h c", h=8), in0=xdt.t[:].rearrange("p (h c) -> p h c", h=8),
                    in1=ehall.t[:, :, col:col + 1].broadcast_to([128, 8, 64]), op=ALU.mult),
                    reads=[ehall, xdt], writes=[xd])
                yield
                if want_out:
                    for g in range(2):
                        S.op("pe", lambda e, g=g: e.matmul(P[7].t[:, g * 256:(g + 1) * 256], xbc.t[:, 6 + g, :], H.t[:, g, :],
                                                           start=True, stop=True), reads=[xbc, H], writes=[P[7]])
                    S.op("dve", lambda e: e.tensor_tensor(
                        out=tmp.t[:].rearrange("p (h c) -> p h c", h=8), in0=P[7].t[:, 0:512].rearrange("p (h c) -> p h c", h=8),
                        in1=ecs.t[:, 0:8].unsqueeze(2).broadcast_to([128, 8, 64]), op=ALU.mult),
                        reads=[P[7], ecs], writes=[tmp])
                    S.op("dve", lambda e: e.tensor_tensor(out=ysb.t[:], in0=tmp.t[:], in1=P[6].t[:, 0:512], op=ALU.add),
                         reads=[tmp, P[6]], writes=[ysb])
                    if d == 0:
                        S.op("dve", lambda e: e.tensor_tensor(
                            out=tmp.t[:].rearrange("p (h c) -> p h c", h=8), in0=xs_tm.t[:].rearrange("p (h c) -> p h c", h=8),
                            in1=ppt.t[:, sdo:sdo + 8].unsqueeze(2).broadcast_to([128, 8, 64]), op=ALU.mult),
                            reads=[xs_tm, ppt], writes=[tmp])
                        S.op("dve", lambda e: e.tensor_tensor(out=ysb.t[:], in0=ysb.t[:], in1=tmp.t[:], op=ALU.add),
                             reads=[ysb, tmp], writes=[ysb])
                        S.dma("pool", q.yf.ap[t0:t0 + 128, :], ysb.t[:], reads=[ysb], writes=q.yf.tk(t0, t0 + 128))
                    else:
                        S.op("dve", lambda e: e.tensor_tensor(out=ysb.t[:], in0=ysb.t[:], in1=yfls[k3].t[:], op=ALU.add),
                             reads=[ysb, yfls[k3]], writes=[ysb])
                for g in range(2):
                    S.op("pe", lambda e, g=g: e.matmul(P[7].t[:, g * 256:(g + 1) * 256], bdt.t[:, g * 128:(g + 1) * 128],
                                                       xd.t[:, g * 256:(g + 1) * 256], start=True, stop=True),
                         reads=[bdt, xd], writes=[P[7]])
                S.op("dve", lambda e: e.tensor_tensor(
                    out=H.t[:].rearrange("p g (h c) -> p (g h) c", h=4), in0=H.t[:].rearrange("p g (h c) -> p (g h) c", h=4),
                    in1=ecs.t[:, 8:16].unsqueeze(2).broadcast_to([128, 8, 64]), op=ALU.mult),
                    reads=[H, ecs], writes=[H])
                S.op("dve", lambda e: e.tensor_tensor(out=H.t[:].rearrange("p g t -> p (g t)"),
                                                      in0=H.t[:].rearrange("p g t -> p (g t)"),
                                                      in1=P[7].t[:, 0:512], op=ALU.add), reads=[H, P[7]], writes=[H])
                yield
                if want_out and d == 1:
                    for kc in range(4):
                        S.op("pe", lambda e, kc=kc: e.transpose(P[6].t[:, kc * 128:(kc + 1) * 128],
                                                                ysb.t[:, kc * 128:(kc + 1) * 128], ident),
                             reads=[ysb, masks], writes=[P[6]])
                    S.op("dve", lambda e: e.tensor_tensor(out=yfm.t[:], in0=P[6].t[:, 0:512].rearrange("p (c t) -> p c t", c=4),
                                                          in1=zts[k3].t[:], op=ALU.mult), reads=[P[6], zts[k3]], writes=[yfm])
                    rms([yfm.t[:, kc, :] for kc in range(4)], 128, 512, rstd, sq2, P[2], [yfm], lnexp=True)
                    S.op("dve", lambda e: e.tensor_tensor(
                        out=yfm.t[:], in0=yfm.t[:], in1=rstd.t[:, 0:128].unsqueeze(1).broadcast_to([128, 4, 128]),
                        op=ALU.mult), reads=[yfm, rstd], writes=[yfm])
                    mgo = PP["mg"][0] + l * 8 + 2
                    S.op("dve", lambda e: e.tensor_tensor(
                        out=y16.t[:], in0=yfm.t[:], in1=ppt.t[:, mgo:mgo + 4].unsqueeze(2).broadcast_to([128, 4, 128]),
                        op=ALU.mult), reads=[yfm, ppt], writes=[y16])
                    S.dma("pool", q.ym.ap[2:6, :, t0:t0 + 128].rearrange("c p t -> p c t"), y16.t[:],
                          reads=[y16], writes=q.ym.tk(t0, t0 + 128))

            for d in range(2):
                if q.ctx:
                    S.op("dve", lambda e: e.memset(H.t[:], 0.0), writes=[H])
                else:
                    S.dma("sp", H.t[:].rearrange("p g t -> p (g t)"), HSTD.ap[b, d], reads=HSTD.tk(), writes=[H])
                order = list(range(NT)) if d == 0 else list(range(NT - 1, -1, -1))
                if d == 0 and not q.ctx:
                    for k in range(2):
                        S.op("dve", lambda e, k=k: e.memset(raws[k].t[:], 0.0), writes=[raws[k]])
                prep_load(order[0], 0, d)
                if len(order) > 1:
                    prep_load(order[1], 1, d)
                for _ in prep(order[0], 0, d):
                    pass
                for idx, ci in enumerate(order):
                    if idx + 2 < len(order):
                        prep_load(order[idx + 2], idx + 2, d)
                    gm_ = main(ci, idx, d)
                    gp_ = prep(order[idx + 1], idx + 1, d) if idx + 1 < len(order) else iter(())
                    dm = dp = False
                    while not (dm and dp):
                        if not dm:
                            try:
                                next(gm_)
                            except StopIteration:
                                dm = True
                        if not dp:
                            try:
                                next(gp_)
                            except StopIteration:
                                dp = True
                if q.ctx:
                    S.dma("pool", HSTD.ap[b, d], H.t[:].rearrange("p g t -> p (g t)"), reads=[H], writes=HSTD.tk())

        def mixer_pair(qs, l, last):
            want = not (qs[0].ctx and last)
            for q in qs:
                proj_phase(q, l)
                ssd_phase(q, l, want)
                if want:
                    sc_phase(q, l)
                    hyprep_phase(q, l)
            if want:
                conv2_phase(qs, l, 0)
                conv2_phase(qs, l, 1)

        convert_weights()
        for l in range(depth):
            last = (l == depth - 1)
            mod_phase(l)
            S.dma("sp", ppb.t[:], din["ppb"][l], writes=[ppb])
            for q in seqs:
                ffn_phase(q, l, 0)
            if dbg == "ffn1":
                break
            filter_phase(l, "b"); kspec_phase(l, "b")
            if not last:
                filter_phase(l, "s"); kspec_phase(l, "s")
            mixer_pair([q for q in seqs if q.ctx], l, last)
            mixer_pair([q for q in seqs if not q.ctx], l, last)
            if dbg in ("mix", "proj"):
                break
            for q in seqs:
                if q.ctx and last:
                    continue
                outproj_phase(q, l)
                if dbg == "outp":
                    continue
                ffn_phase(q, l, 1)
        S.barrier()
        B.o = 0
        ot = [B.take(4096, "p (c t) -> p c t", c=8) for i in range(2)]
        i = 0
        for q in seqs:
            if q.ctx:
                continue
            for tt in range(q.L // 512):
                o = ot[i % 2]; i += 1
                srcd = q.xs
                S.dma("sp", o.t[:], srcd.ap[:, :, tt * 512:(tt + 1) * 512].rearrange("c p t -> p c t"),
                      reads=srcd.tk(tt * 512, tt * 512 + 512), writes=[o])
                S.dma("pool", out[q.bl, :, :, tt * 512:(tt + 1) * 512].rearrange("c p t -> p c t"), o.t[:], reads=[o])
        S.barrier()
        print("instructions:", S.nins)
    return nc


def _prep(inputs):
    sh = _host_shared(inputs)
    x = np.asarray(inputs["x"], np.float32)
    ctx = np.asarray(inputs["ctx"], np.float32)
    c = np.asarray(inputs["c"], np.float32)
    cc = np.asarray(inputs["c_ctx"], np.float32)
    maps = []
    for core in range(NCORE):
        m = dict(sh)
        b0 = core * BPC
        m["xl"] = np.ascontiguousarray(x[b0:b0 + BPC].transpose(0, 2, 1)).reshape(BPC, 8, 128, SEQ)
        m["xc"] = np.ascontiguousarray(ctx[b0:b0 + BPC].transpose(0, 2, 1)).reshape(BPC, 8, 128, CTX)
        cT = np.stack([c[b0], c[b0 + 1], cc], axis=-1)
        m["cT"] = np.ascontiguousarray(cT.reshape(8, 128, 3).transpose(1, 0, 2))
        maps.append(m)
    return maps


def kernel(**inputs):
    maps = _prep(inputs)
    shapes = {k: v.shape for k, v in maps[0].items()}
    nc = build(DEPTH, shapes)
    res = run_bass_kernel_spmd(nc, maps, core_ids=list(range(NCORE)))
    outs = []
    for core in range(NCORE):
        o = res.results[core]["out"].reshape(BPC, D, SEQ)
        outs.append(o.transpose(0, 2, 1))
    return np.ascontiguousarray(np.concatenate(outs, axis=0)).astype(np.float32)
```

```python
import math
import contextlib
import numpy as np
import ml_dtypes
import concourse.bass as bass
import concourse.mybir as mybir
from concourse.bass_utils import run_bass_kernel_spmd

F32 = mybir.dt.float32
BF16 = mybir.dt.bfloat16
AF = mybir.ActivationFunctionType
ALU = mybir.AluOpType

D = 1024; DEPTH = 4; BATCH = 16; SEQ = 4096; CTX = 256; DFF = 2816; NMOD = 9
NCORE = 8; BPC = BATCH // NCORE
EPS = 1e-6
HYW = 256; SSDW = 512; NH = 8; HD = 64; NG = 2; NS = 128
NPXC = 25
NDS = 8
TWO_PI = 2.0 * math.pi
MAGIC = 12582912.0


class Tok:
    __slots__ = ("w", "r")

    def __init__(self):
        self.w = None
        self.r = {}


class V:
    def __init__(self, t, tok=None):
        self.t = t
        self.tok = tok or Tok()

    def __getitem__(self, k):
        return self.t[k]


class Sched:
    def __init__(self, nc, st):
        self.nc = nc
        self.st = st
        self.E = {"pe": nc.tensor, "act": nc.scalar, "dve": nc.vector, "pool": nc.gpsimd, "sp": nc.sync}
        self.sem = {}
        self.cnt = {}
        for k in ["pe", "act", "dve", "pool"]:
            self.sem["c_" + k] = st.enter_context(nc.semaphore("c_" + k))
            self.cnt["c_" + k] = 0
        self.drr = {"sp": 0, "pool": 0}
        for q in ["sp", "pool"]:
            for i in range(NDS):
                key = "d_%s%d" % (q, i)
                self.sem[key] = st.enter_context(nc.semaphore(key))
                self.cnt[key] = 0
        self.seen = {e: {} for e in self.E}
        self.nins = 0

    def _wait(self, eng, key, val):
        if eng == "pe" and key == "c_pe":
            return
        if self.seen[eng].get(key, 0) >= val:
            return
        self.E[eng].wait_ge(self.sem[key], val)
        self.seen[eng][key] = val

    def _deps(self, eng, reads, writes):
        for t in reads:
            if t.w:
                self._wait(eng, *t.w)
        for t in writes:
            if t.w:
                self._wait(eng, *t.w)
            for k, v in t.r.items():
                self._wait(eng, k, v)

    def _mark(self, me, reads, writes):
        for t in reads:
            t.r[me[0]] = me[1]
        for t in writes:
            t.w = me
            t.r = {}

    def op(self, eng, fn, reads=(), writes=()):
        reads = [x.tok if isinstance(x, V) else x for x in reads]
        writes = [x.tok if isinstance(x, V) else x for x in writes]
        self._deps(eng, reads, writes)
        ins = fn(self.E[eng])
        key = "c_" + eng
        self.cnt[key] += 1
        ins.then_inc(self.sem[key], 1)
        self._mark((key, self.cnt[key]), reads, writes)
        self.nins += 1

    def dma(self, q, out, in_, reads=(), writes=()):
        reads = [x.tok if isinstance(x, V) else x for x in reads]
        writes = [x.tok if isinstance(x, V) else x for x in writes]
        i = self.drr[q]
        self.drr[q] = (i + 1) % NDS
        key = "d_%s%d" % (q, i)
        if self.cnt[key] > 0:
            self._wait(q, key, self.cnt[key])
        self._deps(q, reads, writes)
        ins = self.E[q].dma_start(out=out, in_=in_)
        self.cnt[key] += 16
        ins.then_inc(self.sem[key], 16)
        self._mark((key, self.cnt[key]), reads, writes)
        self.nins += 1

    def barrier(self):
        for e in self.E:
            for k, v in self.cnt.items():
                if v > 0:
                    self._wait(e, k, v)


class DT:
    def __init__(self, ap, n):
        self.ap = ap
        self.toks = [Tok() for _ in range(max(1, n))]

    def tk(self, t0=None, t1=None):
        if t0 is None:
            return self.toks
        return self.toks[t0 // 128:(t1 + 127) // 128]


def _consts():
    f32 = np.float32
    c = {}
    r = np.arange(128)
    m = np.zeros((128, 5, 128), f32)
    m[:, 0] = (r[:, None] > r[None, :])
    m[:, 1] = (r[:, None] <= r[None, :])
    m[:, 2] = (r[:, None] < r[None, :])
    m[:, 3] = (r[:, None] >= r[None, :])
    m[:, 4] = np.eye(128)
    c["masks"] = m
    for nm, L in (("b", SEQ), ("s", CTX)):
        N = 2 * L
        nfc = (L + 1 + 127) // 128
        P = nfc * 128
        u = np.arange(P, dtype=np.int64)
        ph = (u[:, None] * u[None, :]) % N
        ang = ph.astype(np.float64) * (2.0 * math.pi / N)
        valid = ((u[:, None] <= L) & (u[None, :] <= L))
        def lay(g):
            g = g.astype(f32).astype(ml_dtypes.bfloat16).reshape(nfc, 128, nfc, 128)
            return np.ascontiguousarray(g.transpose(2, 1, 0, 3))
        c["gc_" + nm] = lay(np.where(valid, np.cos(ang), 0.0))
        c["gs_" + nm] = lay(np.where(valid, np.sin(ang), 0.0))
        wf = np.zeros(P, np.float64)
        wf[:L + 1] = 2.0 / N
        wf[0] = 1.0 / N
        wf[L] = 1.0 / N
        c["wf_" + nm] = np.ascontiguousarray(wf.reshape(nfc, 128).T).astype(f32)
        t = np.linspace(0.0, 1.0, L, dtype=f32)[:, None]
        bands = 16
        w = (2.0 * math.pi * np.arange(L, dtype=f32)[:, None] / L).astype(f32)
        f = np.linspace(1e-4, bands - 1, bands, dtype=f32)[None, :]
        z = np.concatenate([t, np.cos(f * w), -np.sin(f * w)], axis=-1).astype(f32)
        c["zT_" + nm] = np.ascontiguousarray(z.T)
        deltas = np.linspace(math.log(1e-2) / 1.5, math.log(1e-2) / 0.3, HYW, dtype=f32)
        c["dec_" + nm] = np.exp(-t * np.abs(deltas)).astype(f32)
    return c


def _fm(v):
    v = np.asarray(v, np.float32)
    n = v.shape[-1] // 128
    v = v.reshape(v.shape[:-1] + (n, 128))
    return np.ascontiguousarray(np.moveaxis(v, -1, 0))


def _wl(w, kc):
    K, M = w.shape
    return np.ascontiguousarray(w.reshape(kc, 128, M // 128, 128).transpose(2, 1, 0, 3))


PP = {}


def _pp_layout():
    off = 0
    for nm, n in (("ng", DEPTH * 6 * 8), ("bm", DEPTH * 72), ("hcw", DEPTH * 3 * 6), ("hcb", DEPTH * 6),
                  ("scw", DEPTH * 3 * 8), ("scb", DEPTH * 8), ("ccw", DEPTH * 3 * 2), ("mg", DEPTH * 8),
                  ("alog", DEPTH * 16), ("sd", DEPTH * 8),
                  ("wfb", 33), ("wfs", 3)):
        PP[nm] = (off, n)
        off += n
    return off


NPP = _pp_layout()


def _host_shared(inp):
    f32 = np.float32
    sh = {}
    sh.update(_consts())
    pp = np.zeros((128, NPP), f32)

    def put(nm, a):
        o, n = PP[nm]
        pp[:, o:o + n] = np.asarray(a, f32).reshape(128, n)
    put("ng", _fm(inp["norm_g"]))
    put("bm", _fm(inp["b_mod"]))
    put("hcw", _fm(inp["hy_conv_w"]))
    put("hcb", _fm(inp["hy_conv_b"]))
    put("scw", _fm(inp["ssd_conv_w"]))
    put("scb", _fm(inp["ssd_conv_b"]))
    put("ccw", _fm(inp["sc_conv_w"]))
    put("mg", _fm(inp["mix_gain"]))
    sh["ppb"] = np.ascontiguousarray(np.concatenate([
        np.broadcast_to(np.asarray(inp["mix_gain"], f32)[:, None, :HYW], (DEPTH, 128, HYW)),
        np.broadcast_to(np.asarray(inp["hy_bias"], f32).reshape(DEPTH, 1, 512), (DEPTH, 128, 512))], axis=2))
    put("alog", np.broadcast_to(np.asarray(inp["ssd_a_log"]).reshape(1, DEPTH, 16), (128, DEPTH, 16)))
    put("sd", np.broadcast_to(np.asarray(inp["ssd_d"]).reshape(1, DEPTH, 8), (128, DEPTH, 8)))
    put("wfb", sh.pop("wf_b"))
    put("wfs", sh.pop("wf_s"))
    sh["pp"] = pp
    sh["wm"] = np.stack([_wl(np.asarray(inp["w_mod"][l], f32), 8) for l in range(DEPTH)])
    sh["fwi"] = np.stack([np.stack([_wl(np.asarray(inp["ffn_w_in"][l, j], f32), 8) for j in range(2)])
                          for l in range(DEPTH)])
    sh["fwo"] = np.stack([np.stack([_wl(np.asarray(inp["ffn_w_out"][l, j], f32), 22) for j in range(2)])
                          for l in range(DEPTH)])
    wi = np.asarray(inp["w_in"], f32)
    wip = np.zeros((DEPTH, D, NPXC * 128), f32)
    wip[:, :, 0:2304] = wi[:, :, 0:2304]
    wip[:, :, 2304:2320] = wi[:, :, 2304:2320]
    wip[:, :, 2432:3200] = wi[:, :, 2320:3088]
    sh["wi"] = np.stack([_wl(wip[l], 8) for l in range(DEPTH)])
    sh["wo"] = np.stack([_wl(np.asarray(inp["w_out"][l], f32), 8) for l in range(DEPTH)])
    sh["fw1"] = np.ascontiguousarray(np.asarray(inp["hy_fw1"], f32).transpose(1, 0, 2))
    sh["fw23"] = np.ascontiguousarray(np.stack([np.asarray(inp["hy_fw2"], f32), np.asarray(inp["hy_fw3"], f32)],
                                               axis=1).transpose(2, 0, 1, 3))
    sh["fw4"] = np.ascontiguousarray(np.asarray(inp["hy_fw4"], f32))
    sh["fpp"] = np.ascontiguousarray(np.stack([np.asarray(inp["hy_fb1"], f32), np.asarray(inp["hy_fb2"], f32),
                                               np.asarray(inp["hy_fb3"], f32), np.asarray(inp["hy_freq"], f32)],
                                              axis=1).transpose(2, 0, 1))
    sh["dtb"] = np.ascontiguousarray(np.asarray(inp["ssd_dt_bias"], f32).reshape(DEPTH, 16).T)
    return sh


SHARED_SHAPES = None


def build(depth, shapes, dbg=None):
    nc = bass.Bass("TRN2", target_bir_lowering=False)
    din = {k: nc.dram_tensor(k, list(s), BF16 if k[:3] in ("gc_", "gs_") else F32, kind="ExternalInput").ap() for k, s in shapes.items()}
    out = nc.dram_tensor("out", [BPC, 8, 128, SEQ], F32, kind="ExternalOutput").ap()

    def scratch(nm, shape, nt, dt=F32):
        return DT(nc.dram_tensor(nm, list(shape), dt, kind="Internal").ap(), nt)

    with contextlib.ExitStack() as st:
        S = Sched(nc, st)
        st.enter_context(nc.allow_non_contiguous_dma(reason="small strided halo / layout DMAs"))

        def sb(nm, shape, dt=F32):
            return st.enter_context(nc.sbuf_tensor("sb_" + nm, list(shape), dt))

        class Ar:
            def __init__(self, t, size):
                self.t = t; self.size = size; self.o = 0

            def take(self, n, pat=None, parts=None, **kw):
                assert self.o + n <= self.size, (self.o, n, self.size)
                v = self.t[:, self.o:self.o + n] if parts is None else self.t[0:parts, self.o:self.o + n]
                self.o += n
                return V(v.rearrange(pat, **kw) if pat else v)

        def phase_start():
            S.barrier()
            A.o = 0; B.o = 0; C.o = 0

        def ps(nm):
            return V(st.enter_context(nc.psum_tensor(nm, [128, 512], F32)))
        P = [ps("ps%d" % i) for i in range(8)]
        AR_A = sb("arA", [128, 10752])
        AR_B = sb("arB", [128, 16384])
        AR_C = sb("arC", [128, 34048], BF16)
        A = Ar(AR_A, 10752); B = Ar(AR_B, 16384); C = Ar(AR_C, 34048)
        NW = 8
        WPOOL = [V(sb("w%d" % i, [128, 8, 128], BF16)) for i in range(NW)]
        wrr = [0]

        def wnext():
            wrr[0] = (wrr[0] + 1) % NW
            return WPOOL[wrr[0]]
        masks = V(sb("masks", [128, 5, 128]))
        ppt = V(sb("pp", [128, NPP]))
        ones = V(sb("ones", [128, 128]))
        cst = V(sb("cst", [128, 4]))
        sct = V(sb("sct", [128, 8, 3]))
        MOD = V(sb("mod", [128, 9, 8, 3]))
        GIN = V(sb("gin", [128, 3, 8, 3]))
        COUT = V(sb("cout", [128, 3, 8, 3]))
        Aneg = V(sb("aneg", [128, 16]))
        fw1 = V(sb("fw1", [33, 4, 64])); fw23 = V(sb("fw23", [64, 4, 2, 64])); ppb = V(sb("ppb", [128, 768]))
        fpp = V(sb("fpp", [64, 4, 4])); dtb = V(sb("dtb", [16, 4]))
        HSTD = scratch("hstd", [2, 2, 128, 512], 1)
        small = V(sb("small", [128, 64]))

        def pp(nm, *idx_shape):
            o, n = PP[nm]
            return ppt.t[:, o:o + n]

        def viewA(o, n):
            return V(AR_A[:, o:o + n])

        def viewB(o, n):
            return V(AR_B[:, o:o + n])

        S.dma("sp", masks.t[:], din["masks"], writes=[masks])
        S.dma("sp", ppt.t[:], din["pp"], writes=[ppt])
        S.dma("sp", sct.t[:], din["cT"], writes=[sct])
        S.dma("sp", fw1.t[:], din["fw1"], writes=[fw1])
        S.dma("sp", fw23.t[:], din["fw23"], writes=[fw23])
        S.dma("sp", fpp.t[:], din["fpp"], writes=[fpp])
        S.dma("sp", dtb.t[:], din["dtb"], writes=[dtb])
        S.op("dve", lambda e: e.memset(ones.t[:], 1.0), writes=[ones])
        S.op("dve", lambda e: e.memset(cst.t[:, 0:1], EPS), writes=[cst])
        S.op("dve", lambda e: e.memset(cst.t[:, 1:2], 1.0), writes=[cst])
        S.op("dve", lambda e: e.memset(cst.t[:, 2:3], 0.0), writes=[cst])
        S.op("act", lambda e: e.activation(out=sct.t[:], in_=sct.t[:], func=AF.Silu), reads=[sct], writes=[sct])
        ident = masks.t[:, 4, :]

        class Seq:
            pass
        seqs = []
        for s in range(2 * BPC):
            q = Seq()
            q.ctx = s >= BPC
            q.bl = s % BPC
            q.L = CTX if q.ctx else SEQ
            q.T = min(512, q.L)
            q.who = 2 if q.ctx else q.bl
            q.nt = q.L // 128
            q.nm = "s" if q.ctx else "b"
            src = din["xc"] if q.ctx else din["xl"]
            q.xin = DT(src[q.bl], q.nt)
            q.xs = scratch("xs%d" % s, [8, 128, q.L], q.nt)
            q.px = scratch("px%d" % s, [NPXC, 128, q.L], q.nt)
            q.ym = scratch("ym%d" % s, [8, 128, q.L], q.nt, BF16)
            q.yf = scratch("yf%d" % s, [q.L, 512], q.nt)
            q.hv = scratch("hv%d" % s, [q.L, 768], q.nt)
            q.z1 = scratch("z1%d" % s, [q.L, 256], q.nt)
            q.cache = scratch("cache%d" % s, [q.nt, 128, 1808], q.nt)
            q.nfc = (q.L + 1 + 127) // 128
            q.first = True
            seqs.append(q)
        KS = {}
        EO = {}
        for nm, L in (("b", SEQ), ("s", CTX)):
            nfc = (L + 1 + 127) // 128
            KS[nm] = scratch("ks_" + nm, [nfc, 128, 2, 2, 256], 1)
            EO[nm] = scratch("eo_" + nm, [2, L, 512], 1, BF16)
        GT_ = {nm: (DT(din["gc_" + nm], 1), DT(din["gs_" + nm], 1)) for nm in "bs"}

        def xsrc(q):
            return q.xin if q.first else q.xs

        def rms(src_aps, T, nD, rstd, sq2, pn, reads, lnexp=False):
            n = len(src_aps)
            for i, a in enumerate(src_aps):
                sq = sq2[i % 2]
                S.op("act", lambda e, a=a, sq=sq: e.activation(out=sq.t[:, :T], in_=a, func=AF.Square),
                     reads=reads, writes=[sq])
                S.op("pe", lambda e, sq=sq, i=i: e.matmul(pn.t[:, :T], ones.t[:], sq.t[:, :T], start=(i == 0),
                                                           stop=(i == n - 1)), reads=[sq, ones], writes=[pn])
            if lnexp:
                S.op("act", lambda e: e.activation(out=rstd.t[:, :T], in_=pn.t[:, :T], func=AF.Ln,
                                                   bias=cst.t[:, 0:1], scale=1.0 / nD), reads=[pn, cst], writes=[rstd])
                S.op("act", lambda e: e.activation(out=rstd.t[:, :T], in_=rstd.t[:, :T], func=AF.Exp, scale=-0.5),
                     reads=[rstd], writes=[rstd])
                return
            S.op("act", lambda e: e.activation(out=rstd.t[:, :T], in_=pn.t[:, :T], func=AF.Sqrt,
                                               bias=cst.t[:, 0:1], scale=1.0 / nD), reads=[pn, cst], writes=[rstd])
            S.op("dve", lambda e: e.reciprocal(out=rstd.t[:, :T], in_=rstd.t[:, :T]), reads=[rstd], writes=[rstd])

        def load_norm(q, sub, t0, T, xt, ht, rstd, sq2, pn, tmp2):
            xsr = xsrc(q)
            S.dma("sp", xt.t[:, :, :T], xsr.ap[:, :, t0:t0 + T].rearrange("c p t -> p c t"),
                  reads=xsr.tk(t0, t0 + T), writes=[xt])
            rms([xt.t[:, kc, :T] for kc in range(8)], T, D, rstd, sq2, pn, [xt])
            for kc in range(8):
                tp = tmp2[kc % 2]
                S.op("dve", lambda e, kc=kc, tp=tp: e.scalar_tensor_tensor(
                    out=tp.t[:, :T], in0=xt.t[:, kc, :T], scalar=GIN.t[:, sub, kc, q.who:q.who + 1],
                    in1=rstd.t[:, :T], op0=ALU.mult, op1=ALU.mult), reads=[xt, GIN, rstd], writes=[tp])
                S.op("act", lambda e, kc=kc, tp=tp: e.activation(
                    out=ht.t[:, kc, :T], in_=tp.t[:, :T], func=AF.Identity,
                    bias=MOD.t[:, 3 * sub, kc, q.who:q.who + 1], scale=1.0), reads=[tp, MOD], writes=[ht])

        def resid_store(q, sub, t0, T, xt, yt, rstd, sq2, pn, tmp, dst=None):
            rms([yt.t[:, kc, :T] for kc in range(8)], T, D, rstd, sq2, pn, [yt])
            for kc in range(8):
                S.op("dve", lambda e, kc=kc: e.scalar_tensor_tensor(
                    out=yt.t[:, kc, :T], in0=yt.t[:, kc, :T], scalar=COUT.t[:, sub, kc, q.who:q.who + 1],
                    in1=rstd.t[:, :T], op0=ALU.mult, op1=ALU.mult), reads=[yt, COUT, rstd], writes=[yt])
                S.op("dve", lambda e, kc=kc: e.tensor_tensor(out=xt.t[:, kc, :T], in0=xt.t[:, kc, :T],
                                                             in1=yt.t[:, kc, :T], op=ALU.add),
                     reads=[xt, yt], writes=[xt])
            d = dst or q.xs
            S.dma("pool", d.ap[:, :, t0:t0 + T].rearrange("c p t -> p c t"), xt.t[:, :, :T],
                  reads=[xt], writes=d.tk(t0, t0 + T))

        W16 = {}

        def convert_weights():
            phase_start()
            stg = [B.take(4096) for _ in range(3)]
            out16 = [C.take(4096) for _ in range(3)]
            engs = ["act", "dve", "pool"]
            n = [0]
            for nm, grp in (("fwi", 4), ("fwo", 1), ("wi", 4), ("wo", 4)):
                src = din[nm]
                shp = list(src.shape)
                W16[nm] = DT(nc.dram_tensor(nm + "16", shp, BF16, kind="Internal").ap(), 1)
                dst = W16[nm].ap
                lead = shp[:-4]
                noc = shp[-4]
                F = shp[-2] * shp[-1]
                idxs = [()]
                for d_ in lead:
                    idxs = [i + (k,) for i in idxs for k in range(d_)]
                for ix in idxs:
                    sa = src; da = dst
                    for k in ix:
                        sa = sa[k]; da = da[k]
                    for o0 in range(0, noc, grp):
                        g = min(grp, noc - o0)
                        i = n[0] % 3; n[0] += 1
                        sv = stg[i].t[:, 0:g * F].rearrange("p (o f) -> p o f", o=g)
                        ov = out16[i].t[:, 0:g * F].rearrange("p (o f) -> p o f", o=g)
                        S.dma("sp", sv, sa[o0:o0 + g].rearrange("o p k m -> p o (k m)"), writes=[stg[i]])
                        if engs[i] == "act":
                            S.op("act", lambda e, sv=sv, ov=ov: e.activation(out=ov, in_=sv, func=AF.Copy),
                                 reads=[stg[i]], writes=[out16[i]])
                        else:
                            S.op(engs[i], lambda e, sv=sv, ov=ov: e.tensor_copy(out=ov, in_=sv),
                                 reads=[stg[i]], writes=[out16[i]])
                        S.dma("pool", da[o0:o0 + g].rearrange("o p k m -> p o (k m)"), ov, reads=[out16[i]])

        def mod_phase(l):
            phase_start()
            wmd = DT(din["wm"], 1)
            WF32 = [B.take(1024, "p (k m) -> p k m", k=8) for i in range(4)]
            for oc in range(72):
                w = WF32[oc % 4]
                S.dma("sp", w.t[:], wmd.ap[l, oc], writes=[w])
                pm = P[oc % 2]
                for kc in range(8):
                    S.op("pe", lambda e, kc=kc, w=w, pm=pm: e.matmul(pm.t[:, 0:3], w.t[:, kc, :], sct.t[:, kc, :],
                                                                     start=(kc == 0), stop=(kc == 7)),
                         reads=[w, sct], writes=[pm])
                o = PP["bm"][0] + l * 72 + oc
                S.op("act", lambda e, oc=oc, pm=pm, o=o: e.activation(
                    out=MOD.t[:, oc // 8, oc % 8, :], in_=pm.t[:, 0:3], func=AF.Identity,
                    bias=ppt.t[:, o:o + 1], scale=1.0), reads=[pm, ppt], writes=[MOD])
            ngo = PP["ng"][0] + l * 48
            for i in range(3):
                gpre = ppt.t[:, ngo + (2 * i) * 8: ngo + (2 * i) * 8 + 8].unsqueeze(2).broadcast_to([128, 8, 3])
                gpost = ppt.t[:, ngo + (2 * i + 1) * 8: ngo + (2 * i + 1) * 8 + 8].unsqueeze(2).broadcast_to([128, 8, 3])
                S.op("dve", lambda e, i=i: e.tensor_scalar(out=GIN.t[:, i], in0=MOD.t[:, 3 * i + 1], scalar1=1.0,
                                                           scalar2=None, op0=ALU.add), reads=[MOD], writes=[GIN])
                S.op("dve", lambda e, i=i, g=gpre: e.tensor_tensor(out=GIN.t[:, i], in0=GIN.t[:, i], in1=g,
                                                                   op=ALU.mult), reads=[GIN, ppt], writes=[GIN])
                rw = 1.0 if i == 1 else 0.5
                S.op("dve", lambda e, i=i, g=gpost, rw=rw: e.scalar_tensor_tensor(
                    out=COUT.t[:, i], in0=MOD.t[:, 3 * i + 2], scalar=rw, in1=g, op0=ALU.mult, op1=ALU.mult),
                    reads=[MOD, ppt], writes=[COUT])
            o = PP["alog"][0] + l * 16
            S.op("act", lambda e: e.activation(out=Aneg.t[:], in_=ppt.t[:, o:o + 16], func=AF.Exp),
                 reads=[ppt], writes=[Aneg])
            S.op("dve", lambda e: e.tensor_scalar(out=Aneg.t[:], in0=Aneg.t[:], scalar1=-1.0, scalar2=None,
                                                  op0=ALU.mult), reads=[Aneg], writes=[Aneg])

        def ffn_phase(q, l, j):
            phase_start()
            sub = 2 * j
            T = q.T
            xts = [B.take(4096, "p (c t) -> p c t", c=8) for i in range(2)]
            yts = [B.take(4096, "p (c t) -> p c t", c=8) for i in range(2)]
            hts = [C.take(4096, "p (c t) -> p c t", c=8) for i in range(2)]
            at = C.take(11264, "p (c t) -> p c t", c=22)
            wos = [C.take(2816, "p (c m) -> p c m", c=22) for i in range(3)]
            sq2 = [A.take(512) for i in range(2)]
            rstd = A.take(512)
            sg2 = [A.take(512) for i in range(2)]
            tmp2 = [A.take(512) for i in range(2)]
            fwi = W16["fwi"]
            fwo = W16["fwo"]
            ntile = q.L // T
            load_norm(q, sub, 0, T, xts[0], hts[0], rstd, sq2, P[7], tmp2)
            for tt in range(ntile):
                t0 = tt * T
                xt = xts[tt % 2]; ht = hts[tt % 2]; yt = yts[tt % 2]
                for jf in range(22):
                    if jf == 4 and tt > 0:
                        resid_store(q, sub, t0 - T, T, xts[(tt - 1) % 2], yts[(tt - 1) % 2], rstd, sq2, P[7], None)
                    if jf == 12 and tt + 1 < ntile:
                        load_norm(q, sub, t0 + T, T, xts[(tt + 1) % 2], hts[(tt + 1) % 2], rstd, sq2, P[7], tmp2)
                    wg = wnext(); S.dma("sp", wg.t[:], fwi.ap[l, j, jf], writes=[wg])
                    wu = wnext(); S.dma("sp", wu.t[:], fwi.ap[l, j, 22 + jf], writes=[wu])
                    pg = P[(2 * jf) % 4]; pu = P[(2 * jf) % 4 + 1]
                    for kc in range(8):
                        S.op("pe", lambda e, kc=kc, wg=wg, pg=pg: e.matmul(pg.t[:, :T], wg.t[:, kc, :], ht.t[:, kc, :T],
                                                                           start=(kc == 0), stop=(kc == 7)),
                             reads=[wg, ht], writes=[pg])
                    for kc in range(8):
                        S.op("pe", lambda e, kc=kc, wu=wu, pu=pu: e.matmul(pu.t[:, :T], wu.t[:, kc, :], ht.t[:, kc, :T],
                                                                           start=(kc == 0), stop=(kc == 7)),
                             reads=[wu, ht], writes=[pu])
                    sg = sg2[jf % 2]
                    S.op("act", lambda e, sg=sg, pg=pg: e.activation(out=sg.t[:, :T], in_=pg.t[:, :T], func=AF.Silu),
                         reads=[pg], writes=[sg])
                    S.op("dve", lambda e, sg=sg, pu=pu, jf=jf: e.tensor_tensor(out=at.t[:, jf, :T], in0=sg.t[:, :T],
                                                                               in1=pu.t[:, :T], op=ALU.mult),
                         reads=[sg, pu], writes=[at])
                for oc in range(8):
                    wo = wos[oc % 3]
                    S.dma("sp", wo.t[:], fwo.ap[l, j, oc], writes=[wo])
                    py = P[4 + oc % 2]
                    for fc in range(22):
                        S.op("pe", lambda e, fc=fc, wo=wo, py=py: e.matmul(py.t[:, :T], wo.t[:, fc, :], at.t[:, fc, :T],
                                                                           start=(fc == 0), stop=(fc == 21)),
                             reads=[wo, at], writes=[py])
                    S.op("act", lambda e, oc=oc, py=py, yt=yt: e.activation(out=yt.t[:, oc, :T], in_=py.t[:, :T],
                                                                            func=AF.Copy), reads=[py], writes=[yt])
            resid_store(q, sub, (ntile - 1) * T, T, xts[(ntile - 1) % 2], yts[(ntile - 1) % 2], rstd, sq2, P[7], None)
            q.first = False


        def proj_phase(q, l):
            phase_start()
            T = q.T
            xts = [B.take(4096, "p (c t) -> p c t", c=8) for i in range(2)]
            ht = C.take(4096, "p (c t) -> p c t", c=8)
            sq2 = [A.take(512) for i in range(2)]
            rstd = A.take(512)
            tmp2 = [A.take(512) for i in range(2)]
            ot4 = [A.take(512) for i in range(4)]
            wi = W16["wi"]
            for tt in range(q.L // T):
                t0 = tt * T
                xt = xts[tt % 2]
                load_norm(q, 1, t0, T, xt, ht, rstd, sq2, P[7], tmp2)
                for oc in range(NPXC):
                    w = wnext(); S.dma("sp", w.t[:], wi.ap[l, oc], writes=[w])
                    pm = P[oc % 4]; o = ot4[oc % 4]
                    for kc in range(8):
                        S.op("pe", lambda e, kc=kc, w=w, pm=pm: e.matmul(pm.t[:, :T], w.t[:, kc, :], ht.t[:, kc, :T],
                                                                         start=(kc == 0), stop=(kc == 7)),
                             reads=[w, ht], writes=[pm])
                    S.op("act", lambda e, o=o, pm=pm: e.activation(out=o.t[:, :T], in_=pm.t[:, :T], func=AF.Copy),
                         reads=[pm], writes=[o])
                    S.dma("pool", q.px.ap[oc, :, t0:t0 + T], o.t[:, :T], reads=[o], writes=q.px.tk(t0, t0 + T))

        def conv3(acc_ap, raw_ap3, SL, wo, nw, kc, accv, rawv):
            def wj(j):
                o = wo + j * nw + kc
                return ppt.t[:, o:o + 1]
            S.op("dve", lambda e: e.tensor_scalar(out=acc_ap, in0=raw_ap3[:, :, 1:SL + 1], scalar1=wj(1), scalar2=None,
                                                  op0=ALU.mult), reads=[rawv, ppt], writes=[accv])
            S.op("dve", lambda e: e.scalar_tensor_tensor(out=acc_ap, in0=raw_ap3[:, :, 0:SL], scalar=wj(0), in1=acc_ap,
                                                         op0=ALU.mult, op1=ALU.add), reads=[rawv, ppt, accv], writes=[accv])
            S.op("dve", lambda e: e.scalar_tensor_tensor(out=acc_ap, in0=raw_ap3[:, :, 2:SL + 2], scalar=wj(2), in1=acc_ap,
                                                         op0=ALU.mult, op1=ALU.add), reads=[rawv, ppt, accv], writes=[accv])

        def load_halo(q, raw, c0, nch, t0, n, SL):
            nsub = n // SL
            S.op("dve", lambda e: e.memset(raw.t[:], 0.0), writes=[raw])
            for s_ in range(nsub):
                a = t0 + s_ * SL
                S.dma("sp", raw.t[:, :, s_, 1:SL + 1], q.px.ap[c0:c0 + nch, :, a:a + SL].rearrange("c p t -> p c t"),
                      reads=q.px.tk(a, a + SL), writes=[raw])
            if q.ctx:
                if t0 > 0:
                    S.dma("sp", raw.t[:, :, 0, 0:1], q.px.ap[c0:c0 + nch, :, t0 - 1:t0].rearrange("c p t -> p c t"),
                          reads=q.px.tk(t0 - 1, t0), writes=[raw])
                if t0 + n < q.L:
                    S.dma("sp", raw.t[:, :, nsub - 1, SL + 1:SL + 2],
                          q.px.ap[c0:c0 + nch, :, t0 + n:t0 + n + 1].rearrange("c p t -> p c t"),
                          reads=q.px.tk(t0 + n, t0 + n + 1), writes=[raw])

        def sc_phase(q, l):
            phase_start()
            T = q.T
            SL = 64 if not q.ctx else T
            nsub = T // SL
            gt = B.take(6 * T, "p (c t) -> p c t", c=6)
            raw = B.take(2 * nsub * (SL + 2), "p (c s t) -> p c s t", c=2, s=nsub)
            acc = B.take(2 * T, "p (c t) -> p c t", c=2)
            o16 = C.take(2 * T, "p (c t) -> p c t", c=2)
            sq2 = [A.take(512) for i in range(2)]
            rstd = A.take(512)
            for tt in range(q.L // T):
                t0 = tt * T
                S.dma("sp", gt.t[:, :, :T], q.px.ap[19:25, :, t0:t0 + T].rearrange("c p t -> p c t"),
                      reads=q.px.tk(t0, t0 + T), writes=[gt])
                S.op("dve", lambda e: e.memset(raw.t[:], 0.0), writes=[raw])
                for c2 in range(2):
                    S.op("dve", lambda e, c2=c2: e.tensor_tensor(
                        out=raw.t[:, c2, :, 1:SL + 1], in0=gt.t[:, 2 + c2, :T].rearrange("p (s t) -> p s t", s=nsub),
                        in1=gt.t[:, 4 + c2, :T].rearrange("p (s t) -> p s t", s=nsub), op=ALU.mult),
                        reads=[gt], writes=[raw])
                    accap = acc.t[:, c2, :T].rearrange("p (s t) -> p s t", s=nsub)
                    conv3(accap, raw.t[:, c2], SL, PP["ccw"][0] + l * 6, 2, c2, acc, raw)
                    S.op("dve", lambda e, c2=c2: e.tensor_tensor(out=acc.t[:, c2, :T], in0=acc.t[:, c2, :T],
                                                                 in1=gt.t[:, c2, :T], op=ALU.mult),
                         reads=[acc, gt], writes=[acc])
                rms([acc.t[:, c2, :T] for c2 in range(2)], T, 256, rstd, sq2, P[7], [acc])
                for c2 in range(2):
                    o = PP["mg"][0] + l * 8 + 6 + c2
                    S.op("dve", lambda e, c2=c2, o=o: e.scalar_tensor_tensor(
                        out=o16.t[:, c2, :T], in0=acc.t[:, c2, :T], scalar=ppt.t[:, o:o + 1], in1=rstd.t[:, :T],
                        op0=ALU.mult, op1=ALU.mult), reads=[acc, ppt, rstd], writes=[o16])
                S.dma("pool", q.ym.ap[6:8, :, t0:t0 + T].rearrange("c p t -> p c t"), o16.t[:, :, :T],
                      reads=[o16], writes=q.ym.tk(t0, t0 + T))

        def hyprep_phase(q, l):
            phase_start()
            n = 128
            SL = 64 if not q.ctx else 128
            nsub = n // SL
            raws = [B.take(6 * nsub * (SL + 2), "p (c s t) -> p c s t", c=6, s=nsub) for i in range(2)]
            acc = B.take(768, "p (c t) -> p c t", c=6)
            ots = [A.take(768) for i in range(2)]
            for ci in range(q.nt):
                t0 = ci * 128
                raw = raws[ci % 2]
                load_halo(q, raw, 0, 6, t0, n, SL)
                for kc in range(6):
                    accap = acc.t[:, kc, :].rearrange("p (s t) -> p s t", s=nsub)
                    conv3(accap, raw.t[:, kc], SL, PP["hcw"][0] + l * 18, 6, kc, acc, raw)
                    o = PP["hcb"][0] + l * 6 + kc
                    S.op("act", lambda e, kc=kc, o=o: e.activation(out=acc.t[:, kc, :], in_=acc.t[:, kc, :],
                                                                   func=AF.Identity, bias=ppt.t[:, o:o + 1], scale=1.0),
                         reads=[acc, ppt], writes=[acc])
                    pt = P[kc // 4]
                    S.op("pe", lambda e, kc=kc, pt=pt: e.transpose(pt.t[:, (kc % 4) * 128:(kc % 4 + 1) * 128],
                                                                    acc.t[:, kc, :], ident),
                         reads=[acc, masks], writes=[pt])
                ot = ots[ci % 2]
                S.op("act", lambda e, ot=ot: e.activation(out=ot.t[:, 0:512], in_=P[0].t[:, 0:512], func=AF.Copy),
                     reads=[P[0]], writes=[ot])
                S.op("act", lambda e, ot=ot: e.activation(out=ot.t[:, 512:768], in_=P[1].t[:, 0:256], func=AF.Copy),
                     reads=[P[1]], writes=[ot])
                S.dma("pool", q.hv.ap[t0:t0 + 128, :], ot.t[:], reads=[ot], writes=q.hv.tk(t0, t0 + 128))

        def filter_phase(l, nm):
            phase_start()
            L = SEQ if nm == "b" else CTX
            T = min(512, L)
            zt2 = [A.take(512, parts=33) for i in range(2)]
            hs = [A.take(512, parts=64) for i in range(3)]
            arg = A.take(512, parts=64); kk = A.take(512, parts=64)
            dec2 = [A.take(256) for i in range(2)]
            hf2 = [A.take(1024) for i in range(2)]
            eo2 = [C.take(1024) for i in range(2)]
            fw4 = B.take(1024, parts=64)
            S.dma("sp", fw4.t[:], din["fw4"][l], writes=[fw4])
            PI_LO = 3.1415925
            for tt in range(L // T):
                t0 = tt * T
                zt = zt2[tt % 2]
                S.dma("sp", zt.t[:, :T], din["zT_" + nm][:, t0:t0 + T], writes=[zt])
                src, K = zt, 33
                for i in range(3):
                    lhsT = fw1.t[:, l, :] if i == 0 else fw23.t[:, l, i - 1, :]
                    wv = fw1 if i == 0 else fw23
                    S.op("pe", lambda e, lhsT=lhsT, src=src, K=K: e.matmul(P[0].t[0:64, :T], lhsT, src.t[0:K, :T],
                                                                          start=True, stop=True),
                         reads=[wv, src], writes=[P[0]])
                    S.op("dve", lambda e, i=i: e.tensor_scalar(out=arg.t[:, :T], in0=P[0].t[0:64, :T],
                                                               scalar1=fpp.t[:, l, i:i + 1], scalar2=fpp.t[:, l, 3:4],
                                                               op0=ALU.add, op1=ALU.mult), reads=[P[0], fpp], writes=[arg])
                    S.op("dve", lambda e: e.tensor_scalar(out=kk.t[:, :T], in0=arg.t[:, :T], scalar1=1.0 / TWO_PI,
                                                          scalar2=MAGIC, op0=ALU.mult, op1=ALU.add),
                         reads=[arg], writes=[kk])
                    S.op("dve", lambda e: e.tensor_scalar(out=kk.t[:, :T], in0=kk.t[:, :T], scalar1=-MAGIC,
                                                          scalar2=-TWO_PI, op0=ALU.add, op1=ALU.mult),
                         reads=[kk], writes=[kk])
                    S.op("dve", lambda e: e.tensor_tensor(out=arg.t[:, :T], in0=arg.t[:, :T], in1=kk.t[:, :T],
                                                          op=ALU.add), reads=[arg, kk], writes=[arg])
                    S.op("dve", lambda e: e.tensor_scalar(out=arg.t[:, :T], in0=arg.t[:, :T], scalar1=-PI_LO,
                                                          scalar2=PI_LO, op0=ALU.max, op1=ALU.min),
                         reads=[arg], writes=[arg])
                    h = hs[i]
                    S.op("act", lambda e, h=h: e.activation(out=h.t[:, :T], in_=arg.t[:, :T], func=AF.Sin),
                         reads=[arg], writes=[h])
                    src, K = h, 64
                for sub in range(T // 128):
                    ti = tt * (T // 128) + sub
                    dec = dec2[ti % 2]; hf = hf2[ti % 2]; eo = eo2[ti % 2]
                    S.dma("sp", dec.t[:], din["dec_" + nm][ti * 128:(ti + 1) * 128, :], writes=[dec])
                    for o in range(2):
                        S.op("pe", lambda e, o=o, sub=sub: e.matmul(P[1 + o].t[:, 0:512], hs[2].t[:, sub * 128:(sub + 1) * 128],
                                                                    fw4.t[:, o * 512:(o + 1) * 512], start=True, stop=True),
                             reads=[hs[2], fw4], writes=[P[1 + o]])
                        S.op("dve", lambda e, o=o, hf=hf, dec=dec: e.tensor_tensor(
                            out=hf.t[:, o * 512:(o + 1) * 512].rearrange("p (d c) -> p d c", d=2),
                            in0=P[1 + o].t[:, 0:512].rearrange("p (d c) -> p d c", d=2),
                            in1=dec.t[:].unsqueeze(1).broadcast_to([128, 2, 256]), op=ALU.mult),
                            reads=[P[1 + o], dec], writes=[hf])
                    if ti == 0:
                        for o in range(2):
                            S.op("dve", lambda e, o=o, hf=hf: e.memset(hf.t[0:1, o * 512 + 256:o * 512 + 512], 0.0),
                                 writes=[hf])
                    hv4 = hf.t[:].rearrange("p (o d c) -> p o d c", o=2, d=2)
                    S.op("dve", lambda e, eo=eo, hv4=hv4: e.tensor_tensor(
                        out=eo.t[:, 0:512].rearrange("p (o c) -> p o c", o=2), in0=hv4[:, :, 0, :], in1=hv4[:, :, 1, :],
                        op=ALU.add), reads=[hf], writes=[eo])
                    S.op("dve", lambda e, eo=eo, hv4=hv4: e.tensor_tensor(
                        out=eo.t[:, 512:1024].rearrange("p (o c) -> p o c", o=2), in0=hv4[:, :, 1, :], in1=hv4[:, :, 0, :],
                        op=ALU.subtract), reads=[hf], writes=[eo])
                    for ri in range(2):
                        S.dma("pool", EO[nm].ap[ri, ti * 128:(ti + 1) * 128, :], eo.t[:, ri * 512:(ri + 1) * 512],
                              reads=[eo], writes=EO[nm].tk())

        def load_g(G, gt, col, nrow):
            S.dma("sp", gt.t[:, 0:nrow, :], G.ap[col, :, 0:nrow, :], writes=[gt])

        def kspec_phase(l, nm):
            phase_start()
            L = SEQ if nm == "b" else CTX
            NT = L // 128
            nfc = (L + 1 + 127) // 128
            rhs = C.take(NT * 512, "p (i c) -> p i c", i=NT)
            gts = [C.take(4224, "p (i v) -> p i v", i=33) for i in range(3)]
            kts = [A.take(512) for i in range(2)]
            wfo = PP["wfb" if nm == "b" else "wfs"][0]
            for ri in range(2):
                S.dma("sp", rhs.t[:], EO[nm].ap[ri].rearrange("(i p) c -> p i c", p=128), reads=EO[nm].tk(), writes=[rhs])
                for fc in range(nfc):
                    gt = gts[fc % 3]
                    load_g(GT_[nm][ri], gt, fc, NT)
                    pk = P[fc % 2]
                    for i in range(NT):
                        S.op("pe", lambda e, i=i, gt=gt, pk=pk: e.matmul(pk.t[:, 0:512], gt.t[:, i, :], rhs.t[:, i, :],
                                                                         start=(i == 0), stop=(i == NT - 1)),
                             reads=[gt, rhs], writes=[pk])
                    kt = kts[fc % 2]
                    S.op("act", lambda e, kt=kt, pk=pk, fc=fc: e.activation(out=kt.t[:], in_=pk.t[:, 0:512], func=AF.Copy,
                                                                            scale=ppt.t[:, wfo + fc:wfo + fc + 1]),
                         reads=[pk, ppt], writes=[kt])
                    S.dma("pool", KS[nm].ap[fc, :, :, ri, :], kt.t[:].rearrange("p (o c) -> p o c", o=2),
                          reads=[kt], writes=KS[nm].tk())

        def conv_phase(q, l, order):
            phase_start()
            NT = q.nt; nfc = q.nfc; nm = q.nm
            zt = B.take(NT * 256, "p (i c) -> p i c", i=NT)
            z16 = C.take(NT * 256, "p (i c) -> p i c", i=NT)
            Y = C.take(nfc * 512, "p (f r c) -> p f r c", f=nfc, r=2)
            gts4 = [C.take(4224, "p (i v) -> p i v", i=33) for i in range(2)]
            gts = gts4
            kt2 = [A.take(512, "p (r c) -> p r c", r=2) for i in range(2)]
            tm = [A.take(256) for i in range(4)]
            gate2 = [A.take(256) for i in range(2)]
            ot2 = [C.take(256) for i in range(2)]
            srcd = q.hv if order == 0 else q.z1
            srcap = q.hv.ap[:, 0:256] if order == 0 else q.z1.ap
            S.dma("sp", zt.t[:], srcap.rearrange("(i p) c -> p i c", p=128), reads=srcd.tk(), writes=[zt])
            S.op("pool", lambda e: e.tensor_copy(out=z16.t[:], in_=zt.t[:]), reads=[zt], writes=[z16])
            Gc, Gs = GT_[nm]
            for fc in range(nfc):
                load_g(Gc, gts[0], fc, NT)
                load_g(Gs, gts[1], fc, NT)
                for i in range(NT):
                    S.op("pe", lambda e, i=i: e.matmul(P[0].t[:, 0:256], gts[0].t[:, i, :], z16.t[:, i, :], start=(i == 0),
                                                       stop=(i == NT - 1)), reads=[gts[0], z16], writes=[P[0]])
                for i in range(NT):
                    S.op("pe", lambda e, i=i: e.matmul(P[1].t[:, 0:256], gts[1].t[:, i, :], z16.t[:, i, :], start=(i == 0),
                                                       stop=(i == NT - 1)), reads=[gts[1], z16], writes=[P[1]])
                kt = kt2[fc % 2]
                S.dma("sp", kt.t[:], KS[nm].ap[fc, :, order, :, :], reads=KS[nm].tk(), writes=[kt])
                pc, psn = P[0].t[:, 0:256], P[1].t[:, 0:256]
                S.op("dve", lambda e, kt=kt: e.tensor_tensor(out=tm[0].t[:], in0=pc, in1=kt.t[:, 0, :], op=ALU.mult),
                     reads=[P[0], kt], writes=[tm[0]])
                S.op("dve", lambda e, kt=kt: e.tensor_tensor(out=tm[1].t[:], in0=psn, in1=kt.t[:, 1, :], op=ALU.mult),
                     reads=[P[1], kt], writes=[tm[1]])
                S.op("dve", lambda e, fc=fc: e.tensor_tensor(out=Y.t[:, fc, 0, :], in0=tm[0].t[:], in1=tm[1].t[:], op=ALU.add),
                     reads=[tm[0], tm[1]], writes=[Y])
                S.op("dve", lambda e, kt=kt: e.tensor_tensor(out=tm[2].t[:], in0=psn, in1=kt.t[:, 0, :], op=ALU.mult),
                     reads=[P[1], kt], writes=[tm[2]])
                S.op("dve", lambda e, kt=kt: e.tensor_tensor(out=tm[3].t[:], in0=pc, in1=kt.t[:, 1, :], op=ALU.mult),
                     reads=[P[0], kt], writes=[tm[3]])
                S.op("dve", lambda e, fc=fc: e.tensor_tensor(out=Y.t[:, fc, 1, :], in0=tm[2].t[:], in1=tm[3].t[:],
                                                             op=ALU.subtract), reads=[tm[2], tm[3]], writes=[Y])
            for tt in range(NT):
                load_g(Gc, gts[0], tt, nfc)
                load_g(Gs, gts[1], tt, nfc)
                py = P[2 + tt % 2]
                for i in range(nfc):
                    S.op("pe", lambda e, i=i, py=py: e.matmul(py.t[:, 0:256], gts[0].t[:, i, :], Y.t[:, i, 0, :],
                                                              start=(i == 0), stop=False), reads=[gts[0], Y], writes=[py])
                for i in range(nfc):
                    S.op("pe", lambda e, i=i, py=py: e.matmul(py.t[:, 0:256], gts[1].t[:, i, :], Y.t[:, i, 1, :],
                                                              start=False, stop=(i == nfc - 1)), reads=[gts[1], Y], writes=[py])
                gate = gate2[tt % 2]
                S.dma("sp", gate.t[:], q.hv.ap[tt * 128:(tt + 1) * 128, 256 * (order + 1):256 * (order + 2)],
                      reads=q.hv.tk(tt * 128, tt * 128 + 128), writes=[gate])
                t_ = tm[tt % 2]
                S.op("dve", lambda e, t_=t_, tt=tt: e.tensor_tensor(out=t_.t[:], in0=zt.t[:, tt, :],
                                                                    in1=ppb.t[:, 256 + order * 256:512 + order * 256],
                                                                    op=ALU.mult), reads=[zt, ppb], writes=[t_])
                S.op("dve", lambda e, t_=t_, py=py: e.tensor_tensor(out=t_.t[:], in0=t_.t[:], in1=py.t[:, 0:256], op=ALU.add),
                     reads=[t_, py], writes=[t_])
                S.op("dve", lambda e, t_=t_, gate=gate: e.tensor_tensor(out=t_.t[:], in0=t_.t[:], in1=gate.t[:], op=ALU.mult),
                     reads=[t_, gate], writes=[t_])
                if order == 0:
                    S.dma("pool", q.z1.ap[tt * 128:(tt + 1) * 128, :], t_.t[:], reads=[t_],
                          writes=q.z1.tk(tt * 128, tt * 128 + 128))
                else:
                    sq = tm[2 + tt % 2]
                    S.op("dve", lambda e, t_=t_, sq=sq: e.tensor_tensor(out=sq.t[:], in0=t_.t[:], in1=t_.t[:], op=ALU.mult),
                         reads=[t_], writes=[sq])
                    S.op("dve", lambda e, sq=sq: e.tensor_reduce(out=small.t[:, 0:1], in_=sq.t[:], op=ALU.add,
                                                                 axis=mybir.AxisListType.X), reads=[sq], writes=[small])
                    S.op("act", lambda e: e.activation(out=small.t[:, 1:2], in_=small.t[:, 0:1], func=AF.Sqrt,
                                                       bias=cst.t[:, 0:1], scale=1.0 / 256), reads=[small, cst], writes=[small])
                    S.op("dve", lambda e: e.reciprocal(out=small.t[:, 2:3], in_=small.t[:, 1:2]), reads=[small], writes=[small])
                    S.op("dve", lambda e, t_=t_: e.scalar_tensor_tensor(out=t_.t[:], in0=t_.t[:], scalar=small.t[:, 2:3],
                                                                        in1=ppb.t[:, 0:256], op0=ALU.mult, op1=ALU.mult),
                         reads=[t_, small, ppb], writes=[t_])
                    for c2 in range(2):
                        S.op("pe", lambda e, c2=c2, t_=t_: e.transpose(P[4].t[:, c2 * 128:(c2 + 1) * 128],
                                                                        t_.t[:, c2 * 128:(c2 + 1) * 128], ident),
                             reads=[t_, masks], writes=[P[4]])
                    ot = ot2[tt % 2]
                    S.op("act", lambda e, ot=ot: e.activation(out=ot.t[:], in_=P[4].t[:, 0:256], func=AF.Copy),
                         reads=[P[4]], writes=[ot])
                    S.dma("pool", q.ym.ap[0:2, :, tt * 128:(tt + 1) * 128].rearrange("c p t -> p c t"),
                          ot.t[:].rearrange("p (c t) -> p c t", c=2), reads=[ot], writes=q.ym.tk(tt * 128, tt * 128 + 128))

        def conv2_phase(qs, l, order):
            phase_start()
            q0 = qs[0]
            NT = q0.nt; nfc = q0.nfc; nm = q0.nm
            z16 = C.take(NT * 512, "p (i s c) -> p i s c", i=NT, s=2)
            Ylast = C.take(1024, "p (r s c) -> p r s c", r=2, s=2)
            gts = [C.take(4224, "p (i v) -> p i v", i=33) for _ in range(2)]
            ot2 = [C.take(512, "p (s c t) -> p s c t", s=2, c=2) for _ in range(2)]
            Ymain = V(AR_B[:, :].bitcast(BF16)[:, 0:(nfc - 1) * 1024].rearrange("p (f r s c) -> p f r s c", f=nfc - 1, r=2, s=2))
            B.o = 16384
            stg = [A.take(512, "p (s c) -> p s c", s=2) for _ in range(2)]
            kt2 = [A.take(512, "p (r c) -> p r c", r=2) for _ in range(2)]
            tm = [A.take(512, "p (s c) -> p s c", s=2) for _ in range(4)]
            gate2 = [A.take(512, "p (s c) -> p s c", s=2) for _ in range(2)]

            def Yv(fc, r):
                if fc < nfc - 1:
                    return Ymain.t[:, fc, r], Ymain
                return Ylast.t[:, r], Ylast

            def srcrows(q, r0):
                if order == 0:
                    return q.hv, q.hv.ap[r0:r0 + 128, 0:256]
                return q.z1, q.z1.ap[r0:r0 + 128, :]
            for i in range(NT):
                st_ = stg[i % 2]
                for s_, q in enumerate(qs):
                    sd, sa = srcrows(q, i * 128)
                    S.dma("sp", st_.t[:, s_, :], sa, reads=sd.tk(i * 128, i * 128 + 128), writes=[st_])
                if i % 2:
                    S.op("act", lambda e, i=i, st_=st_: e.activation(out=z16.t[:, i], in_=st_.t[:], func=AF.Copy),
                         reads=[st_], writes=[z16])
                else:
                    S.op("dve", lambda e, i=i, st_=st_: e.tensor_copy(out=z16.t[:, i], in_=st_.t[:]),
                         reads=[st_], writes=[z16])
            Gc, Gs = GT_[nm]
            for fc in range(nfc):
                load_g(Gc, gts[0], fc, NT)
                load_g(Gs, gts[1], fc, NT)
                for gi in range(2):
                    for i in range(NT):
                        S.op("pe", lambda e, i=i, gi=gi: e.matmul(P[gi].t[:, 0:512], gts[gi].t[:, i, :],
                                                                  z16.t[:, i].rearrange("p s c -> p (s c)"),
                                                                  start=(i == 0), stop=(i == NT - 1)),
                             reads=[gts[gi], z16], writes=[P[gi]])
                kt = kt2[fc % 2]
                S.dma("sp", kt.t[:], KS[nm].ap[fc, :, order, :, :], reads=KS[nm].tk(), writes=[kt])
                pc = P[0].t[:, 0:512].rearrange("p (s c) -> p s c", s=2)
                psn = P[1].t[:, 0:512].rearrange("p (s c) -> p s c", s=2)
                kre = kt.t[:, 0, :].unsqueeze(1).broadcast_to([128, 2, 256])
                kim = kt.t[:, 1, :].unsqueeze(1).broadcast_to([128, 2, 256])
                yre, yrev = Yv(fc, 0)
                yim, yimv = Yv(fc, 1)
                S.op("dve", lambda e, kre=kre: e.tensor_tensor(out=tm[0].t[:], in0=pc, in1=kre, op=ALU.mult),
                     reads=[P[0], kt], writes=[tm[0]])
                S.op("dve", lambda e, kim=kim: e.tensor_tensor(out=tm[1].t[:], in0=psn, in1=kim, op=ALU.mult),
                     reads=[P[1], kt], writes=[tm[1]])
                S.op("dve", lambda e, yre=yre: e.tensor_tensor(out=yre, in0=tm[0].t[:], in1=tm[1].t[:], op=ALU.add),
                     reads=[tm[0], tm[1]], writes=[yrev])
                S.op("dve", lambda e, kre=kre: e.tensor_tensor(out=tm[2].t[:], in0=psn, in1=kre, op=ALU.mult),
                     reads=[P[1], kt], writes=[tm[2]])
                S.op("dve", lambda e, kim=kim: e.tensor_tensor(out=tm[3].t[:], in0=pc, in1=kim, op=ALU.mult),
                     reads=[P[0], kt], writes=[tm[3]])
                S.op("dve", lambda e, yim=yim: e.tensor_tensor(out=yim, in0=tm[2].t[:], in1=tm[3].t[:], op=ALU.subtract),
                     reads=[tm[2], tm[3]], writes=[yimv])
            for tt in range(NT):
                load_g(Gc, gts[0], tt, nfc)
                load_g(Gs, gts[1], tt, nfc)
                py = P[2 + tt % 2]
                r0 = tt * 128
                n_mm = 2 * nfc
                k_ = 0
                for gi in range(2):
                    for i in range(nfc):
                        ya, yv_ = Yv(i, gi)
                        S.op("pe", lambda e, i=i, gi=gi, ya=ya, k_=k_, py=py: e.matmul(
                            py.t[:, 0:512], gts[gi].t[:, i, :], ya.rearrange("p s c -> p (s c)"),
                            start=(k_ == 0), stop=(k_ == n_mm - 1)), reads=[gts[gi], Ymain, Ylast], writes=[py])
                        k_ += 1
                gate = gate2[tt % 2]
                zf = stg[tt % 2]
                for s_, q in enumerate(qs):
                    S.dma("sp", gate.t[:, s_, :], q.hv.ap[r0:r0 + 128, 256 * (order + 1):256 * (order + 2)],
                          reads=q.hv.tk(r0, r0 + 128), writes=[gate])
                    sd, sa = srcrows(q, r0)
                    S.dma("sp", zf.t[:, s_, :], sa, reads=sd.tk(r0, r0 + 128), writes=[zf])
                t_ = tm[tt % 2]
                bb = ppb.t[:, 256 + order * 256:512 + order * 256].unsqueeze(1).broadcast_to([128, 2, 256])
                S.op("dve", lambda e, t_=t_, zf=zf: e.tensor_tensor(out=t_.t[:], in0=zf.t[:], in1=bb, op=ALU.mult),
                     reads=[zf, ppb], writes=[t_])
                S.op("dve", lambda e, t_=t_, py=py: e.tensor_tensor(
                    out=t_.t[:], in0=t_.t[:], in1=py.t[:, 0:512].rearrange("p (s c) -> p s c", s=2), op=ALU.add),
                    reads=[t_, py], writes=[t_])
                S.op("dve", lambda e, t_=t_, gate=gate: e.tensor_tensor(out=t_.t[:], in0=t_.t[:], in1=gate.t[:], op=ALU.mult),
                     reads=[t_, gate], writes=[t_])
                if order == 0:
                    for s_, q in enumerate(qs):
                        S.dma("pool", q.z1.ap[r0:r0 + 128, :], t_.t[:, s_, :], reads=[t_], writes=q.z1.tk(r0, r0 + 128))
                else:
                    sq = tm[2 + tt % 2]
                    S.op("dve", lambda e, t_=t_, sq=sq: e.tensor_tensor(out=sq.t[:], in0=t_.t[:], in1=t_.t[:], op=ALU.mult),
                         reads=[t_], writes=[sq])
                    S.op("dve", lambda e, sq=sq: e.tensor_reduce(out=small.t[:, 0:2], in_=sq.t[:], op=ALU.add,
                                                                 axis=mybir.AxisListType.X), reads=[sq], writes=[small])
                    S.op("act", lambda e: e.activation(out=small.t[:, 2:4], in_=small.t[:, 0:2], func=AF.Sqrt,
                                                       bias=cst.t[:, 0:1], scale=1.0 / 256), reads=[small, cst], writes=[small])
                    S.op("dve", lambda e: e.reciprocal(out=small.t[:, 4:6], in_=small.t[:, 2:4]), reads=[small], writes=[small])
                    S.op("dve", lambda e, t_=t_: e.tensor_tensor(
                        out=t_.t[:], in0=t_.t[:], in1=small.t[:, 4:6].unsqueeze(2).broadcast_to([128, 2, 256]), op=ALU.mult),
                        reads=[t_, small], writes=[t_])
                    S.op("dve", lambda e, t_=t_: e.tensor_tensor(
                        out=t_.t[:], in0=t_.t[:], in1=ppb.t[:, 0:256].unsqueeze(1).broadcast_to([128, 2, 256]), op=ALU.mult),
                        reads=[t_, ppb], writes=[t_])
                    for s_ in range(2):
                        for c2 in range(2):
                            j = s_ * 2 + c2
                            S.op("pe", lambda e, s_=s_, c2=c2, j=j, t_=t_: e.transpose(
                                P[4].t[:, j * 128:(j + 1) * 128], t_.t[:, s_, c2 * 128:(c2 + 1) * 128], ident),
                                reads=[t_, masks], writes=[P[4]])
                    ot = ot2[tt % 2]
                    S.op("act", lambda e, ot=ot: e.activation(out=ot.t[:].rearrange("p s c t -> p (s c t)"),
                                                              in_=P[4].t[:, 0:512], func=AF.Copy), reads=[P[4]], writes=[ot])
                    for s_, q in enumerate(qs):
                        S.dma("pool", q.ym.ap[0:2, :, r0:r0 + 128].rearrange("c p t -> p c t"), ot.t[:, s_],
                              reads=[ot], writes=q.ym.tk(r0, r0 + 128))

        def outproj_phase(q, l):
            phase_start()
            T = q.T
            xts = [B.take(4096, "p (c t) -> p c t", c=8) for i in range(2)]
            yt = B.take(4096, "p (c t) -> p c t", c=8)
            ymt = C.take(4096, "p (c t) -> p c t", c=8)
            sq2 = [A.take(512) for i in range(2)]
            rstd = A.take(512)
            wod = W16["wo"]
            for tt in range(q.L // T):
                t0 = tt * T
                xt = xts[tt % 2]
                S.dma("sp", xt.t[:, :, :T], q.xs.ap[:, :, t0:t0 + T].rearrange("c p t -> p c t"),
                      reads=q.xs.tk(t0, t0 + T), writes=[xt])
                S.dma("sp", ymt.t[:, :, :T], q.ym.ap[:, :, t0:t0 + T].rearrange("c p t -> p c t"),
                      reads=q.ym.tk(t0, t0 + T), writes=[ymt])
                for oc in range(8):
                    w = wnext(); S.dma("sp", w.t[:], wod.ap[l, oc], writes=[w])
                    py = P[oc % 2]
                    for kc in range(8):
                        S.op("pe", lambda e, kc=kc, w=w, py=py: e.matmul(py.t[:, :T], w.t[:, kc, :], ymt.t[:, kc, :T],
                                                                         start=(kc == 0), stop=(kc == 7)),
                             reads=[w, ymt], writes=[py])
                    S.op("act", lambda e, oc=oc, py=py: e.activation(out=yt.t[:, oc, :T], in_=py.t[:, :T], func=AF.Copy),
                         reads=[py], writes=[yt])
                resid_store(q, 1, t0, T, xt, yt, rstd, sq2, P[7], None)

        def ssd_phase(q, l, want_out):
            phase_start()
            NT = q.nt
            SL = 64 if not q.ctx else 128
            nsub = 128 // SL
            b = q.bl
            raws = [B.take(8 * nsub * (SL + 2), "p (c s t) -> p c s t", c=8, s=nsub) for _ in range(2)]
            acc = B.take(1024, "p (c s t) -> p c s t", c=8, s=nsub)
            ct0 = B.take(1024, "p (c s t) -> p c s t", c=8, s=nsub)
            ct2 = B.take(1024, "p (c s t) -> p c s t", c=8, s=nsub)
            xbcs = [B.take(1024, "p (c t) -> p c t", c=8) for _ in range(3)]
            ltall = B.take(1024, "p (h t) -> p h t", h=8)
            ehall = B.take(1024, "p (h t) -> p h t", h=8)
            mhall = B.take(1024, "p (h t) -> p h t", h=8)
            dtrs = [A.take(128, parts=16) for _ in range(2)]
            dtfs = [A.take(128, parts=16) for _ in range(2)]
            xs_tms = [A.take(512) for _ in range(3)]
            bdts = [A.take(272) for _ in range(3)]
            a_tms = [A.take(16) for _ in range(3)]
            xdt = A.take(512); xd = A.take(512)
            gm = A.take(256, "p (g t) -> p g t", g=2)
            ecs = A.take(16); H = A.take(512, "p (g t) -> p g t", g=2); ysb = A.take(512); tmp = A.take(512)
            yfls = [A.take(512) for _ in range(3)]
            zts = [A.take(512, "p (c t) -> p c t", c=4) for _ in range(3)]
            zsg = A.take(512, "p (c t) -> p c t", c=4)
            yfm = A.take(512, "p (c t) -> p c t", c=4)
            rstd = A.take(128); sq2 = [A.take(128) for _ in range(2)]
            y16 = C.take(512, "p (c t) -> p c t", c=4)
            scwo = PP["scw"][0] + l * 24
            scbo = PP["scb"][0] + l * 8
            sdo = PP["sd"][0] + l * 8

            def wb(j):
                return ppt.t[:, scwo + j * 8:scwo + j * 8 + 8].unsqueeze(2).unsqueeze(3).broadcast_to([128, 8, nsub, SL])

            def silu_to(out_ap, x_ap, sg, xv, sgv, outv):
                S.op("act", lambda e: e.activation(out=sg, in_=x_ap, func=AF.Exp, scale=-1.0), reads=[xv], writes=[sgv])
                S.op("act", lambda e: e.activation(out=sg, in_=sg, func=AF.Ln, bias=cst.t[:, 1:2], scale=1.0),
                     reads=[sgv, cst], writes=[sgv])
                S.op("act", lambda e: e.activation(out=sg, in_=sg, func=AF.Exp, scale=-1.0), reads=[sgv], writes=[sgv])
                S.op("dve", lambda e: e.tensor_tensor(out=out_ap, in0=x_ap, in1=sg, op=ALU.mult),
                     reads=[xv, sgv], writes=[outv])

            def prep_load(ci, pos, d):
                t0 = ci * 128
                k = pos % 2; k3 = pos % 3
                raw = raws[k]; dtr = dtrs[k]
                if d == 1:
                    ck = q.cache.toks[ci:ci + 1]
                    S.dma("sp", xbcs[k3].t[:].rearrange("p c t -> p (c t)"), q.cache.ap[ci, :, 0:1024], reads=ck, writes=[xbcs[k3]])
                    S.dma("sp", xs_tms[k3].t[:], q.cache.ap[ci, :, 1024:1536], reads=ck, writes=[xs_tms[k3]])
                    S.dma("sp", bdts[k3].t[:], q.cache.ap[ci, :, 1536:1808], reads=ck, writes=[bdts[k3]])
                    if want_out:
                        S.dma("sp", zts[k3].t[:], q.px.ap[6:10, :, t0:t0 + 128].rearrange("c p t -> p c t"),
                              reads=q.px.tk(t0, t0 + 128), writes=[zts[k3]])
                        S.dma("sp", yfls[k3].t[:], q.yf.ap[t0:t0 + 128, :], reads=q.yf.tk(t0, t0 + 128), writes=[yfls[k3]])
                    return
                if q.ctx:
                    S.op("dve", lambda e: e.memset(raw.t[:], 0.0), writes=[raw])
                for s_ in range(nsub):
                    a0 = t0 + s_ * SL
                    S.dma("sp", raw.t[:, :, s_, 1:SL + 1], q.px.ap[10:18, :, a0:a0 + SL].rearrange("c p t -> p c t"),
                          reads=q.px.tk(a0, a0 + SL), writes=[raw])
                if q.ctx:
                    if t0 > 0:
                        S.dma("sp", raw.t[:, :, 0, 0:1], q.px.ap[10:18, :, t0 - 1:t0].rearrange("c p t -> p c t"),
                              reads=q.px.tk(t0 - 1, t0), writes=[raw])
                    if t0 + 128 < q.L:
                        S.dma("sp", raw.t[:, :, nsub - 1, SL + 1:SL + 2],
                              q.px.ap[10:18, :, t0 + 128:t0 + 129].rearrange("c p t -> p c t"),
                              reads=q.px.tk(t0 + 128, t0 + 129), writes=[raw])
                S.dma("sp", dtr.t[:], q.px.ap[18, 0:16, t0:t0 + 128], reads=q.px.tk(t0, t0 + 128), writes=[dtr])

            def prep(ci, pos, d):
                k = pos % 2; k3 = pos % 3
                raw = raws[k]; xbc = xbcs[k3]; dtr = dtrs[k]; dtf = dtfs[k]; xs_tm = xs_tms[k3]; bdt = bdts[k3]
                a_tm = a_tms[k3]
                if d == 1:
                    S.op("dve", lambda e: e.tensor_tensor(out=a_tm.t[:], in0=bdt.t[:, 256:272], in1=Aneg.t[:], op=ALU.mult),
                         reads=[bdt, Aneg], writes=[a_tm])
                    yield
                    if want_out:
                        silu_to(zts[k3].t[:], zts[k3].t[:], zsg.t[:], zts[k3], zsg, zts[k3])
                    return
                S.op("dve", lambda e: e.tensor_tensor(out=ct0.t[:], in0=raw.t[:, :, :, 0:SL], in1=wb(0), op=ALU.mult),
                     reads=[raw, ppt], writes=[ct0])
                S.op("dve", lambda e: e.tensor_tensor(out=ct2.t[:], in0=raw.t[:, :, :, 2:SL + 2], in1=wb(2), op=ALU.mult),
                     reads=[raw, ppt], writes=[ct2])
                S.op("dve", lambda e: e.tensor_tensor(out=acc.t[:], in0=raw.t[:, :, :, 1:SL + 1], in1=wb(1), op=ALU.mult),
                     reads=[raw, ppt], writes=[acc])
                S.op("dve", lambda e: e.tensor_tensor(out=acc.t[:], in0=acc.t[:], in1=ct0.t[:], op=ALU.add),
                     reads=[acc, ct0], writes=[acc])
                S.op("dve", lambda e: e.tensor_tensor(out=acc.t[:], in0=acc.t[:], in1=ct2.t[:], op=ALU.add),
                     reads=[acc, ct2], writes=[acc])
                S.op("dve", lambda e: e.tensor_tensor(
                    out=acc.t[:], in0=acc.t[:],
                    in1=ppt.t[:, scbo:scbo + 8].unsqueeze(2).unsqueeze(3).broadcast_to([128, 8, nsub, SL]), op=ALU.add),
                    reads=[acc, ppt], writes=[acc])
                yield
                silu_to(xbc.t[:].rearrange("p c (s t) -> p c s t", s=nsub), acc.t[:], ct0.t[:], acc, ct0, xbc)
                S.op("act", lambda e: e.activation(out=dtr.t[:], in_=dtr.t[:], func=AF.Exp, bias=dtb.t[:, l:l + 1],
                                                   scale=1.0), reads=[dtr, dtb], writes=[dtr])
                S.op("act", lambda e: e.activation(out=dtf.t[:], in_=dtr.t[:], func=AF.Ln, bias=cst.t[0:16, 1:2],
                                                   scale=1.0), reads=[dtr, cst], writes=[dtf])
                yield
                for kc in range(4):
                    S.op("pe", lambda e, kc=kc: e.transpose(P[0].t[:, kc * 128:(kc + 1) * 128], xbc.t[:, kc, :], ident),
                         reads=[xbc, masks], writes=[P[0]])
                S.op("act", lambda e: e.activation(out=xs_tm.t[:], in_=P[0].t[:, 0:512], func=AF.Copy),
                     reads=[P[0]], writes=[xs_tm])
                for g in range(2):
                    S.op("pe", lambda e, g=g: e.transpose(P[1].t[:, g * 128:(g + 1) * 128], xbc.t[:, 4 + g, :], ident),
                         reads=[xbc, masks], writes=[P[1]])
                S.op("pe", lambda e: e.transpose(P[1].t[:, 256:272], dtf.t[:], masks.t[0:16, 4, 0:16]),
                     reads=[dtf, masks], writes=[P[1]])
                S.op("act", lambda e: e.activation(out=bdt.t[:], in_=P[1].t[:, 0:272], func=AF.Copy),
                     reads=[P[1]], writes=[bdt])
                S.op("dve", lambda e: e.tensor_tensor(out=a_tm.t[:], in0=bdt.t[:, 256:272], in1=Aneg.t[:], op=ALU.mult),
                     reads=[bdt, Aneg], writes=[a_tm])
                ck = q.cache.toks[ci:ci + 1]
                S.dma("pool", q.cache.ap[ci, :, 0:1024], xbc.t[:].rearrange("p c t -> p (c t)"), reads=[xbc], writes=ck)
                S.dma("pool", q.cache.ap[ci, :, 1024:1536], xs_tm.t[:], reads=[xs_tm], writes=ck)
                S.dma("pool", q.cache.ap[ci, :, 1536:1808], bdt.t[:], reads=[bdt], writes=ck)

            def main(ci, pos, d):
                t0 = ci * 128
                k = pos % 2; k3 = pos % 3
                xbc = xbcs[k3]; xs_tm = xs_tms[k3]; bdt = bdts[k3]; a_tm = a_tms[k3]
                mA = masks.t[:, 0 if d == 0 else 2, :]
                mB = masks.t[:, 1 if d == 0 else 3, :]
                col = 127 if d == 0 else 0
                a = a_tm.t[:, 8 * d:8 * d + 8]
                S.op("dve", lambda e: e.tensor_tensor(
                    out=xdt.t[:].rearrange("p (h c) -> p h c", h=8), in0=xs_tm.t[:].rearrange("p (h c) -> p h c", h=8),
                    in1=bdt.t[:, 256 + 8 * d:264 + 8 * d].unsqueeze(2).broadcast_to([128, 8, 64]), op=ALU.mult),
                    reads=[xs_tm, bdt], writes=[xdt])
                if want_out:
                    for g in range(2):
                        S.op("pe", lambda e, g=g: e.matmul(P[2].t[:, g * 128:(g + 1) * 128], xbc.t[:, 4 + g, :],
                                                           xbc.t[:, 6 + g, :], start=True, stop=True),
                             reads=[xbc], writes=[P[2]])
                    S.op("dve", lambda e: e.tensor_tensor(
                        out=gm.t[:], in0=P[2].t[:, 0:256].rearrange("p (g t) -> p g t", g=2),
                        in1=mB.unsqueeze(1).broadcast_to([128, 2, 128]), op=ALU.mult), reads=[P[2], masks], writes=[gm])
                S.op("pe", lambda e: e.matmul(P[3].t[:, 0:8], mB, a, start=True, stop=True),
                     reads=[masks, a_tm], writes=[P[3]])
                S.op("pe", lambda e: e.matmul(P[3].t[:, 8:16], ones.t[:], a, start=True, stop=True),
                     reads=[ones, a_tm], writes=[P[3]])
                S.op("act", lambda e: e.activation(out=ecs.t[:], in_=P[3].t[:, 0:16], func=AF.Exp),
                     reads=[P[3]], writes=[ecs])
                S.op("dve", lambda e: e.tensor_tensor(
                    out=ltall.t[:], in0=mA.unsqueeze(1).broadcast_to([128, 8, 128]),
                    in1=a.unsqueeze(2).broadcast_to([128, 8, 128]), op=ALU.mult), reads=[masks, a_tm], writes=[ltall])
                for h in range(8):
                    pseg = P[4 + h // 4]
                    S.op("pe", lambda e, h=h, pseg=pseg: e.matmul(pseg.t[:, (h % 4) * 128:(h % 4 + 1) * 128], ltall.t[:, h, :],
                                                                  mB, start=True, stop=True),
                         reads=[ltall, masks], writes=[pseg])
                yield
                for hh in range(2):
                    S.op("act", lambda e, hh=hh: e.activation(
                        out=ehall.t[:, 4 * hh:4 * hh + 4, :], in_=P[4 + hh].t[:, 0:512].rearrange("p (h t) -> p h t", h=4),
                        func=AF.Exp), reads=[P[4 + hh]], writes=[ehall])
                if want_out:
                    for g in range(2):
                        S.op("dve", lambda e, g=g: e.tensor_tensor(
                            out=mhall.t[:, 4 * g:4 * g + 4, :], in0=ehall.t[:, 4 * g:4 * g + 4, :],
                            in1=gm.t[:, g, :].unsqueeze(1).broadcast_to([128, 4, 128]), op=ALU.mult),
                            reads=[ehall, gm], writes=[mhall])
                    for h in range(8):
                        S.op("pe", lambda e, h=h: e.matmul(P[6].t[:, h * 64:(h + 1) * 64], mhall.t[:, h, :],
                                                           xdt.t[:, h * 64:(h + 1) * 64], start=True, stop=True),
                             reads=[mhall, xdt], writes=[P[6]])
                S.op("dve", lambda e: e.tensor_tensor(
                    out=xd.t[:].rearrange("p (h c) -> p h c", h=8), in0=xdt.t[:].rearrange("p (h c) -> p h c", h=8),
                    in1=ehall.t[:, :, col:col + 1].broadcast_to([128, 8, 64]), op=ALU.mult),
                    reads=[ehall, xdt], writes=[xd])
                yield
                if want_out:
                    for g in range(2):
                        S.op("pe", lambda e, g=g: e.matmul(P[7].t[:, g * 256:(g + 1) * 256], xbc.t[:, 6 + g, :], H.t[:, g, :],
                                                           start=True, stop=True), reads=[xbc, H], writes=[P[7]])
                    S.op("dve", lambda e: e.tensor_tensor(
                        out=tmp.t[:].rearrange("p (h c) -> p h c", h=8), in0=P[7].t[:, 0:512].rearrange("p (h c) -> p h c", h=8),
                        in1=ecs.t[:, 0:8].unsqueeze(2).broadcast_to([128, 8, 64]), op=ALU.mult),
                        reads=[P[7], ecs], writes=[tmp])
                    S.op("dve", lambda e: e.tensor_tensor(out=ysb.t[:], in0=tmp.t[:], in1=P[6].t[:, 0:512], op=ALU.add),
                         reads=[tmp, P[6]], writes=[ysb])
                    if d == 0:
                        S.op("dve", lambda e: e.tensor_tensor(
                            out=tmp.t[:].rearrange("p (h c) -> p h c", h=8), in0=xs_tm.t[:].rearrange("p (h c) -> p h c", h=8),
                            in1=ppt.t[:, sdo:sdo + 8].unsqueeze(2).broadcast_to([128, 8, 64]), op=ALU.mult),
                            reads=[xs_tm, ppt], writes=[tmp])
                        S.op("dve", lambda e: e.tensor_tensor(out=ysb.t[:], in0=ysb.t[:], in1=tmp.t[:], op=ALU.add),
                             reads=[ysb, tmp], writes=[ysb])
                        S.dma("pool", q.yf.ap[t0:t0 + 128, :], ysb.t[:], reads=[ysb], writes=q.yf.tk(t0, t0 + 128))
                    else:
                        S.op("dve", lambda e: e.tensor_tensor(out=ysb.t[:], in0=ysb.t[:], in1=yfls[k3].t[:], op=ALU.add),
                             reads=[ysb, yfls[k3]], writes=[ysb])
                for g in range(2):
                    S.op("pe", lambda e, g=g: e.matmul(P[7].t[:, g * 256:(g + 1) * 256], bdt.t[:, g * 128:(g + 1) * 128],
                                                       xd.t[:, g * 256:(g + 1) * 256], start=True, stop=True),
                         reads=[bdt, xd], writes=[P[7]])
                S.op("dve", lambda e: e.tensor_tensor(
                    out=H.t[:].rearrange("p g (h c) -> p (g h) c", h=4), in0=H.t[:].rearrange("p g (h c) -> p (g h) c", h=4),
                    in1=ecs.t[:, 8:16].unsqueeze(2).broadcast_to([128, 8, 64]), op=ALU.mult),
                    reads=[H, ecs], writes=[H])
                S.op("dve", lambda e: e.tensor_tensor(out=H.t[:].rearrange("p g t -> p (g t)"),
                                                      in0=H.t[:].rearrange("p g t -> p (g t)"),
                                                      in1=P[7].t[:, 0:512], op=ALU.add), reads=[H, P[7]], writes=[H])
                yield
                if want_out and d == 1:
                    for kc in range(4):
                        S.op("pe", lambda e, kc=kc: e.transpose(P[6].t[:, kc * 128:(kc + 1) * 128],
                                                                ysb.t[:, kc * 128:(kc + 1) * 128], ident),
                             reads=[ysb, masks], writes=[P[6]])
                    S.op("dve", lambda e: e.tensor_tensor(out=yfm.t[:], in0=P[6].t[:, 0:512].rearrange("p (c t) -> p c t", c=4),
                                                          in1=zts[k3].t[:], op=ALU.mult), reads=[P[6], zts[k3]], writes=[yfm])
                    rms([yfm.t[:, kc, :] for kc in range(4)], 128, 512, rstd, sq2, P[2], [yfm], lnexp=True)
                    S.op("dve", lambda e: e.tensor_tensor(
                        out=yfm.t[:], in0=yfm.t[:], in1=rstd.t[:, 0:128].unsqueeze(1).broadcast_to([128, 4, 128]),
                        op=ALU.mult), reads=[yfm, rstd], writes=[yfm])
                    mgo = PP["mg"][0] + l * 8 + 2
                    S.op("dve", lambda e: e.tensor_tensor(
                        out=y16.t[:], in0=yfm.t[:], in1=ppt.t[:, mgo:mgo + 4].unsqueeze(2).broadcast_to([128, 4, 128]),
                        op=ALU.mult), reads=[yfm, ppt], writes=[y16])
                    S.dma("pool", q.ym.ap[2:6, :, t0:t0 + 128].rearrange("c p t -> p c t"), y16.t[:],
                          reads=[y16], writes=q.ym.tk(t0, t0 + 128))

            for d in range(2):
                if q.ctx:
                    S.op("dve", lambda e: e.memset(H.t[:], 0.0), writes=[H])
                else:
                    S.dma("sp", H.t[:].rearrange("p g t -> p (g t)"), HSTD.ap[b, d], reads=HSTD.tk(), writes=[H])
                order = list(range(NT)) if d == 0 else list(range(NT - 1, -1, -1))
                if d == 0 and not q.ctx:
                    for k in range(2):
                        S.op("dve", lambda e, k=k: e.memset(raws[k].t[:], 0.0), writes=[raws[k]])
                prep_load(order[0], 0, d)
                if len(order) > 1:
                    prep_load(order[1], 1, d)
                for _ in prep(order[0], 0, d):
                    pass
                for idx, ci in enumerate(order):
                    if idx + 2 < len(order):
                        prep_load(order[idx + 2], idx + 2, d)
                    gm_ = main(ci, idx, d)
                    gp_ = prep(order[idx + 1], idx + 1, d) if idx + 1 < len(order) else iter(())
                    dm = dp = False
                    while not (dm and dp):
                        if not dm:
                            try:
                                next(gm_)
                            except StopIteration:
                                dm = True
                        if not dp:
                            try:
                                next(gp_)
                            except StopIteration:
                                dp = True
                if q.ctx:
                    S.dma("pool", HSTD.ap[b, d], H.t[:].rearrange("p g t -> p (g t)"), reads=[H], writes=HSTD.tk())

        def mixer_pair(qs, l, last):
            want = not (qs[0].ctx and last)
            for q in qs:
                proj_phase(q, l)
                ssd_phase(q, l, want)
                if want:
                    sc_phase(q, l)
                    hyprep_phase(q, l)
            if want:
                conv2_phase(qs, l, 0)
                conv2_phase(qs, l, 1)

        convert_weights()
        for l in range(depth):
            last = (l == depth - 1)
            mod_phase(l)
            S.dma("sp", ppb.t[:], din["ppb"][l], writes=[ppb])
            for q in seqs:
                ffn_phase(q, l, 0)
            if dbg == "ffn1":
                break
            filter_phase(l, "b"); kspec_phase(l, "b")
            if not last:
                filter_phase(l, "s"); kspec_phase(l, "s")
            mixer_pair([q for q in seqs if q.ctx], l, last)
            mixer_pair([q for q in seqs if not q.ctx], l, last)
            if dbg in ("mix", "proj"):
                break
            for q in seqs:
                if q.ctx and last:
                    continue
                outproj_phase(q, l)
                if dbg == "outp":
                    continue
                ffn_phase(q, l, 1)
        S.barrier()
        B.o = 0
        ot = [B.take(4096, "p (c t) -> p c t", c=8) for i in range(2)]
        i = 0
        for q in seqs:
            if q.ctx:
                continue
            for tt in range(q.L // 512):
                o = ot[i % 2]; i += 1
                srcd = q.xs
                S.dma("sp", o.t[:], srcd.ap[:, :, tt * 512:(tt + 1) * 512].rearrange("c p t -> p c t"),
                      reads=srcd.tk(tt * 512, tt * 512 + 512), writes=[o])
                S.dma("pool", out[q.bl, :, :, tt * 512:(tt + 1) * 512].rearrange("c p t -> p c t"), o.t[:], reads=[o])
        S.barrier()
        print("instructions:", S.nins)
    return nc


def _prep(inputs):
    sh = _host_shared(inputs)
    x = np.asarray(inputs["x"], np.float32)
    ctx = np.asarray(inputs["ctx"], np.float32)
    c = np.asarray(inputs["c"], np.float32)
    cc = np.asarray(inputs["c_ctx"], np.float32)
    maps = []
    for core in range(NCORE):
        m = dict(sh)
        b0 = core * BPC
        m["xl"] = np.ascontiguousarray(x[b0:b0 + BPC].transpose(0, 2, 1)).reshape(BPC, 8, 128, SEQ)
        m["xc"] = np.ascontiguousarray(ctx[b0:b0 + BPC].transpose(0, 2, 1)).reshape(BPC, 8, 128, CTX)
        cT = np.stack([c[b0], c[b0 + 1], cc], axis=-1)
        m["cT"] = np.ascontiguousarray(cT.reshape(8, 128, 3).transpose(1, 0, 2))
        maps.append(m)
    return maps


def kernel(**inputs):
    maps = _prep(inputs)
    shapes = {k: v.shape for k, v in maps[0].items()}
    nc = build(DEPTH, shapes)
    res = run_bass_kernel_spmd(nc, maps, core_ids=list(range(NCORE)))
    outs = []
    for core in range(NCORE):
        o = res.results[core]["out"].reshape(BPC, D, SEQ)
        outs.append(o.transpose(0, 2, 1))
    return np.ascontiguousarray(np.concatenate(outs, axis=0)).astype(np.float32)
```

```python
import math
import contextlib
import numpy as np
import ml_dtypes
import concourse.bass as bass
import concourse.mybir as mybir
from concourse.bass_utils import run_bass_kernel_spmd

F32 = mybir.dt.float32
BF16 = mybir.dt.bfloat16
AF = mybir.ActivationFunctionType
ALU = mybir.AluOpType

D = 1024; DEPTH = 4; BATCH = 16; SEQ = 4096; CTX = 256; DFF = 2816; NMOD = 9
NCORE = 8; BPC = BATCH // NCORE
EPS = 1e-6
HYW = 256; SSDW = 512; NH = 8; HD = 64; NG = 2; NS = 128
NPXC = 25
NDS = 8
TWO_PI = 2.0 * math.pi
MAGIC = 12582912.0


class Tok:
    __slots__ = ("w", "r")

    def __init__(self):
        self.w = None
        self.r = {}


class V:
    def __init__(self, t, tok=None):
        self.t = t
        self.tok = tok or Tok()

    def __getitem__(self, k):
        return self.t[k]


class Sched:
    def __init__(self, nc, st):
        self.nc = nc
        self.st = st
        self.E = {"pe": nc.tensor, "act": nc.scalar, "dve": nc.vector, "pool": nc.gpsimd, "sp": nc.sync}
        self.sem = {}
        self.cnt = {}
        for k in ["pe", "act", "dve", "pool"]:
            self.sem["c_" + k] = st.enter_context(nc.semaphore("c_" + k))
            self.cnt["c_" + k] = 0
        self.drr = {"sp": 0, "pool": 0}
        for q in ["sp", "pool"]:
            for i in range(NDS):
                key = "d_%s%d" % (q, i)
                self.sem[key] = st.enter_context(nc.semaphore(key))
                self.cnt[key] = 0
        self.seen = {e: {} for e in self.E}
        self.nins = 0

    def _wait(self, eng, key, val):
        if eng == "pe" and key == "c_pe":
            return
        if self.seen[eng].get(key, 0) >= val:
            return
        self.E[eng].wait_ge(self.sem[key], val)
        self.seen[eng][key] = val

    def _deps(self, eng, reads, writes):
        for t in reads:
            if t.w:
                self._wait(eng, *t.w)
        for t in writes:
            if t.w:
                self._wait(eng, *t.w)
            for k, v in t.r.items():
                self._wait(eng, k, v)

    def _mark(self, me, reads, writes):
        for t in reads:
            t.r[me[0]] = me[1]
        for t in writes:
            t.w = me
            t.r = {}

    def op(self, eng, fn, reads=(), writes=()):
        reads = [x.tok if isinstance(x, V) else x for x in reads]
        writes = [x.tok if isinstance(x, V) else x for x in writes]
        self._deps(eng, reads, writes)
        ins = fn(self.E[eng])
        key = "c_" + eng
        self.cnt[key] += 1
        ins.then_inc(self.sem[key], 1)
        self._mark((key, self.cnt[key]), reads, writes)
        self.nins += 1

    def dma(self, q, out, in_, reads=(), writes=()):
        reads = [x.tok if isinstance(x, V) else x for x in reads]
        writes = [x.tok if isinstance(x, V) else x for x in writes]
        i = self.drr[q]
        self.drr[q] = (i + 1) % NDS
        key = "d_%s%d" % (q, i)
        if self.cnt[key] > 0:
            self._wait(q, key, self.cnt[key])
        self._deps(q, reads, writes)
        ins = self.E[q].dma_start(out=out, in_=in_)
        self.cnt[key] += 16
        ins.then_inc(self.sem[key], 16)
        self._mark((key, self.cnt[key]), reads, writes)
        self.nins += 1

    def barrier(self):
        for e in self.E:
            for k, v in self.cnt.items():
                if v > 0:
                    self._wait(e, k, v)


class DT:
    def __init__(self, ap, n):
        self.ap = ap
        self.toks = [Tok() for _ in range(max(1, n))]

    def tk(self, t0=None, t1=None):
        if t0 is None:
            return self.toks
        return self.toks[t0 // 128:(t1 + 127) // 128]


def _consts():
    f32 = np.float32
    c = {}
    r = np.arange(128)
    m = np.zeros((128, 5, 128), f32)
    m[:, 0] = (r[:, None] > r[None, :])
    m[:, 1] = (r[:, None] <= r[None, :])
    m[:, 2] = (r[:, None] < r[None, :])
    m[:, 3] = (r[:, None] >= r[None, :])
    m[:, 4] = np.eye(128)
    c["masks"] = m
    for nm, L in (("b", SEQ), ("s", CTX)):
        N = 2 * L
        nfc = (L + 1 + 127) // 128
        P = nfc * 128
        u = np.arange(P, dtype=np.int64)
        ph = (u[:, None] * u[None, :]) % N
        ang = ph.astype(np.float64) * (2.0 * math.pi / N)
        valid = ((u[:, None] <= L) & (u[None, :] <= L))
        def lay(g):
            g = g.astype(f32).astype(ml_dtypes.bfloat16).reshape(nfc, 128, nfc, 128)
            return np.ascontiguousarray(g.transpose(2, 1, 0, 3))
        c["gc_" + nm] = lay(np.where(valid, np.cos(ang), 0.0))
        c["gs_" + nm] = lay(np.where(valid, np.sin(ang), 0.0))
        wf = np.zeros(P, np.float64)
        wf[:L + 1] = 2.0 / N
        wf[0] = 1.0 / N
        wf[L] = 1.0 / N
        c["wf_" + nm] = np.ascontiguousarray(wf.reshape(nfc, 128).T).astype(f32)
        t = np.linspace(0.0, 1.0, L, dtype=f32)[:, None]
        bands = 16
        w = (2.0 * math.pi * np.arange(L, dtype=f32)[:, None] / L).astype(f32)
        f = np.linspace(1e-4, bands - 1, bands, dtype=f32)[None, :]
        z = np.concatenate([t, np.cos(f * w), -np.sin(f * w)], axis=-1).astype(f32)
        c["zT_" + nm] = np.ascontiguousarray(z.T)
        deltas = np.linspace(math.log(1e-2) / 1.5, math.log(1e-2) / 0.3, HYW, dtype=f32)
        c["dec_" + nm] = np.exp(-t * np.abs(deltas)).astype(f32)
    return c


def _fm(v):
    v = np.asarray(v, np.float32)
    n = v.shape[-1] // 128
    v = v.reshape(v.shape[:-1] + (n, 128))
    return np.ascontiguousarray(np.moveaxis(v, -1, 0))


def _wl(w, kc):
    K, M = w.shape
    return np.ascontiguousarray(w.reshape(kc, 128, M // 128, 128).transpose(2, 1, 0, 3))


PP = {}


def _pp_layout():
    off = 0
    for nm, n in (("ng", DEPTH * 6 * 8), ("bm", DEPTH * 72), ("hcw", DEPTH * 3 * 6), ("hcb", DEPTH * 6),
                  ("scw", DEPTH * 3 * 8), ("scb", DEPTH * 8), ("ccw", DEPTH * 3 * 2), ("mg", DEPTH * 8),
                  ("alog", DEPTH * 16), ("sd", DEPTH * 8),
                  ("wfb", 33), ("wfs", 3)):
        PP[nm] = (off, n)
        off += n
    return off


NPP = _pp_layout()


def _host_shared(inp):
    f32 = np.float32
    sh = {}
    sh.update(_consts())
    pp = np.zeros((128, NPP), f32)

    def put(nm, a):
        o, n = PP[nm]
        pp[:, o:o + n] = np.asarray(a, f32).reshape(128, n)
    put("ng", _fm(inp["norm_g"]))
    put("bm", _fm(inp["b_mod"]))
    put("hcw", _fm(inp["hy_conv_w"]))
    put("hcb", _fm(inp["hy_conv_b"]))
    put("scw", _fm(inp["ssd_conv_w"]))
    put("scb", _fm(inp["ssd_conv_b"]))
    put("ccw", _fm(inp["sc_conv_w"]))
    put("mg", _fm(inp["mix_gain"]))
    sh["ppb"] = np.ascontiguousarray(np.concatenate([
        np.broadcast_to(np.asarray(inp["mix_gain"], f32)[:, None, :HYW], (DEPTH, 128, HYW)),
        np.broadcast_to(np.asarray(inp["hy_bias"], f32).reshape(DEPTH, 1, 512), (DEPTH, 128, 512))], axis=2))
    put("alog", np.broadcast_to(np.asarray(inp["ssd_a_log"]).reshape(1, DEPTH, 16), (128, DEPTH, 16)))
    put("sd", np.broadcast_to(np.asarray(inp["ssd_d"]).reshape(1, DEPTH, 8), (128, DEPTH, 8)))
    put("wfb", sh.pop("wf_b"))
    put("wfs", sh.pop("wf_s"))
    sh["pp"] = pp
    sh["wm"] = np.stack([_wl(np.asarray(inp["w_mod"][l], f32), 8) for l in range(DEPTH)])
    sh["fwi"] = np.stack([np.stack([_wl(np.asarray(inp["ffn_w_in"][l, j], f32), 8) for j in range(2)])
                          for l in range(DEPTH)])
    sh["fwo"] = np.stack([np.stack([_wl(np.asarray(inp["ffn_w_out"][l, j], f32), 22) for j in range(2)])
                          for l in range(DEPTH)])
    wi = np.asarray(inp["w_in"], f32)
    wip = np.zeros((DEPTH, D, NPXC * 128), f32)
    wip[:, :, 0:2304] = wi[:, :, 0:2304]
    wip[:, :, 2304:2320] = wi[:, :, 2304:2320]
    wip[:, :, 2432:3200] = wi[:, :, 2320:3088]
    sh["wi"] = np.stack([_wl(wip[l], 8) for l in range(DEPTH)])
    sh["wo"] = np.stack([_wl(np.asarray(inp["w_out"][l], f32), 8) for l in range(DEPTH)])
    sh["fw1"] = np.ascontiguousarray(np.asarray(inp["hy_fw1"], f32).transpose(1, 0, 2))
    sh["fw23"] = np.ascontiguousarray(np.stack([np.asarray(inp["hy_fw2"], f32), np.asarray(inp["hy_fw3"], f32)],
                                               axis=1).transpose(2, 0, 1, 3))
    sh["fw4"] = np.ascontiguousarray(np.asarray(inp["hy_fw4"], f32))
    sh["fpp"] = np.ascontiguousarray(np.stack([np.asarray(inp["hy_fb1"], f32), np.asarray(inp["hy_fb2"], f32),
                                               np.asarray(inp["hy_fb3"], f32), np.asarray(inp["hy_freq"], f32)],
                                              axis=1).transpose(2, 0, 1))
    sh["dtb"] = np.ascontiguousarray(np.asarray(inp["ssd_dt_bias"], f32).reshape(DEPTH, 16).T)
    return sh


SHARED_SHAPES = None


def build(depth, shapes, dbg=None):
    nc = bass.Bass("TRN2", target_bir_lowering=False)
    din = {k: nc.dram_tensor(k, list(s), BF16 if k[:3] in ("gc_", "gs_") else F32, kind="ExternalInput").ap() for k, s in shapes.items()}
    out = nc.dram_tensor("out", [BPC, 8, 128, SEQ], F32, kind="ExternalOutput").ap()

    def scratch(nm, shape, nt, dt=F32):
        return DT(nc.dram_tensor(nm, list(shape), dt, kind="Internal").ap(), nt)

    with contextlib.ExitStack() as st:
        S = Sched(nc, st)
        st.enter_context(nc.allow_non_contiguous_dma(reason="small strided halo / layout DMAs"))

        def sb(nm, shape, dt=F32):
            return st.enter_context(nc.sbuf_tensor("sb_" + nm, list(shape), dt))

        class Ar:
            def __init__(self, t, size):
                self.t = t; self.size = size; self.o = 0

            def take(self, n, pat=None, parts=None, **kw):
                assert self.o + n <= self.size, (self.o, n, self.size)
                v = self.t[:, self.o:self.o + n] if parts is None else self.t[0:parts, self.o:self.o + n]
                self.o += n
                return V(v.rearrange(pat, **kw) if pat else v)

        def phase_start():
            S.barrier()
            A.o = 0; B.o = 0; C.o = 0

        def ps(nm):
            return V(st.enter_context(nc.psum_tensor(nm, [128, 512], F32)))
        P = [ps("ps%d" % i) for i in range(8)]
        AR_A = sb("arA", [128, 10752])
        AR_B = sb("arB", [128, 16384])
        AR_C = sb("arC", [128, 34048], BF16)
        A = Ar(AR_A, 10752); B = Ar(AR_B, 16384); C = Ar(AR_C, 34048)
        NW = 8
        WPOOL = [V(sb("w%d" % i, [128, 8, 128], BF16)) for i in range(NW)]
        wrr = [0]

        def wnext():
            wrr[0] = (wrr[0] + 1) % NW
            return WPOOL[wrr[0]]
        masks = V(sb("masks", [128, 5, 128]))
        ppt = V(sb("pp", [128, NPP]))
        ones = V(sb("ones", [128, 128]))
        cst = V(sb("cst", [128, 4]))
        sct = V(sb("sct", [128, 8, 3]))
        MOD = V(sb("mod", [128, 9, 8, 3]))
        GIN = V(sb("gin", [128, 3, 8, 3]))
        COUT = V(sb("cout", [128, 3, 8, 3]))
        Aneg = V(sb("aneg", [128, 16]))
        fw1 = V(sb("fw1", [33, 4, 64])); fw23 = V(sb("fw23", [64, 4, 2, 64])); ppb = V(sb("ppb", [128, 768]))
        fpp = V(sb("fpp", [64, 4, 4])); dtb = V(sb("dtb", [16, 4]))
        HSTD = scratch("hstd", [2, 2, 128, 512], 1)
        small = V(sb("small", [128, 64]))

        def pp(nm, *idx_shape):
            o, n = PP[nm]
            return ppt.t[:, o:o + n]

        def viewA(o, n):
            return V(AR_A[:, o:o + n])

        def viewB(o, n):
            return V(AR_B[:, o:o + n])

        S.dma("sp", masks.t[:], din["masks"], writes=[masks])
        S.dma("sp", ppt.t[:], din["pp"], writes=[ppt])
        S.dma("sp", sct.t[:], din["cT"], writes=[sct])
        S.dma("sp", fw1.t[:], din["fw1"], writes=[fw1])
        S.dma("sp", fw23.t[:], din["fw23"], writes=[fw23])
        S.dma("sp", fpp.t[:], din["fpp"], writes=[fpp])
        S.dma("sp", dtb.t[:], din["dtb"], writes=[dtb])
        S.op("dve", lambda e: e.memset(ones.t[:], 1.0), writes=[ones])
        S.op("dve", lambda e: e.memset(cst.t[:, 0:1], EPS), writes=[cst])
        S.op("dve", lambda e: e.memset(cst.t[:, 1:2], 1.0), writes=[cst])
        S.op("dve", lambda e: e.memset(cst.t[:, 2:3], 0.0), writes=[cst])
        S.op("act", lambda e: e.activation(out=sct.t[:], in_=sct.t[:], func=AF.Silu), reads=[sct], writes=[sct])
        ident = masks.t[:, 4, :]

        class Seq:
            pass
        seqs = []
        for s in range(2 * BPC):
            q = Seq()
            q.ctx = s >= BPC
            q.bl = s % BPC
            q.L = CTX if q.ctx else SEQ
            q.T = min(512, q.L)
            q.who = 2 if q.ctx else q.bl
            q.nt = q.L // 128
            q.nm = "s" if q.ctx else "b"
            src = din["xc"] if q.ctx else din["xl"]
            q.xin = DT(src[q.bl], q.nt)
            q.xs = scratch("xs%d" % s, [8, 128, q.L], q.nt)
            q.px = scratch("px%d" % s, [NPXC, 128, q.L], q.nt)
            q.ym = scratch("ym%d" % s, [8, 128, q.L], q.nt, BF16)
            q.yf = scratch("yf%d" % s, [q.L, 512], q.nt)
            q.hv = scratch("hv%d" % s, [q.L, 768], q.nt)
            q.z1 = scratch("z1%d" % s, [q.L, 256], q.nt)
            q.cache = scratch("cache%d" % s, [q.nt, 128, 1808], q.nt)
            q.nfc = (q.L + 1 + 127) // 128
            q.first = True
            seqs.append(q)
        KS = {}
        EO = {}
        for nm, L in (("b", SEQ), ("s", CTX)):
            nfc = (L + 1 + 127) // 128
            KS[nm] = scratch("ks_" + nm, [nfc, 128, 2, 2, 256], 1)
            EO[nm] = scratch("eo_" + nm, [2, L, 512], 1, BF16)
        GT_ = {nm: (DT(din["gc_" + nm], 1), DT(din["gs_" + nm], 1)) for nm in "bs"}

        def xsrc(q):
            return q.xin if q.first else q.xs

        def rms(src_aps, T, nD, rstd, sq2, pn, reads, lnexp=False):
            n = len(src_aps)
            for i, a in enumerate(src_aps):
                sq = sq2[i % 2]
                S.op("act", lambda e, a=a, sq=sq: e.activation(out=sq.t[:, :T], in_=a, func=AF.Square),
                     reads=reads, writes=[sq])
                S.op("pe", lambda e, sq=sq, i=i: e.matmul(pn.t[:, :T], ones.t[:], sq.t[:, :T], start=(i == 0),
                                                           stop=(i == n - 1)), reads=[sq, ones], writes=[pn])
            if lnexp:
                S.op("act", lambda e: e.activation(out=rstd.t[:, :T], in_=pn.t[:, :T], func=AF.Ln,
                                                   bias=cst.t[:, 0:1], scale=1.0 / nD), reads=[pn, cst], writes=[rstd])
                S.op("act", lambda e: e.activation(out=rstd.t[:, :T], in_=rstd.t[:, :T], func=AF.Exp, scale=-0.5),
                     reads=[rstd], writes=[rstd])
                return
            S.op("act", lambda e: e.activation(out=rstd.t[:, :T], in_=pn.t[:, :T], func=AF.Sqrt,
                                               bias=cst.t[:, 0:1], scale=1.0 / nD), reads=[pn, cst], writes=[rstd])
            S.op("dve", lambda e: e.reciprocal(out=rstd.t[:, :T], in_=rstd.t[:, :T]), reads=[rstd], writes=[rstd])

        def load_norm(q, sub, t0, T, xt, ht, rstd, sq2, pn, tmp2):
            xsr = xsrc(q)
            S.dma("sp", xt.t[:, :, :T], xsr.ap[:, :, t0:t0 + T].rearrange("c p t -> p c t"),
                  reads=xsr.tk(t0, t0 + T), writes=[xt])
            rms([xt.t[:, kc, :T] for kc in range(8)], T, D, rstd, sq2, pn, [xt])
            for kc in range(8):
                tp = tmp2[kc % 2]
                S.op("dve", lambda e, kc=kc, tp=tp: e.scalar_tensor_tensor(
                    out=tp.t[:, :T], in0=xt.t[:, kc, :T], scalar=GIN.t[:, sub, kc, q.who:q.who + 1],
                    in1=rstd.t[:, :T], op0=ALU.mult, op1=ALU.mult), reads=[xt, GIN, rstd], writes=[tp])
                S.op("act", lambda e, kc=kc, tp=tp: e.activation(
                    out=ht.t[:, kc, :T], in_=tp.t[:, :T], func=AF.Identity,
                    bias=MOD.t[:, 3 * sub, kc, q.who:q.who + 1], scale=1.0), reads=[tp, MOD], writes=[ht])

        def resid_store(q, sub, t0, T, xt, yt, rstd, sq2, pn, tmp, dst=None):
            rms([yt.t[:, kc, :T] for kc in range(8)], T, D, rstd, sq2, pn, [yt])
            for kc in range(8):
                S.op("dve", lambda e, kc=kc: e.scalar_tensor_tensor(
                    out=yt.t[:, kc, :T], in0=yt.t[:, kc, :T], scalar=COUT.t[:, sub, kc, q.who:q.who + 1],
                    in1=rstd.t[:, :T], op0=ALU.mult, op1=ALU.mult), reads=[yt, COUT, rstd], writes=[yt])
                S.op("dve", lambda e, kc=kc: e.tensor_tensor(out=xt.t[:, kc, :T], in0=xt.t[:, kc, :T],
                                                             in1=yt.t[:, kc, :T], op=ALU.add),
                     reads=[xt, yt], writes=[xt])
            d = dst or q.xs
            S.dma("pool", d.ap[:, :, t0:t0 + T].rearrange("c p t -> p c t"), xt.t[:, :, :T],
                  reads=[xt], writes=d.tk(t0, t0 + T))

        W16 = {}

        def convert_weights():
            phase_start()
            stg = [B.take(4096) for _ in range(3)]
            out16 = [C.take(4096) for _ in range(3)]
            engs = ["act", "dve", "pool"]
            n = [0]
            for nm, grp in (("fwi", 4), ("fwo", 1), ("wi", 4), ("wo", 4)):
                src = din[nm]
                shp = list(src.shape)
                W16[nm] = DT(nc.dram_tensor(nm + "16", shp, BF16, kind="Internal").ap(), 1)
                dst = W16[nm].ap
                lead = shp[:-4]
                noc = shp[-4]
                F = shp[-2] * shp[-1]
                idxs = [()]
                for d_ in lead:
                    idxs = [i + (k,) for i in idxs for k in range(d_)]
                for ix in idxs:
                    sa = src; da = dst
                    for k in ix:
                        sa = sa[k]; da = da[k]
                    for o0 in range(0, noc, grp):
                        g = min(grp, noc - o0)
                        i = n[0] % 3; n[0] += 1
                        sv = stg[i].t[:, 0:g * F].rearrange("p (o f) -> p o f", o=g)
                        ov = out16[i].t[:, 0:g * F].rearrange("p (o f) -> p o f", o=g)
                        S.dma("sp", sv, sa[o0:o0 + g].rearrange("o p k m -> p o (k m)"), writes=[stg[i]])
                        if engs[i] == "act":
                            S.op("act", lambda e, sv=sv, ov=ov: e.activation(out=ov, in_=sv, func=AF.Copy),
                                 reads=[stg[i]], writes=[out16[i]])
                        else:
                            S.op(engs[i], lambda e, sv=sv, ov=ov: e.tensor_copy(out=ov, in_=sv),
                                 reads=[stg[i]], writes=[out16[i]])
                        S.dma("pool", da[o0:o0 + g].rearrange("o p k m -> p o (k m)"), ov, reads=[out16[i]])

        def mod_phase(l):
            phase_start()
            wmd = DT(din["wm"], 1)
            WF32 = [B.take(1024, "p (k m) -> p k m", k=8) for i in range(4)]
            for oc in range(72):
                w = WF32[oc % 4]
                S.dma("sp", w.t[:], wmd.ap[l, oc], writes=[w])
                pm = P[oc % 2]
                for kc in range(8):
                    S.op("pe", lambda e, kc=kc, w=w, pm=pm: e.matmul(pm.t[:, 0:3], w.t[:, kc, :], sct.t[:, kc, :],
                                                                     start=(kc == 0), stop=(kc == 7)),
                         reads=[w, sct], writes=[pm])
                o = PP["bm"][0] + l * 72 + oc
                S.op("act", lambda e, oc=oc, pm=pm, o=o: e.activation(
                    out=MOD.t[:, oc // 8, oc % 8, :], in_=pm.t[:, 0:3], func=AF.Identity,
                    bias=ppt.t[:, o:o + 1], scale=1.0), reads=[pm, ppt], writes=[MOD])
            ngo = PP["ng"][0] + l * 48
            for i in range(3):
                gpre = ppt.t[:, ngo + (2 * i) * 8: ngo + (2 * i) * 8 + 8].unsqueeze(2).broadcast_to([128, 8, 3])
                gpost = ppt.t[:, ngo + (2 * i + 1) * 8: ngo + (2 * i + 1) * 8 + 8].unsqueeze(2).broadcast_to([128, 8, 3])
                S.op("dve", lambda e, i=i: e.tensor_scalar(out=GIN.t[:, i], in0=MOD.t[:, 3 * i + 1], scalar1=1.0,
                                                           scalar2=None, op0=ALU.add), reads=[MOD], writes=[GIN])
                S.op("dve", lambda e, i=i, g=gpre: e.tensor_tensor(out=GIN.t[:, i], in0=GIN.t[:, i], in1=g,
                                                                   op=ALU.mult), reads=[GIN, ppt], writes=[GIN])
                rw = 1.0 if i == 1 else 0.5
                S.op("dve", lambda e, i=i, g=gpost, rw=rw: e.scalar_tensor_tensor(
                    out=COUT.t[:, i], in0=MOD.t[:, 3 * i + 2], scalar=rw, in1=g, op0=ALU.mult, op1=ALU.mult),
                    reads=[MOD, ppt], writes=[COUT])
            o = PP["alog"][0] + l * 16
            S.op("act", lambda e: e.activation(out=Aneg.t[:], in_=ppt.t[:, o:o + 16], func=AF.Exp),
                 reads=[ppt], writes=[Aneg])
            S.op("dve", lambda e: e.tensor_scalar(out=Aneg.t[:], in0=Aneg.t[:], scalar1=-1.0, scalar2=None,
                                                  op0=ALU.mult), reads=[Aneg], writes=[Aneg])

        def ffn_phase(q, l, j):
            phase_start()
            sub = 2 * j
            T = q.T
            xts = [B.take(4096, "p (c t) -> p c t", c=8) for i in range(2)]
            yts = [B.take(4096, "p (c t) -> p c t", c=8) for i in range(2)]
            hts = [C.take(4096, "p (c t) -> p c t", c=8) for i in range(2)]
            at = C.take(11264, "p (c t) -> p c t", c=22)
            wos = [C.take(2816, "p (c m) -> p c m", c=22) for i in range(3)]
            sq2 = [A.take(512) for i in range(2)]
            rstd = A.take(512)
            sg2 = [A.take(512) for i in range(2)]
            tmp2 = [A.take(512) for i in range(2)]
            fwi = W16["fwi"]
            fwo = W16["fwo"]
            ntile = q.L // T
            load_norm(q, sub, 0, T, xts[0], hts[0], rstd, sq2, P[7], tmp2)
            for tt in range(ntile):
                t0 = tt * T
                xt = xts[tt % 2]; ht = hts[tt % 2]; yt = yts[tt % 2]
                for jf in range(22):
                    if jf == 4 and tt > 0:
                        resid_store(q, sub, t0 - T, T, xts[(tt - 1) % 2], yts[(tt - 1) % 2], rstd, sq2, P[7], None)
                    if jf == 12 and tt + 1 < ntile:
                        load_norm(q, sub, t0 + T, T, xts[(tt + 1) % 2], hts[(tt + 1) % 2], rstd, sq2, P[7], tmp2)
                    wg = wnext(); S.dma("sp", wg.t[:], fwi.ap[l, j, jf], writes=[wg])
                    wu = wnext(); S.dma("sp", wu.t[:], fwi.ap[l, j, 22 + jf], writes=[wu])
                    pg = P[(2 * jf) % 4]; pu = P[(2 * jf) % 4 + 1]
                    for kc in range(8):
                        S.op("pe", lambda e, kc=kc, wg=wg, pg=pg: e.matmul(pg.t[:, :T], wg.t[:, kc, :], ht.t[:, kc, :T],
                                                                           start=(kc == 0), stop=(kc == 7)),
                             reads=[wg, ht], writes=[pg])
                    for kc in range(8):
                        S.op("pe", lambda e, kc=kc, wu=wu, pu=pu: e.matmul(pu.t[:, :T], wu.t[:, kc, :], ht.t[:, kc, :T],
                                                                           start=(kc == 0), stop=(kc == 7)),
                             reads=[wu, ht], writes=[pu])
                    sg = sg2[jf % 2]
                    S.op("act", lambda e, sg=sg, pg=pg: e.activation(out=sg.t[:, :T], in_=pg.t[:, :T], func=AF.Silu),
                         reads=[pg], writes=[sg])
                    S.op("dve", lambda e, sg=sg, pu=pu, jf=jf: e.tensor_tensor(out=at.t[:, jf, :T], in0=sg.t[:, :T],
                                                                               in1=pu.t[:, :T], op=ALU.mult),
                         reads=[sg, pu], writes=[at])
                for oc in range(8):
                    wo = wos[oc % 3]
                    S.dma("sp", wo.t[:], fwo.ap[l, j, oc], writes=[wo])
                    py = P[4 + oc % 2]
                    for fc in range(22):
                        S.op("pe", lambda e, fc=fc, wo=wo, py=py: e.matmul(py.t[:, :T], wo.t[:, fc, :], at.t[:, fc, :T],
                                                                           start=(fc == 0), stop=(fc == 21)),
                             reads=[wo, at], writes=[py])
                    S.op("act", lambda e, oc=oc, py=py, yt=yt: e.activation(out=yt.t[:, oc, :T], in_=py.t[:, :T],
                                                                            func=AF.Copy), reads=[py], writes=[yt])
            resid_store(q, sub, (ntile - 1) * T, T, xts[(ntile - 1) % 2], yts[(ntile - 1) % 2], rstd, sq2, P[7], None)
            q.first = False


        def proj_phase(q, l):
            phase_start()
            T = q.T
            xts = [B.take(4096, "p (c t) -> p c t", c=8) for i in range(2)]
            ht = C.take(4096, "p (c t) -> p c t", c=8)
            sq2 = [A.take(512) for i in range(2)]
            rstd = A.take(512)
            tmp2 = [A.take(512) for i in range(2)]
            ot4 = [A.take(512) for i in range(4)]
            wi = W16["wi"]
            for tt in range(q.L // T):
                t0 = tt * T
                xt = xts[tt % 2]
                load_norm(q, 1, t0, T, xt, ht, rstd, sq2, P[7], tmp2)
                for oc in range(NPXC):
                    w = wnext(); S.dma("sp", w.t[:], wi.ap[l, oc], writes=[w])
                    pm = P[oc % 4]; o = ot4[oc % 4]
                    for kc in range(8):
                        S.op("pe", lambda e, kc=kc, w=w, pm=pm: e.matmul(pm.t[:, :T], w.t[:, kc, :], ht.t[:, kc, :T],
                                                                         start=(kc == 0), stop=(kc == 7)),
                             reads=[w, ht], writes=[pm])
                    S.op("act", lambda e, o=o, pm=pm: e.activation(out=o.t[:, :T], in_=pm.t[:, :T], func=AF.Copy),
                         reads=[pm], writes=[o])
                    S.dma("pool", q.px.ap[oc, :, t0:t0 + T], o.t[:, :T], reads=[o], writes=q.px.tk(t0, t0 + T))

        def conv3(acc_ap, raw_ap3, SL, wo, nw, kc, accv, rawv):
            def wj(j):
                o = wo + j * nw + kc
                return ppt.t[:, o:o + 1]
            S.op("dve", lambda e: e.tensor_scalar(out=acc_ap, in0=raw_ap3[:, :, 1:SL + 1], scalar1=wj(1), scalar2=None,
                                                  op0=ALU.mult), reads=[rawv, ppt], writes=[accv])
            S.op("dve", lambda e: e.scalar_tensor_tensor(out=acc_ap, in0=raw_ap3[:, :, 0:SL], scalar=wj(0), in1=acc_ap,
                                                         op0=ALU.mult, op1=ALU.add), reads=[rawv, ppt, accv], writes=[accv])
            S.op("dve", lambda e: e.scalar_tensor_tensor(out=acc_ap, in0=raw_ap3[:, :, 2:SL + 2], scalar=wj(2), in1=acc_ap,
                                                         op0=ALU.mult, op1=ALU.add), reads=[rawv, ppt, accv], writes=[accv])

        def load_halo(q, raw, c0, nch, t0, n, SL):
            nsub = n // SL
            S.op("dve", lambda e: e.memset(raw.t[:], 0.0), writes=[raw])
            for s_ in range(nsub):
                a = t0 + s_ * SL
                S.dma("sp", raw.t[:, :, s_, 1:SL + 1], q.px.ap[c0:c0 + nch, :, a:a + SL].rearrange("c p t -> p c t"),
                      reads=q.px.tk(a, a + SL), writes=[raw])
            if q.ctx:
                if t0 > 0:
                    S.dma("sp", raw.t[:, :, 0, 0:1], q.px.ap[c0:c0 + nch, :, t0 - 1:t0].rearrange("c p t -> p c t"),
                          reads=q.px.tk(t0 - 1, t0), writes=[raw])
                if t0 + n < q.L:
                    S.dma("sp", raw.t[:, :, nsub - 1, SL + 1:SL + 2],
                          q.px.ap[c0:c0 + nch, :, t0 + n:t0 + n + 1].rearrange("c p t -> p c t"),
                          reads=q.px.tk(t0 + n, t0 + n + 1), writes=[raw])

        def sc_phase(q, l):
            phase_start()
            T = q.T
            SL = 64 if not q.ctx else T
            nsub = T // SL
            gt = B.take(6 * T, "p (c t) -> p c t", c=6)
            raw = B.take(2 * nsub * (SL + 2), "p (c s t) -> p c s t", c=2, s=nsub)
            acc = B.take(2 * T, "p (c t) -> p c t", c=2)
            o16 = C.take(2 * T, "p (c t) -> p c t", c=2)
            sq2 = [A.take(512) for i in range(2)]
            rstd = A.take(512)
            for tt in range(q.L // T):
                t0 = tt * T
                S.dma("sp", gt.t[:, :, :T], q.px.ap[19:25, :, t0:t0 + T].rearrange("c p t -> p c t"),
                      reads=q.px.tk(t0, t0 + T), writes=[gt])
                S.op("dve", lambda e: e.memset(raw.t[:], 0.0), writes=[raw])
                for c2 in range(2):
                    S.op("dve", lambda e, c2=c2: e.tensor_tensor(
                        out=raw.t[:, c2, :, 1:SL + 1], in0=gt.t[:, 2 + c2, :T].rearrange("p (s t) -> p s t", s=nsub),
                        in1=gt.t[:, 4 + c2, :T].rearrange("p (s t) -> p s t", s=nsub), op=ALU.mult),
                        reads=[gt], writes=[raw])
                    accap = acc.t[:, c2, :T].rearrange("p (s t) -> p s t", s=nsub)
                    conv3(accap, raw.t[:, c2], SL, PP["ccw"][0] + l * 6, 2, c2, acc, raw)
                    S.op("dve", lambda e, c2=c2: e.tensor_tensor(out=acc.t[:, c2, :T], in0=acc.t[:, c2, :T],
                                                                 in1=gt.t[:, c2, :T], op=ALU.mult),
                         reads=[acc, gt], writes=[acc])
                rms([acc.t[:, c2, :T] for c2 in range(2)], T, 256, rstd, sq2, P[7], [acc])
                for c2 in range(2):
                    o = PP["mg"][0] + l * 8 + 6 + c2
                    S.op("dve", lambda e, c2=c2, o=o: e.scalar_tensor_tensor(
                        out=o16.t[:, c2, :T], in0=acc.t[:, c2, :T], scalar=ppt.t[:, o:o + 1], in1=rstd.t[:, :T],
                        op0=ALU.mult, op1=ALU.mult), reads=[acc, ppt, rstd], writes=[o16])
                S.dma("pool", q.ym.ap[6:8, :, t0:t0 + T].rearrange("c p t -> p c t"), o16.t[:, :, :T],
                      reads=[o16], writes=q.ym.tk(t0, t0 + T))

        def hyprep_phase(q, l):
            phase_start()
            T = q.T
            SL = 64 if not q.ctx else T
            nsub = T // SL
            raws = [B.take(6 * nsub * (SL + 2), "p (c s t) -> p c s t", c=6, s=nsub) for i in range(2)]
            acc = B.take(6 * T, "p (c s t) -> p c s t", c=6, s=nsub)
            ctm = B.take(6 * T, "p (c s t) -> p c s t", c=6, s=nsub)
            ots = [A.take(768) for i in range(2)]
            hwo = PP["hcw"][0] + l * 18
            hbo = PP["hcb"][0] + l * 6

            def wb(j):
                return ppt.t[:, hwo + j * 6:hwo + j * 6 + 6].unsqueeze(2).unsqueeze(3).broadcast_to([128, 6, nsub, SL])
            for k in range(2):
                S.op("dve", lambda e, k=k: e.memset(raws[k].t[:], 0.0), writes=[raws[k]])
            n_ot = 0
            for tt in range(q.L // T):
                t0 = tt * T
                raw = raws[tt % 2]
                for s_ in range(nsub):
                    a0 = t0 + s_ * SL
                    S.dma("sp", raw.t[:, :, s_, 1:SL + 1], q.px.ap[0:6, :, a0:a0 + SL].rearrange("c p t -> p c t"),
                          reads=q.px.tk(a0, a0 + SL), writes=[raw])
                S.op("dve", lambda e, raw=raw: e.tensor_tensor(out=acc.t[:], in0=raw.t[:, :, :, 1:SL + 1], in1=wb(1), op=ALU.mult),
                     reads=[raw, ppt], writes=[acc])
                for j, sl in ((0, slice(0, SL)), (2, slice(2, SL + 2))):
                    S.op("dve", lambda e, raw=raw, j=j, sl=sl: e.tensor_tensor(out=ctm.t[:], in0=raw.t[:, :, :, sl], in1=wb(j),
                                                                              op=ALU.mult), reads=[raw, ppt], writes=[ctm])
                    S.op("dve", lambda e: e.tensor_tensor(out=acc.t[:], in0=acc.t[:], in1=ctm.t[:], op=ALU.add),
                         reads=[acc, ctm], writes=[acc])
                S.op("dve", lambda e: e.tensor_tensor(
                    out=acc.t[:], in0=acc.t[:],
                    in1=ppt.t[:, hbo:hbo + 6].unsqueeze(2).unsqueeze(3).broadcast_to([128, 6, nsub, SL]), op=ALU.add),
                    reads=[acc, ppt], writes=[acc])
                accf = acc.t[:].rearrange("p c s t -> p c (s t)")
                for sub in range(T // 128):
                    for kc in range(6):
                        pt = P[(sub % 2) * 2 + kc // 4]
                        S.op("pe", lambda e, kc=kc, pt=pt, sub=sub: e.transpose(
                            pt.t[:, (kc % 4) * 128:(kc % 4 + 1) * 128], accf[:, kc, sub * 128:(sub + 1) * 128], ident),
                            reads=[acc, masks], writes=[pt])
                    ot = ots[n_ot % 2]; n_ot += 1
                    pa = P[(sub % 2) * 2]; pb = P[(sub % 2) * 2 + 1]
                    S.op("act", lambda e, ot=ot, pa=pa: e.activation(out=ot.t[:, 0:512], in_=pa.t[:, 0:512], func=AF.Copy),
                         reads=[pa], writes=[ot])
                    S.op("act", lambda e, ot=ot, pb=pb: e.activation(out=ot.t[:, 512:768], in_=pb.t[:, 0:256], func=AF.Copy),
                         reads=[pb], writes=[ot])
                    r0 = t0 + sub * 128
                    S.dma("pool", q.hv.ap[r0:r0 + 128, :], ot.t[:], reads=[ot], writes=q.hv.tk(r0, r0 + 128))

        def filter_phase(l, nm):
            phase_start()
            L = SEQ if nm == "b" else CTX
            T = min(512, L)
            zt2 = [A.take(512, parts=33) for i in range(2)]
            hs = [A.take(512, parts=64) for i in range(3)]
            arg = A.take(512, parts=64); kk = A.take(512, parts=64)
            dec2 = [A.take(256) for i in range(2)]
            hf2 = [A.take(1024) for i in range(2)]
            eo2 = [C.take(1024) for i in range(2)]
            fw4 = B.take(1024, parts=64)
            S.dma("sp", fw4.t[:], din["fw4"][l], writes=[fw4])
            PI_LO = 3.1415925
            for tt in range(L // T):
                t0 = tt * T
                zt = zt2[tt % 2]
                S.dma("sp", zt.t[:, :T], din["zT_" + nm][:, t0:t0 + T], writes=[zt])
                src, K = zt, 33
                for i in range(3):
                    lhsT = fw1.t[:, l, :] if i == 0 else fw23.t[:, l, i - 1, :]
                    wv = fw1 if i == 0 else fw23
                    S.op("pe", lambda e, lhsT=lhsT, src=src, K=K: e.matmul(P[0].t[0:64, :T], lhsT, src.t[0:K, :T],
                                                                          start=True, stop=True),
                         reads=[wv, src], writes=[P[0]])
                    S.op("dve", lambda e, i=i: e.tensor_scalar(out=arg.t[:, :T], in0=P[0].t[0:64, :T],
                                                               scalar1=fpp.t[:, l, i:i + 1], scalar2=fpp.t[:, l, 3:4],
                                                               op0=ALU.add, op1=ALU.mult), reads=[P[0], fpp], writes=[arg])
                    S.op("dve", lambda e: e.tensor_scalar(out=kk.t[:, :T], in0=arg.t[:, :T], scalar1=1.0 / TWO_PI,
                                                          scalar2=MAGIC, op0=ALU.mult, op1=ALU.add),
                         reads=[arg], writes=[kk])
                    S.op("dve", lambda e: e.tensor_scalar(out=kk.t[:, :T], in0=kk.t[:, :T], scalar1=-MAGIC,
                                                          scalar2=-TWO_PI, op0=ALU.add, op1=ALU.mult),
                         reads=[kk], writes=[kk])
                    S.op("dve", lambda e: e.tensor_tensor(out=arg.t[:, :T], in0=arg.t[:, :T], in1=kk.t[:, :T],
                                                          op=ALU.add), reads=[arg, kk], writes=[arg])
                    S.op("dve", lambda e: e.tensor_scalar(out=arg.t[:, :T], in0=arg.t[:, :T], scalar1=-PI_LO,
                                                          scalar2=PI_LO, op0=ALU.max, op1=ALU.min),
                         reads=[arg], writes=[arg])
                    h = hs[i]
                    S.op("act", lambda e, h=h: e.activation(out=h.t[:, :T], in_=arg.t[:, :T], func=AF.Sin),
                         reads=[arg], writes=[h])
                    src, K = h, 64
                for sub in range(T // 128):
                    ti = tt * (T // 128) + sub
                    dec = dec2[ti % 2]; hf = hf2[ti % 2]; eo = eo2[ti % 2]
                    S.dma("sp", dec.t[:], din["dec_" + nm][ti * 128:(ti + 1) * 128, :], writes=[dec])
                    for o in range(2):
                        S.op("pe", lambda e, o=o, sub=sub: e.matmul(P[1 + o].t[:, 0:512], hs[2].t[:, sub * 128:(sub + 1) * 128],
                                                                    fw4.t[:, o * 512:(o + 1) * 512], start=True, stop=True),
                             reads=[hs[2], fw4], writes=[P[1 + o]])
                        S.op("dve", lambda e, o=o, hf=hf, dec=dec: e.tensor_tensor(
                            out=hf.t[:, o * 512:(o + 1) * 512].rearrange("p (d c) -> p d c", d=2),
                            in0=P[1 + o].t[:, 0:512].rearrange("p (d c) -> p d c", d=2),
                            in1=dec.t[:].unsqueeze(1).broadcast_to([128, 2, 256]), op=ALU.mult),
                            reads=[P[1 + o], dec], writes=[hf])
                    if ti == 0:
                        for o in range(2):
                            S.op("dve", lambda e, o=o, hf=hf: e.memset(hf.t[0:1, o * 512 + 256:o * 512 + 512], 0.0),
                                 writes=[hf])
                    hv4 = hf.t[:].rearrange("p (o d c) -> p o d c", o=2, d=2)
                    S.op("dve", lambda e, eo=eo, hv4=hv4: e.tensor_tensor(
                        out=eo.t[:, 0:512].rearrange("p (o c) -> p o c", o=2), in0=hv4[:, :, 0, :], in1=hv4[:, :, 1, :],
                        op=ALU.add), reads=[hf], writes=[eo])
                    S.op("dve", lambda e, eo=eo, hv4=hv4: e.tensor_tensor(
                        out=eo.t[:, 512:1024].rearrange("p (o c) -> p o c", o=2), in0=hv4[:, :, 1, :], in1=hv4[:, :, 0, :],
                        op=ALU.subtract), reads=[hf], writes=[eo])
                    for ri in range(2):
                        S.dma("pool", EO[nm].ap[ri, ti * 128:(ti + 1) * 128, :], eo.t[:, ri * 512:(ri + 1) * 512],
                              reads=[eo], writes=EO[nm].tk())

        def load_g(G, gt, col, nrow):
            S.dma("sp", gt.t[:, 0:nrow, :], G.ap[col, :, 0:nrow, :], writes=[gt])

        def kspec_phase(l, nm):
            phase_start()
            L = SEQ if nm == "b" else CTX
            NT = L // 128
            nfc = (L + 1 + 127) // 128
            rhs = C.take(NT * 512, "p (i c) -> p i c", i=NT)
            gts = [C.take(4224, "p (i v) -> p i v", i=33) for i in range(3)]
            kts = [A.take(512) for i in range(2)]
            wfo = PP["wfb" if nm == "b" else "wfs"][0]
            for ri in range(2):
                S.dma("sp", rhs.t[:], EO[nm].ap[ri].rearrange("(i p) c -> p i c", p=128), reads=EO[nm].tk(), writes=[rhs])
                for fc in range(nfc):
                    gt = gts[fc % 3]
                    load_g(GT_[nm][ri], gt, fc, NT)
                    pk = P[fc % 2]
                    for i in range(NT):
                        S.op("pe", lambda e, i=i, gt=gt, pk=pk: e.matmul(pk.t[:, 0:512], gt.t[:, i, :], rhs.t[:, i, :],
                                                                         start=(i == 0), stop=(i == NT - 1)),
                             reads=[gt, rhs], writes=[pk])
                    kt = kts[fc % 2]
                    S.op("act", lambda e, kt=kt, pk=pk, fc=fc: e.activation(out=kt.t[:], in_=pk.t[:, 0:512], func=AF.Copy,
                                                                            scale=ppt.t[:, wfo + fc:wfo + fc + 1]),
                         reads=[pk, ppt], writes=[kt])
                    S.dma("pool", KS[nm].ap[fc, :, :, ri, :], kt.t[:].rearrange("p (o c) -> p o c", o=2),
                          reads=[kt], writes=KS[nm].tk())

        def conv_phase(q, l, order):
            phase_start()
            NT = q.nt; nfc = q.nfc; nm = q.nm
            zt = B.take(NT * 256, "p (i c) -> p i c", i=NT)
            z16 = C.take(NT * 256, "p (i c) -> p i c", i=NT)
            Y = C.take(nfc * 512, "p (f r c) -> p f r c", f=nfc, r=2)
            gts4 = [C.take(4224, "p (i v) -> p i v", i=33) for i in range(2)]
            gts = gts4
            kt2 = [A.take(512, "p (r c) -> p r c", r=2) for i in range(2)]
            tm = [A.take(256) for i in range(4)]
            gate2 = [A.take(256) for i in range(2)]
            ot2 = [C.take(256) for i in range(2)]
            srcd = q.hv if order == 0 else q.z1
            srcap = q.hv.ap[:, 0:256] if order == 0 else q.z1.ap
            S.dma("sp", zt.t[:], srcap.rearrange("(i p) c -> p i c", p=128), reads=srcd.tk(), writes=[zt])
            S.op("pool", lambda e: e.tensor_copy(out=z16.t[:], in_=zt.t[:]), reads=[zt], writes=[z16])
            Gc, Gs = GT_[nm]
            for fc in range(nfc):
                load_g(Gc, gts[0], fc, NT)
                load_g(Gs, gts[1], fc, NT)
                for i in range(NT):
                    S.op("pe", lambda e, i=i: e.matmul(P[0].t[:, 0:256], gts[0].t[:, i, :], z16.t[:, i, :], start=(i == 0),
                                                       stop=(i == NT - 1)), reads=[gts[0], z16], writes=[P[0]])
                for i in range(NT):
                    S.op("pe", lambda e, i=i: e.matmul(P[1].t[:, 0:256], gts[1].t[:, i, :], z16.t[:, i, :], start=(i == 0),
                                                       stop=(i == NT - 1)), reads=[gts[1], z16], writes=[P[1]])
                kt = kt2[fc % 2]
                S.dma("sp", kt.t[:], KS[nm].ap[fc, :, order, :, :], reads=KS[nm].tk(), writes=[kt])
                pc, psn = P[0].t[:, 0:256], P[1].t[:, 0:256]
                S.op("dve", lambda e, kt=kt: e.tensor_tensor(out=tm[0].t[:], in0=pc, in1=kt.t[:, 0, :], op=ALU.mult),
                     reads=[P[0], kt], writes=[tm[0]])
                S.op("dve", lambda e, kt=kt: e.tensor_tensor(out=tm[1].t[:], in0=psn, in1=kt.t[:, 1, :], op=ALU.mult),
                     reads=[P[1], kt], writes=[tm[1]])
                S.op("dve", lambda e, fc=fc: e.tensor_tensor(out=Y.t[:, fc, 0, :], in0=tm[0].t[:], in1=tm[1].t[:], op=ALU.add),
                     reads=[tm[0], tm[1]], writes=[Y])
                S.op("dve", lambda e, kt=kt: e.tensor_tensor(out=tm[2].t[:], in0=psn, in1=kt.t[:, 0, :], op=ALU.mult),
                     reads=[P[1], kt], writes=[tm[2]])
                S.op("dve", lambda e, kt=kt: e.tensor_tensor(out=tm[3].t[:], in0=pc, in1=kt.t[:, 1, :], op=ALU.mult),
                     reads=[P[0], kt], writes=[tm[3]])
                S.op("dve", lambda e, fc=fc: e.tensor_tensor(out=Y.t[:, fc, 1, :], in0=tm[2].t[:], in1=tm[3].t[:],
                                                             op=ALU.subtract), reads=[tm[2], tm[3]], writes=[Y])
            for tt in range(NT):
                load_g(Gc, gts[0], tt, nfc)
                load_g(Gs, gts[1], tt, nfc)
                py = P[2 + tt % 2]
                for i in range(nfc):
                    S.op("pe", lambda e, i=i, py=py: e.matmul(py.t[:, 0:256], gts[0].t[:, i, :], Y.t[:, i, 0, :],
                                                              start=(i == 0), stop=False), reads=[gts[0], Y], writes=[py])
                for i in range(nfc):
                    S.op("pe", lambda e, i=i, py=py: e.matmul(py.t[:, 0:256], gts[1].t[:, i, :], Y.t[:, i, 1, :],
                                                              start=False, stop=(i == nfc - 1)), reads=[gts[1], Y], writes=[py])
                gate = gate2[tt % 2]
                S.dma("sp", gate.t[:], q.hv.ap[tt * 128:(tt + 1) * 128, 256 * (order + 1):256 * (order + 2)],
                      reads=q.hv.tk(tt * 128, tt * 128 + 128), writes=[gate])
                t_ = tm[tt % 2]
                S.op("dve", lambda e, t_=t_, tt=tt: e.tensor_tensor(out=t_.t[:], in0=zt.t[:, tt, :],
                                                                    in1=ppb.t[:, 256 + order * 256:512 + order * 256],
                                                                    op=ALU.mult), reads=[zt, ppb], writes=[t_])
                S.op("dve", lambda e, t_=t_, py=py: e.tensor_tensor(out=t_.t[:], in0=t_.t[:], in1=py.t[:, 0:256], op=ALU.add),
                     reads=[t_, py], writes=[t_])
                S.op("dve", lambda e, t_=t_, gate=gate: e.tensor_tensor(out=t_.t[:], in0=t_.t[:], in1=gate.t[:], op=ALU.mult),
                     reads=[t_, gate], writes=[t_])
                if order == 0:
                    S.dma("pool", q.z1.ap[tt * 128:(tt + 1) * 128, :], t_.t[:], reads=[t_],
                          writes=q.z1.tk(tt * 128, tt * 128 + 128))
                else:
                    sq = tm[2 + tt % 2]
                    S.op("dve", lambda e, t_=t_, sq=sq: e.tensor_tensor(out=sq.t[:], in0=t_.t[:], in1=t_.t[:], op=ALU.mult),
                         reads=[t_], writes=[sq])
                    S.op("dve", lambda e, sq=sq: e.tensor_reduce(out=small.t[:, 0:1], in_=sq.t[:], op=ALU.add,
                                                                 axis=mybir.AxisListType.X), reads=[sq], writes=[small])
                    S.op("act", lambda e: e.activation(out=small.t[:, 1:2], in_=small.t[:, 0:1], func=AF.Sqrt,
                                                       bias=cst.t[:, 0:1], scale=1.0 / 256), reads=[small, cst], writes=[small])
                    S.op("dve", lambda e: e.reciprocal(out=small.t[:, 2:3], in_=small.t[:, 1:2]), reads=[small], writes=[small])
                    S.op("dve", lambda e, t_=t_: e.scalar_tensor_tensor(out=t_.t[:], in0=t_.t[:], scalar=small.t[:, 2:3],
                                                                        in1=ppb.t[:, 0:256], op0=ALU.mult, op1=ALU.mult),
                         reads=[t_, small, ppb], writes=[t_])
                    for c2 in range(2):
                        S.op("pe", lambda e, c2=c2, t_=t_: e.transpose(P[4].t[:, c2 * 128:(c2 + 1) * 128],
                                                                        t_.t[:, c2 * 128:(c2 + 1) * 128], ident),
                             reads=[t_, masks], writes=[P[4]])
                    ot = ot2[tt % 2]
                    S.op("act", lambda e, ot=ot: e.activation(out=ot.t[:], in_=P[4].t[:, 0:256], func=AF.Copy),
                         reads=[P[4]], writes=[ot])
                    S.dma("pool", q.ym.ap[0:2, :, tt * 128:(tt + 1) * 128].rearrange("c p t -> p c t"),
                          ot.t[:].rearrange("p (c t) -> p c t", c=2), reads=[ot], writes=q.ym.tk(tt * 128, tt * 128 + 128))

        def conv2_phase(qs, l, order):
            phase_start()
            q0 = qs[0]
            NT = q0.nt; nfc = q0.nfc; nm = q0.nm
            z16 = C.take(NT * 512, "p (i s c) -> p i s c", i=NT, s=2)
            Ylast = C.take(1024, "p (r s c) -> p r s c", r=2, s=2)
            gts = [C.take(4224, "p (i v) -> p i v", i=33) for _ in range(2)]
            ot2 = [C.take(512, "p (s c t) -> p s c t", s=2, c=2) for _ in range(2)]
            Ymain = V(AR_B[:, :].bitcast(BF16)[:, 0:(nfc - 1) * 1024].rearrange("p (f r s c) -> p f r s c", f=nfc - 1, r=2, s=2))
            B.o = 16384
            stg = [A.take(512, "p (s c) -> p s c", s=2) for _ in range(2)]
            kt2 = [A.take(512, "p (r c) -> p r c", r=2) for _ in range(2)]
            tm = [A.take(512, "p (s c) -> p s c", s=2) for _ in range(4)]
            gate2 = [A.take(512, "p (s c) -> p s c", s=2) for _ in range(2)]

            def Yv(fc, r):
                if fc < nfc - 1:
                    return Ymain.t[:, fc, r], Ymain
                return Ylast.t[:, r], Ylast

            def srcrows(q, r0):
                if order == 0:
                    return q.hv, q.hv.ap[r0:r0 + 128, 0:256]
                return q.z1, q.z1.ap[r0:r0 + 128, :]
            for i in range(NT):
                st_ = stg[i % 2]
                for s_, q in enumerate(qs):
                    sd, sa = srcrows(q, i * 128)
                    S.dma("sp", st_.t[:, s_, :], sa, reads=sd.tk(i * 128, i * 128 + 128), writes=[st_])
                if i % 2:
                    S.op("act", lambda e, i=i, st_=st_: e.activation(out=z16.t[:, i], in_=st_.t[:], func=AF.Copy),
                         reads=[st_], writes=[z16])
                else:
                    S.op("dve", lambda e, i=i, st_=st_: e.tensor_copy(out=z16.t[:, i], in_=st_.t[:]),
                         reads=[st_], writes=[z16])
            Gc, Gs = GT_[nm]
            for fc in range(nfc):
                load_g(Gc, gts[0], fc, NT)
                load_g(Gs, gts[1], fc, NT)
                for gi in range(2):
                    for i in range(NT):
                        S.op("pe", lambda e, i=i, gi=gi: e.matmul(P[gi].t[:, 0:512], gts[gi].t[:, i, :],
                                                                  z16.t[:, i].rearrange("p s c -> p (s c)"),
                                                                  start=(i == 0), stop=(i == NT - 1)),
                             reads=[gts[gi], z16], writes=[P[gi]])
                kt = kt2[fc % 2]
                S.dma("sp", kt.t[:], KS[nm].ap[fc, :, order, :, :], reads=KS[nm].tk(), writes=[kt])
                pc = P[0].t[:, 0:512].rearrange("p (s c) -> p s c", s=2)
                psn = P[1].t[:, 0:512].rearrange("p (s c) -> p s c", s=2)
                kre = kt.t[:, 0, :].unsqueeze(1).broadcast_to([128, 2, 256])
                kim = kt.t[:, 1, :].unsqueeze(1).broadcast_to([128, 2, 256])
                yre, yrev = Yv(fc, 0)
                yim, yimv = Yv(fc, 1)
                S.op("dve", lambda e, kre=kre: e.tensor_tensor(out=tm[0].t[:], in0=pc, in1=kre, op=ALU.mult),
                     reads=[P[0], kt], writes=[tm[0]])
                S.op("dve", lambda e, kim=kim: e.tensor_tensor(out=tm[1].t[:], in0=psn, in1=kim, op=ALU.mult),
                     reads=[P[1], kt], writes=[tm[1]])
                S.op("dve", lambda e, yre=yre: e.tensor_tensor(out=yre, in0=tm[0].t[:], in1=tm[1].t[:], op=ALU.add),
                     reads=[tm[0], tm[1]], writes=[yrev])
                S.op("dve", lambda e, kre=kre: e.tensor_tensor(out=tm[2].t[:], in0=psn, in1=kre, op=ALU.mult),
                     reads=[P[1], kt], writes=[tm[2]])
                S.op("dve", lambda e, kim=kim: e.tensor_tensor(out=tm[3].t[:], in0=pc, in1=kim, op=ALU.mult),
                     reads=[P[0], kt], writes=[tm[3]])
                S.op("dve", lambda e, yim=yim: e.tensor_tensor(out=yim, in0=tm[2].t[:], in1=tm[3].t[:], op=ALU.subtract),
                     reads=[tm[2], tm[3]], writes=[yimv])
            for tt in range(NT):
                load_g(Gc, gts[0], tt, nfc)
                load_g(Gs, gts[1], tt, nfc)
                py = P[2 + tt % 2]
                r0 = tt * 128
                n_mm = 2 * nfc
                k_ = 0
                for gi in range(2):
                    for i in range(nfc):
                        ya, yv_ = Yv(i, gi)
                        S.op("pe", lambda e, i=i, gi=gi, ya=ya, k_=k_, py=py: e.matmul(
                            py.t[:, 0:512], gts[gi].t[:, i, :], ya.rearrange("p s c -> p (s c)"),
                            start=(k_ == 0), stop=(k_ == n_mm - 1)), reads=[gts[gi], Ymain, Ylast], writes=[py])
                        k_ += 1
                gate = gate2[tt % 2]
                zf = stg[tt % 2]
                for s_, q in enumerate(qs):
                    S.dma("sp", gate.t[:, s_, :], q.hv.ap[r0:r0 + 128, 256 * (order + 1):256 * (order + 2)],
                          reads=q.hv.tk(r0, r0 + 128), writes=[gate])
                    sd, sa = srcrows(q, r0)
                    S.dma("sp", zf.t[:, s_, :], sa, reads=sd.tk(r0, r0 + 128), writes=[zf])
                t_ = tm[tt % 2]
                bb = ppb.t[:, 256 + order * 256:512 + order * 256].unsqueeze(1).broadcast_to([128, 2, 256])
                S.op("dve", lambda e, t_=t_, zf=zf: e.tensor_tensor(out=t_.t[:], in0=zf.t[:], in1=bb, op=ALU.mult),
                     reads=[zf, ppb], writes=[t_])
                S.op("dve", lambda e, t_=t_, py=py: e.tensor_tensor(
                    out=t_.t[:], in0=t_.t[:], in1=py.t[:, 0:512].rearrange("p (s c) -> p s c", s=2), op=ALU.add),
                    reads=[t_, py], writes=[t_])
                S.op("dve", lambda e, t_=t_, gate=gate: e.tensor_tensor(out=t_.t[:], in0=t_.t[:], in1=gate.t[:], op=ALU.mult),
                     reads=[t_, gate], writes=[t_])
                if order == 0:
                    for s_, q in enumerate(qs):
                        S.dma("pool", q.z1.ap[r0:r0 + 128, :], t_.t[:, s_, :], reads=[t_], writes=q.z1.tk(r0, r0 + 128))
                else:
                    sq = tm[2 + tt % 2]
                    S.op("dve", lambda e, t_=t_, sq=sq: e.tensor_tensor(out=sq.t[:], in0=t_.t[:], in1=t_.t[:], op=ALU.mult),
                         reads=[t_], writes=[sq])
                    S.op("dve", lambda e, sq=sq: e.tensor_reduce(out=small.t[:, 0:2], in_=sq.t[:], op=ALU.add,
                                                                 axis=mybir.AxisListType.X), reads=[sq], writes=[small])
                    S.op("act", lambda e: e.activation(out=small.t[:, 2:4], in_=small.t[:, 0:2], func=AF.Sqrt,
                                                       bias=cst.t[:, 0:1], scale=1.0 / 256), reads=[small, cst], writes=[small])
                    S.op("dve", lambda e: e.reciprocal(out=small.t[:, 4:6], in_=small.t[:, 2:4]), reads=[small], writes=[small])
                    S.op("dve", lambda e, t_=t_: e.tensor_tensor(
                        out=t_.t[:], in0=t_.t[:], in1=small.t[:, 4:6].unsqueeze(2).broadcast_to([128, 2, 256]), op=ALU.mult),
                        reads=[t_, small], writes=[t_])
                    S.op("dve", lambda e, t_=t_: e.tensor_tensor(
                        out=t_.t[:], in0=t_.t[:], in1=ppb.t[:, 0:256].unsqueeze(1).broadcast_to([128, 2, 256]), op=ALU.mult),
                        reads=[t_, ppb], writes=[t_])
                    for s_ in range(2):
                        for c2 in range(2):
                            j = s_ * 2 + c2
                            S.op("pe", lambda e, s_=s_, c2=c2, j=j, t_=t_: e.transpose(
                                P[4].t[:, j * 128:(j + 1) * 128], t_.t[:, s_, c2 * 128:(c2 + 1) * 128], ident),
                                reads=[t_, masks], writes=[P[4]])
                    ot = ot2[tt % 2]
                    S.op("act", lambda e, ot=ot: e.activation(out=ot.t[:].rearrange("p s c t -> p (s c t)"),
                                                              in_=P[4].t[:, 0:512], func=AF.Copy), reads=[P[4]], writes=[ot])
                    for s_, q in enumerate(qs):
                        S.dma("pool", q.ym.ap[0:2, :, r0:r0 + 128].rearrange("c p t -> p c t"), ot.t[:, s_],
                              reads=[ot], writes=q.ym.tk(r0, r0 + 128))

        def outproj_phase(q, l):
            phase_start()
            T = q.T
            xts = [B.take(4096, "p (c t) -> p c t", c=8) for i in range(2)]
            yt = B.take(4096, "p (c t) -> p c t", c=8)
            ymt = C.take(4096, "p (c t) -> p c t", c=8)
            sq2 = [A.take(512) for i in range(2)]
            rstd = A.take(512)
            wod = W16["wo"]
            for tt in range(q.L // T):
                t0 = tt * T
                xt = xts[tt % 2]
                S.dma("sp", xt.t[:, :, :T], q.xs.ap[:, :, t0:t0 + T].rearrange("c p t -> p c t"),
                      reads=q.xs.tk(t0, t0 + T), writes=[xt])
                S.dma("sp", ymt.t[:, :, :T], q.ym.ap[:, :, t0:t0 + T].rearrange("c p t -> p c t"),
                      reads=q.ym.tk(t0, t0 + T), writes=[ymt])
                for oc in range(8):
                    w = wnext(); S.dma("sp", w.t[:], wod.ap[l, oc], writes=[w])
                    py = P[oc % 2]
                    for kc in range(8):
                        S.op("pe", lambda e, kc=kc, w=w, py=py: e.matmul(py.t[:, :T], w.t[:, kc, :], ymt.t[:, kc, :T],
                                                                         start=(kc == 0), stop=(kc == 7)),
                             reads=[w, ymt], writes=[py])
                    S.op("act", lambda e, oc=oc, py=py: e.activation(out=yt.t[:, oc, :T], in_=py.t[:, :T], func=AF.Copy),
                         reads=[py], writes=[yt])
                resid_store(q, 1, t0, T, xt, yt, rstd, sq2, P[7], None)

        def ssd_phase(q, l, want_out):
            phase_start()
            NT = q.nt
            SL = 64 if not q.ctx else 128
            nsub = 128 // SL
            b = q.bl
            raws = [B.take(8 * nsub * (SL + 2), "p (c s t) -> p c s t", c=8, s=nsub) for _ in range(2)]
            acc = B.take(1024, "p (c s t) -> p c s t", c=8, s=nsub)
            ct0 = B.take(1024, "p (c s t) -> p c s t", c=8, s=nsub)
            ct2 = B.take(1024, "p (c s t) -> p c s t", c=8, s=nsub)
            xbcs = [B.take(1024, "p (c t) -> p c t", c=8) for _ in range(3)]
            ltall = B.take(1024, "p (h t) -> p h t", h=8)
            ehall = B.take(1024, "p (h t) -> p h t", h=8)
            mhall = B.take(1024, "p (h t) -> p h t", h=8)
            dtrs = [A.take(128, parts=16) for _ in range(2)]
            dtfs = [A.take(128, parts=16) for _ in range(2)]
            xs_tms = [A.take(512) for _ in range(3)]
            bdts = [A.take(272) for _ in range(3)]
            a_tms = [A.take(16) for _ in range(3)]
            xdt = A.take(512); xd = A.take(512)
            gm = A.take(256, "p (g t) -> p g t", g=2)
            ecs = A.take(16); H = A.take(512, "p (g t) -> p g t", g=2); ysb = A.take(512); tmp = A.take(512)
            yfls = [A.take(512) for _ in range(3)]
            zts = [A.take(512, "p (c t) -> p c t", c=4) for _ in range(3)]
            zsg = A.take(512, "p (c t) -> p c t", c=4)
            yfm = A.take(512, "p (c t) -> p c t", c=4)
            rstd = A.take(128); sq2 = [A.take(128) for _ in range(2)]
            y16 = C.take(512, "p (c t) -> p c t", c=4)
            scwo = PP["scw"][0] + l * 24
            scbo = PP["scb"][0] + l * 8
            sdo = PP["sd"][0] + l * 8

            def wb(j):
                return ppt.t[:, scwo + j * 8:scwo + j * 8 + 8].unsqueeze(2).unsqueeze(3).broadcast_to([128, 8, nsub, SL])

            def silu_to(out_ap, x_ap, sg, xv, sgv, outv):
                S.op("act", lambda e: e.activation(out=sg, in_=x_ap, func=AF.Exp, scale=-1.0), reads=[xv], writes=[sgv])
                S.op("act", lambda e: e.activation(out=sg, in_=sg, func=AF.Ln, bias=cst.t[:, 1:2], scale=1.0),
                     reads=[sgv, cst], writes=[sgv])
                S.op("act", lambda e: e.activation(out=sg, in_=sg, func=AF.Exp, scale=-1.0), reads=[sgv], writes=[sgv])
                S.op("dve", lambda e: e.tensor_tensor(out=out_ap, in0=x_ap, in1=sg, op=ALU.mult),
                     reads=[xv, sgv], writes=[outv])

            def prep_load(ci, pos, d):
                t0 = ci * 128
                k = pos % 2; k3 = pos % 3
                raw = raws[k]; dtr = dtrs[k]
                if d == 1:
                    ck = q.cache.toks[ci:ci + 1]
                    S.dma("sp", xbcs[k3].t[:].rearrange("p c t -> p (c t)"), q.cache.ap[ci, :, 0:1024], reads=ck, writes=[xbcs[k3]])
                    S.dma("sp", xs_tms[k3].t[:], q.cache.ap[ci, :, 1024:1536], reads=ck, writes=[xs_tms[k3]])
                    S.dma("sp", bdts[k3].t[:], q.cache.ap[ci, :, 1536:1808], reads=ck, writes=[bdts[k3]])
                    if want_out:
                        S.dma("sp", zts[k3].t[:], q.px.ap[6:10, :, t0:t0 + 128].rearrange("c p t -> p c t"),
                              reads=q.px.tk(t0, t0 + 128), writes=[zts[k3]])
                        S.dma("sp", yfls[k3].t[:], q.yf.ap[t0:t0 + 128, :], reads=q.yf.tk(t0, t0 + 128), writes=[yfls[k3]])
                    return
                if q.ctx:
                    S.op("dve", lambda e: e.memset(raw.t[:], 0.0), writes=[raw])
                for s_ in range(nsub):
                    a0 = t0 + s_ * SL
                    S.dma("sp", raw.t[:, :, s_, 1:SL + 1], q.px.ap[10:18, :, a0:a0 + SL].rearrange("c p t -> p c t"),
                          reads=q.px.tk(a0, a0 + SL), writes=[raw])
                if q.ctx:
                    if t0 > 0:
                        S.dma("sp", raw.t[:, :, 0, 0:1], q.px.ap[10:18, :, t0 - 1:t0].rearrange("c p t -> p c t"),
                              reads=q.px.tk(t0 - 1, t0), writes=[raw])
                    if t0 + 128 < q.L:
                        S.dma("sp", raw.t[:, :, nsub - 1, SL + 1:SL + 2],
                              q.px.ap[10:18, :, t0 + 128:t0 + 129].rearrange("c p t -> p c t"),
                              reads=q.px.tk(t0 + 128, t0 + 129), writes=[raw])
                S.dma("sp", dtr.t[:], q.px.ap[18, 0:16, t0:t0 + 128], reads=q.px.tk(t0, t0 + 128), writes=[dtr])

            def prep(ci, pos, d):
                k = pos % 2; k3 = pos % 3
                raw = raws[k]; xbc = xbcs[k3]; dtr = dtrs[k]; dtf = dtfs[k]; xs_tm = xs_tms[k3]; bdt = bdts[k3]
                a_tm = a_tms[k3]
                if d == 1:
                    S.op("dve", lambda e: e.tensor_tensor(out=a_tm.t[:], in0=bdt.t[:, 256:272], in1=Aneg.t[:], op=ALU.mult),
                         reads=[bdt, Aneg], writes=[a_tm])
                    yield
                    if want_out:
                        silu_to(zts[k3].t[:], zts[k3].t[:], zsg.t[:], zts[k3], zsg, zts[k3])
                    return
                S.op("dve", lambda e: e.tensor_tensor(out=ct0.t[:], in0=raw.t[:, :, :, 0:SL], in1=wb(0), op=ALU.mult),
                     reads=[raw, ppt], writes=[ct0])
                S.op("dve", lambda e: e.tensor_tensor(out=ct2.t[:], in0=raw.t[:, :, :, 2:SL + 2], in1=wb(2), op=ALU.mult),
                     reads=[raw, ppt], writes=[ct2])
                S.op("dve", lambda e: e.tensor_tensor(out=acc.t[:], in0=raw.t[:, :, :, 1:SL + 1], in1=wb(1), op=ALU.mult),
                     reads=[raw, ppt], writes=[acc])
                S.op("dve", lambda e: e.tensor_tensor(out=acc.t[:], in0=acc.t[:], in1=ct0.t[:], op=ALU.add),
                     reads=[acc, ct0], writes=[acc])
                S.op("dve", lambda e: e.tensor_tensor(out=acc.t[:], in0=acc.t[:], in1=ct2.t[:], op=ALU.add),
                     reads=[acc, ct2], writes=[acc])
                S.op("dve", lambda e: e.tensor_tensor(
                    out=acc.t[:], in0=acc.t[:],
                    in1=ppt.t[:, scbo:scbo + 8].unsqueeze(2).unsqueeze(3).broadcast_to([128, 8, nsub, SL]), op=ALU.add),
                    reads=[acc, ppt], writes=[acc])
                yield
                silu_to(xbc.t[:].rearrange("p c (s t) -> p c s t", s=nsub), acc.t[:], ct0.t[:], acc, ct0, xbc)
                S.op("act", lambda e: e.activation(out=dtr.t[:], in_=dtr.t[:], func=AF.Exp, bias=dtb.t[:, l:l + 1],
                                                   scale=1.0), reads=[dtr, dtb], writes=[dtr])
                S.op("act", lambda e: e.activation(out=dtf.t[:], in_=dtr.t[:], func=AF.Ln, bias=cst.t[0:16, 1:2],
                                                   scale=1.0), reads=[dtr, cst], writes=[dtf])
                yield
                for kc in range(4):
                    S.op("pe", lambda e, kc=kc: e.transpose(P[0].t[:, kc * 128:(kc + 1) * 128], xbc.t[:, kc, :], ident),
                         reads=[xbc, masks], writes=[P[0]])
                S.op("act", lambda e: e.activation(out=xs_tm.t[:], in_=P[0].t[:, 0:512], func=AF.Copy),
                     reads=[P[0]], writes=[xs_tm])
                for g in range(2):
                    S.op("pe", lambda e, g=g: e.transpose(P[1].t[:, g * 128:(g + 1) * 128], xbc.t[:, 4 + g, :], ident),
                         reads=[xbc, masks], writes=[P[1]])
                S.op("pe", lambda e: e.transpose(P[1].t[:, 256:272], dtf.t[:], masks.t[0:16, 4, 0:16]),
                     reads=[dtf, masks], writes=[P[1]])
                S.op("act", lambda e: e.activation(out=bdt.t[:], in_=P[1].t[:, 0:272], func=AF.Copy),
                     reads=[P[1]], writes=[bdt])
                S.op("dve", lambda e: e.tensor_tensor(out=a_tm.t[:], in0=bdt.t[:, 256:272], in1=Aneg.t[:], op=ALU.mult),
                     reads=[bdt, Aneg], writes=[a_tm])
                ck = q.cache.toks[ci:ci + 1]
                S.dma("pool", q.cache.ap[ci, :, 0:1024], xbc.t[:].rearrange("p c t -> p (c t)"), reads=[xbc], writes=ck)
                S.dma("pool", q.cache.ap[ci, :, 1024:1536], xs_tm.t[:], reads=[xs_tm], writes=ck)
                S.dma("pool", q.cache.ap[ci, :, 1536:1808], bdt.t[:], reads=[bdt], writes=ck)

            def main(ci, pos, d):
                t0 = ci * 128
                k = pos % 2; k3 = pos % 3
                xbc = xbcs[k3]; xs_tm = xs_tms[k3]; bdt = bdts[k3]; a_tm = a_tms[k3]
                mA = masks.t[:, 0 if d == 0 else 2, :]
                mB = masks.t[:, 1 if d == 0 else 3, :]
                col = 127 if d == 0 else 0
                a = a_tm.t[:, 8 * d:8 * d + 8]
                S.op("dve", lambda e: e.tensor_tensor(
                    out=xdt.t[:].rearrange("p (h c) -> p h c", h=8), in0=xs_tm.t[:].rearrange("p (h c) -> p h c", h=8),
                    in1=bdt.t[:, 256 + 8 * d:264 + 8 * d].unsqueeze(2).broadcast_to([128, 8, 64]), op=ALU.mult),
                    reads=[xs_tm, bdt], writes=[xdt])
                if want_out:
                    for g in range(2):
                        S.op("pe", lambda e, g=g: e.matmul(P[2].t[:, g * 128:(g + 1) * 128], xbc.t[:, 4 + g, :],
                                                           xbc.t[:, 6 + g, :], start=True, stop=True),
                             reads=[xbc], writes=[P[2]])
                    S.op("dve", lambda e: e.tensor_tensor(
                        out=gm.t[:], in0=P[2].t[:, 0:256].rearrange("p (g t) -> p g t", g=2),
                        in1=mB.unsqueeze(1).broadcast_to([128, 2, 128]), op=ALU.mult), reads=[P[2], masks], writes=[gm])
                S.op("pe", lambda e: e.matmul(P[3].t[:, 0:8], mB, a, start=True, stop=True),
                     reads=[masks, a_tm], writes=[P[3]])
                S.op("pe", lambda e: e.matmul(P[3].t[:, 8:16], ones.t[:], a, start=True, stop=True),
                     reads=[ones, a_tm], writes=[P[3]])
                S.op("act", lambda e: e.activation(out=ecs.t[:], in_=P[3].t[:, 0:16], func=AF.Exp),
                     reads=[P[3]], writes=[ecs])
                S.op("dve", lambda e: e.tensor_tensor(
                    out=ltall.t[:], in0=mA.unsqueeze(1).broadcast_to([128, 8, 128]),
                    in1=a.unsqueeze(2).broadcast_to([128, 8, 128]), op=ALU.mult), reads=[masks, a_tm], writes=[ltall])
                for h in range(8):
                    pseg = P[4 + h // 4]
                    S.op("pe", lambda e, h=h, pseg=pseg: e.matmul(pseg.t[:, (h % 4) * 128:(h % 4 + 1) * 128], ltall.t[:, h, :],
                                                                  mB, start=True, stop=True),
                         reads=[ltall, masks], writes=[pseg])
                yield
                for hh in range(2):
                    S.op("act", lambda e, hh=hh: e.activation(
                        out=ehall.t[:, 4 * hh:4 * hh + 4, :], in_=P[4 + hh].t[:, 0:512].rearrange("p (h t) -> p h t", h=4),
                        func=AF.Exp), reads=[P[4 + hh]], writes=[ehall])
                if want_out:
                    for g in range(2):
                        S.op("dve", lambda e, g=g: e.tensor_tensor(
                            out=mhall.t[:, 4 * g:4 * g + 4, :], in0=ehall.t[:, 4 * g:4 * g + 4, :],
                            in1=gm.t[:, g, :].unsqueeze(1).broadcast_to([128, 4, 128]), op=ALU.mult),
                            reads=[ehall, gm], writes=[mhall])
                    for h in range(8):
                        S.op("pe", lambda e, h=h: e.matmul(P[6].t[:, h * 64:(h + 1) * 64], mhall.t[:, h, :],
                                                           xdt.t[:, h * 64:(h + 1) * 64], start=True, stop=True),
                             reads=[mhall, xdt], writes=[P[6]])
                S.op("dve", lambda e: e.tensor_tensor(
                    out=xd.t[:].rearrange("p (h c) -> p h c", h=8), in0=xdt.t[:].rearrange("p (h c) -> p h c", h=8),
                    in1=ehall.t[:, :, col:col + 1].broadcast_to([128, 8, 64]), op=ALU.mult),
                    reads=[ehall, xdt], writes=[xd])
                yield
                if want_out:
                    for g in range(2):
                        S.op("pe", lambda e, g=g: e.matmul(P[7].t[:, g * 256:(g + 1) * 256], xbc.t[:, 6 + g, :], H.t[:, g, :],
                                                           start=True, stop=True), reads=[xbc, H], writes=[P[7]])
                    S.op("dve", lambda e: e.tensor_tensor(
                        out=tmp.t[:].rearrange("p (h c) -> p h c", h=8), in0=P[7].t[:, 0:512].rearrange("p (h c) -> p h c", h=8),
                        in1=ecs.t[:, 0:8].unsqueeze(2).broadcast_to([128, 8, 64]), op=ALU.mult),
                        reads=[P[7], ecs], writes=[tmp])
                    S.op("dve", lambda e: e.tensor_tensor(out=ysb.t[:], in0=tmp.t[:], in1=P[6].t[:, 0:512], op=ALU.add),
                         reads=[tmp, P[6]], writes=[ysb])
                    if d == 0:
                        S.op("dve", lambda e: e.tensor_tensor(
                            out=tmp.t[:].rearrange("p (h c) -> p h c", h=8), in0=xs_tm.t[:].rearrange("p (h c) -> p h c", h=8),
                            in1=ppt.t[:, sdo:sdo + 8].unsqueeze(2).broadcast_to([128, 8, 64]), op=ALU.mult),
                            reads=[xs_tm, ppt], writes=[tmp])
                        S.op("dve", lambda e: e.tensor_tensor(out=ysb.t[:], in0=ysb.t[:], in1=tmp.t[:], op=ALU.add),
                             reads=[ysb, tmp], writes=[ysb])
                        S.dma("pool", q.yf.ap[t0:t0 + 128, :], ysb.t[:], reads=[ysb], writes=q.yf.tk(t0, t0 + 128))
                    else:
                        S.op("dve", lambda e: e.tensor_tensor(out=ysb.t[:], in0=ysb.t[:], in1=yfls[k3].t[:], op=ALU.add),
                             reads=[ysb, yfls[k3]], writes=[ysb])
                for g in range(2):
                    S.op("pe", lambda e, g=g: e.matmul(P[7].t[:, g * 256:(g + 1) * 256], bdt.t[:, g * 128:(g + 1) * 128],
                                                       xd.t[:, g * 256:(g + 1) * 256], start=True, stop=True),
                         reads=[bdt, xd], writes=[P[7]])
                S.op("dve", lambda e: e.tensor_tensor(
                    out=H.t[:].rearrange("p g (h c) -> p (g h) c", h=4), in0=H.t[:].rearrange("p g (h c) -> p (g h) c", h=4),
                    in1=ecs.t[:, 8:16].unsqueeze(2).broadcast_to([128, 8, 64]), op=ALU.mult),
                    reads=[H, ecs], writes=[H])
                S.op("dve", lambda e: e.tensor_tensor(out=H.t[:].rearrange("p g t -> p (g t)"),
                                                      in0=H.t[:].rearrange("p g t -> p (g t)"),
                                                      in1=P[7].t[:, 0:512], op=ALU.add), reads=[H, P[7]], writes=[H])
                yield
                if want_out and d == 1:
                    for kc in range(4):
                        S.op("pe", lambda e, kc=kc: e.transpose(P[6].t[:, kc * 128:(kc + 1) * 128],
                                                                ysb.t[:, kc * 128:(kc + 1) * 128], ident),
                             reads=[ysb, masks], writes=[P[6]])
                    S.op("dve", lambda e: e.tensor_tensor(out=yfm.t[:], in0=P[6].t[:, 0:512].rearrange("p (c t) -> p c t", c=4),
                                                          in1=zts[k3].t[:], op=ALU.mult), reads=[P[6], zts[k3]], writes=[yfm])
                    rms([yfm.t[:, kc, :] for kc in range(4)], 128, 512, rstd, sq2, P[2], [yfm], lnexp=True)
                    S.op("dve", lambda e: e.tensor_tensor(
                        out=yfm.t[:], in0=yfm.t[:], in1=rstd.t[:, 0:128].unsqueeze(1).broadcast_to([128, 4, 128]),
                        op=ALU.mult), reads=[yfm, rstd], writes=[yfm])
                    mgo = PP["mg"][0] + l * 8 + 2
                    S.op("dve", lambda e: e.tensor_tensor(
                        out=y16.t[:], in0=yfm.t[:], in1=ppt.t[:, mgo:mgo + 4].unsqueeze(2).broadcast_to([128, 4, 128]),
                        op=ALU.mult), reads=[yfm, ppt], writes=[y16])
                    S.dma("pool", q.ym.ap[2:6, :, t0:t0 + 128].rearrange("c p t -> p c t"), y16.t[:],
                          reads=[y16], writes=q.ym.tk(t0, t0 + 128))

            for d in range(2):
                if q.ctx:
                    S.op("dve", lambda e: e.memset(H.t[:], 0.0), writes=[H])
                else:
                    S.dma("sp", H.t[:].rearrange("p g t -> p (g t)"), HSTD.ap[b, d], reads=HSTD.tk(), writes=[H])
                order = list(range(NT)) if d == 0 else list(range(NT - 1, -1, -1))
                if d == 0 and not q.ctx:
                    for k in range(2):
                        S.op("dve", lambda e, k=k: e.memset(raws[k].t[:], 0.0), writes=[raws[k]])
                prep_load(order[0], 0, d)
                if len(order) > 1:
                    prep_load(order[1], 1, d)
                for _ in prep(order[0], 0, d):
                    pass
                for idx, ci in enumerate(order):
                    if idx + 2 < len(order):
                        prep_load(order[idx + 2], idx + 2, d)
                    gm_ = main(ci, idx, d)
                    gp_ = prep(order[idx + 1], idx + 1, d) if idx + 1 < len(order) else iter(())
                    dm = dp = False
                    while not (dm and dp):
                        if not dm:
                            try:
                                next(gm_)
                            except StopIteration:
                                dm = True
                        if not dp:
                            try:
                                next(gp_)
                            except StopIteration:
                                dp = True
                if q.ctx:
                    S.dma("pool", HSTD.ap[b, d], H.t[:].rearrange("p g t -> p (g t)"), reads=[H], writes=HSTD.tk())

        def mixer_pair(qs, l, last):
            want = not (qs[0].ctx and last)
            for q in qs:
                proj_phase(q, l)
                ssd_phase(q, l, want)
                if want:
                    sc_phase(q, l)
                    hyprep_phase(q, l)
            if want:
                conv2_phase(qs, l, 0)
                conv2_phase(qs, l, 1)

        convert_weights()
        for l in range(depth):
            last = (l == depth - 1)
            mod_phase(l)
            S.dma("sp", ppb.t[:], din["ppb"][l], writes=[ppb])
            for q in seqs:
                ffn_phase(q, l, 0)
            if dbg == "ffn1":
                break
            filter_phase(l, "b"); kspec_phase(l, "b")
            if not last:
                filter_phase(l, "s"); kspec_phase(l, "s")
            mixer_pair([q for q in seqs if q.ctx], l, last)
            mixer_pair([q for q in seqs if not q.ctx], l, last)
            if dbg in ("mix", "proj"):
                break
            for q in seqs:
                if q.ctx and last:
                    continue
                outproj_phase(q, l)
                if dbg == "outp":
                    continue
                ffn_phase(q, l, 1)
        S.barrier()
        B.o = 0
        ot = [B.take(4096, "p (c t) -> p c t", c=8) for i in range(2)]
        i = 0
        for q in seqs:
            if q.ctx:
                continue
            for tt in range(q.L // 512):
                o = ot[i % 2]; i += 1
                srcd = q.xs
                S.dma("sp", o.t[:], srcd.ap[:, :, tt * 512:(tt + 1) * 512].rearrange("c p t -> p c t"),
                      reads=srcd.tk(tt * 512, tt * 512 + 512), writes=[o])
                S.dma("pool", out[q.bl, :, :, tt * 512:(tt + 1) * 512].rearrange("c p t -> p c t"), o.t[:], reads=[o])
        S.barrier()
        print("instructions:", S.nins)
    return nc


def _prep(inputs):
    sh = _host_shared(inputs)
    x = np.asarray(inputs["x"], np.float32)
    ctx = np.asarray(inputs["ctx"], np.float32)
    c = np.asarray(inputs["c"], np.float32)
    cc = np.asarray(inputs["c_ctx"], np.float32)
    maps = []
    for core in range(NCORE):
        m = dict(sh)
        b0 = core * BPC
        m["xl"] = np.ascontiguousarray(x[b0:b0 + BPC].transpose(0, 2, 1)).reshape(BPC, 8, 128, SEQ)
        m["xc"] = np.ascontiguousarray(ctx[b0:b0 + BPC].transpose(0, 2, 1)).reshape(BPC, 8, 128, CTX)
        cT = np.stack([c[b0], c[b0 + 1], cc], axis=-1)
        m["cT"] = np.ascontiguousarray(cT.reshape(8, 128, 3).transpose(1, 0, 2))
        maps.append(m)
    return maps


def kernel(**inputs):
    maps = _prep(inputs)
    shapes = {k: v.shape for k, v in maps[0].items()}
    nc = build(DEPTH, shapes)
    res = run_bass_kernel_spmd(nc, maps, core_ids=list(range(NCORE)))
    outs = []
    for core in range(NCORE):
        o = res.results[core]["out"].reshape(BPC, D, SEQ)
        outs.append(o.transpose(0, 2, 1))
    return np.ascontiguousarray(np.concatenate(outs, axis=0)).astype(np.float32)
```
